# Optimizing a Trainium2 kernel written in Bass

```python
import math
import jax
import jax.numpy as jnp
from jax import lax
import numpy as np

D_MODEL = 1024
BATCH = 16
SEQ = 256
DEPTH = 4
DEC_BATCH = 8
DEC_SEQ = 1024
PAST_LEN = 256

GRID_W = 64
N_EVEN = (DEPTH + 1) // 2
N_ODD = DEPTH // 2
HEAD_DIM = 128
MIX_HALF = D_MODEL // 2
DN_HEADS = MIX_HALF // HEAD_DIM
DN_CHUNK = 64
SHORT_CONV = 3
HY_WIDTH = MIX_HALF
HY_ORDER = 2
HY_BANDS = 16
HY_EMB_DIM = 1 + 2 * HY_BANDS
HY_FFN = 64
HY_DECAY_PCT_SHORT = 0.3
HY_DECAY_PCT_LONG = 1.5
HY_TARGET = 1e-2
C_HEADS = MIX_HALF // HEAD_DIM
C_KV = C_HEADS // 2
D_HEADS = MIX_HALF // HEAD_DIM
D_KV = D_HEADS // 2
WINDOW = 128
Q_BLOCK = 128
ROPE_THETA = 10000.0
D_FF = 2816
FFN_CONV = 3
EPS = 1e-6
NEG_BIG = -1e30
EVEN_IN = 4 * DN_HEADS * HEAD_DIM + 4 * DN_HEADS + 3 * HY_WIDTH
ODD_IN = (C_HEADS + 2 * C_KV + D_HEADS + 2 * D_KV) * HEAD_DIM
F32 = jnp.float32

kernel_name = 'hybrid_diffusion_prefix_step'


def rmsnorm(x, g):
    xf = x.astype(F32)
    y = xf * lax.rsqrt(jnp.mean(xf * xf, axis=-1, keepdims=True) + EPS)
    return (y * g.astype(F32)).astype(x.dtype)


def l2norm(x):
    return x * lax.rsqrt(jnp.sum(x * x, axis=-1, keepdims=True) + EPS)


def dwconv(x, w, b=None):
    k = w.shape[0]
    pad = k // 2
    L = x.shape[1]
    xp = jnp.pad(x, ((0, 0), (pad, pad), (0, 0)))
    y = xp[:, 0:L] * w[0]
    for i in range(1, k):
        y = y + xp[:, i:i + L] * w[i]
    return y if b is None else y + b


def modulation(cvec, w_mod, b_mod):
    m = jax.nn.silu(cvec) @ w_mod + b_mod
    return jnp.split(m[:, None, :], 6, axis=-1)


def gated_delta_chunked(q, k, v, g, beta, s0):
    B, L, H, DK = q.shape
    DV = v.shape[-1]
    C = DN_CHUNK
    N = L // C

    def chunks(t):
        t = t.reshape((B, N, C, H) + t.shape[3:])
        return jnp.moveaxis(t, (1, 3), (0, 2))

    qc, kc, vc, gc, bc = chunks(q), chunks(k), chunks(v), chunks(g), chunks(beta)
    gcum = jnp.cumsum(gc, axis=-1)
    idx = jnp.arange(C)
    tril = idx[:, None] >= idx[None, :]
    decay = jnp.exp(jnp.where(tril, gcum[..., :, None] - gcum[..., None, :], -jnp.inf))
    kb = kc * bc[..., None]
    a = jnp.einsum('nbhid,nbhjd->nbhij', kb, kc) * decay
    a = jnp.where(idx[:, None] > idx[None, :], a, 0.0)
    m = a + jnp.eye(C, dtype=a.dtype)
    rhs = jnp.concatenate([vc * bc[..., None], kb * jnp.exp(gcum)[..., None]], axis=-1)
    sol = lax.linalg.triangular_solve(m, rhs, left_side=True, lower=True, unit_diagonal=True)
    u, w = sol[..., :DV], sol[..., DV:]
    qg = qc * jnp.exp(gcum)[..., None]
    kdec = kc * jnp.exp(gcum[..., -1:] - gcum)[..., None]
    glast = jnp.exp(gcum[..., -1])
    attn = jnp.einsum('nbhid,nbhjd->nbhij', qc, kc) * decay

    def step(s, xs):
        qg_i, kdec_i, u_i, w_i, attn_i, gl_i = xs
        v_new = u_i - jnp.einsum('bhcd,bhde->bhce', w_i, s)
        o = jnp.einsum('bhcd,bhde->bhce', qg_i, s) + jnp.einsum('bhij,bhje->bhie', attn_i, v_new)
        s = s * gl_i[..., None, None] + jnp.einsum('bhcd,bhce->bhde', kdec_i, v_new)
        return s, o

    s_fin, o = lax.scan(step, s0, (qg, kdec, u, w, attn, glast))
    o = jnp.moveaxis(o, (0, 2), (1, 3)).reshape(B, L, H, DV)
    return o, s_fin


def deltanet_bidir(q, k, v, beta, g, s0):
    o_f, s_f = gated_delta_chunked(q, k, v, g[:, :, 0], beta[:, :, 0], s0[:, 0])
    fl = lambda t: jnp.flip(t, axis=1)
    o_b, s_b = gated_delta_chunked(fl(q), fl(k), fl(v), fl(g[:, :, 1]), fl(beta[:, :, 1]), s0[:, 1])
    return o_f + fl(o_b), jnp.stack([s_f, s_b], axis=1)


def hyena_filters(L, p, j):
    t = jnp.linspace(0.0, 1.0, L, dtype=F32)[:, None]
    w = (2.0 * math.pi / L) * jnp.arange(L, dtype=F32)[:, None]
    f = jnp.linspace(1e-4, HY_BANDS - 1, HY_BANDS, dtype=F32)[None, :]
    feats = jnp.concatenate([t, jnp.cos(f * w), -jnp.sin(f * w)], axis=-1)
    z = jnp.sin(p['hy_freq1'][j].astype(F32) * (feats @ p['hy_w1'][j].astype(F32) + p['hy_b1'][j].astype(F32)))
    z = jnp.sin(p['hy_freq2'][j].astype(F32) * (z @ p['hy_w2'][j].astype(F32) + p['hy_b2'][j].astype(F32)))
    h = z @ p['hy_w3'][j].astype(F32)
    deltas = jnp.abs(jnp.linspace(math.log(HY_TARGET) / HY_DECAY_PCT_LONG,
                                  math.log(HY_TARGET) / HY_DECAY_PCT_SHORT, HY_WIDTH, dtype=F32))
    h = h.reshape(L, HY_ORDER, 2, HY_WIDTH) * jnp.exp(-t * deltas)[:, None, None, :]
    return jnp.transpose(h, (1, 2, 0, 3))


def long_conv_bidir(u, h_fwd, h_bwd, bias):
    L = u.shape[1]
    taps = jnp.concatenate([h_fwd, jnp.zeros_like(h_fwd[:1]), jnp.flip(h_bwd[1:], axis=0)], axis=0)
    uf = jnp.fft.rfft(u, n=2 * L, axis=1)
    hf = jnp.fft.rfft(taps, n=2 * L, axis=0)
    y = jnp.fft.irfft(uf * hf[None], n=2 * L, axis=1)[:, :L]
    return y + u * bias


def hyena_mixer(xh, conv_w, conv_b, filt, bias):
    u = dwconv(xh, conv_w, conv_b).astype(F32)
    x1, x2, z = jnp.split(u, 3, axis=-1)
    for o, gate in enumerate((x1, x2)):
        z = gate * long_conv_bidir(z, filt[o, 0], filt[o, 1], bias[o])
    return z


def even_mixer(h, p, j, s0):
    B, L, _ = h.shape
    ha = DN_HEADS * HEAD_DIM
    proj = h @ p['ev_w_in'][j]
    qkv, z, b_raw, a_raw, hy_in = jnp.split(
        proj, [3 * ha, 4 * ha, 4 * ha + 2 * DN_HEADS, 4 * ha + 4 * DN_HEADS], axis=-1)
    qkv = jax.nn.silu(dwconv(qkv, p['dn_conv_w'][j])).astype(F32)
    q, k, v = [t.reshape(B, L, DN_HEADS, HEAD_DIM) for t in jnp.split(qkv, 3, axis=-1)]
    q = l2norm(q) * (HEAD_DIM ** -0.5)
    k = l2norm(k)
    beta = jax.nn.sigmoid(b_raw.astype(F32)).reshape(B, L, 2, DN_HEADS)
    g = -jnp.exp(p['dn_a_log'][j].astype(F32)) * jax.nn.softplus(
        a_raw.astype(F32).reshape(B, L, 2, DN_HEADS) + p['dn_dt_bias'][j].astype(F32))
    o, s_fin = deltanet_bidir(q, k, v, beta, g, s0.astype(F32))
    o = rmsnorm(o, p['dn_norm'][j]) * jax.nn.silu(z.astype(F32).reshape(B, L, DN_HEADS, HEAD_DIM))
    filt = hyena_filters(L, p, j)
    yb = hyena_mixer(hy_in, p['hy_conv_w'][j], p['hy_conv_b'][j], filt, p['hy_bias'][j].astype(F32))
    mix = jnp.concatenate([o.reshape(B, L, ha), yb], axis=-1)
    return mix.astype(h.dtype), s_fin


def rope_2d_tables(L):
    rows = L // GRID_W
    row = jnp.repeat(jnp.arange(rows), GRID_W).astype(F32)
    col = (jnp.arange(rows * GRID_W) % GRID_W).astype(F32)
    half = HEAD_DIM // 2
    inv = ROPE_THETA ** (-jnp.arange(0, half, 2, dtype=F32) / half)
    ang = jnp.concatenate([row[:, None] * inv, col[:, None] * inv], axis=-1)
    return jnp.cos(ang), jnp.sin(ang)


def apply_rope_2d(x, cos, sin):
    B, L, H, HD = x.shape
    xf = x.astype(F32).reshape(B, L, H, 2, 2, HD // 4)
    x1, x2 = xf[..., 0, :], xf[..., 1, :]
    c = cos.reshape(L, 2, HD // 4)[None, :, None]
    s = sin.reshape(L, 2, HD // 4)[None, :, None]
    out = jnp.stack([x1 * c - x2 * s, x2 * c + x1 * s], axis=-2).reshape(B, L, H, HD)
    return out.astype(x.dtype)


def softmax_with_sink(s, sink):
    if sink is None:
        return jax.nn.softmax(s, axis=-1)
    sk = sink.astype(F32).reshape(s.shape[-4], s.shape[-3], 1, 1)
    m = jnp.maximum(jnp.max(s, axis=-1, keepdims=True), sk)
    e = jnp.exp(s - m)
    return e / (jnp.sum(e, axis=-1, keepdims=True) + jnp.exp(sk - m))


def dense_attention(q, k, v, sink):
    B, Lq, H, HD = q.shape
    KV = k.shape[2]
    G = H // KV
    nb = Lq // Q_BLOCK
    qb = q.reshape(B, nb, Q_BLOCK, KV, G, HD).transpose(1, 0, 2, 3, 4, 5)
    scale = HD ** -0.5

    def one_block(qi):
        s = jnp.einsum('bqkgd,bskd->bkgqs', qi, k, preferred_element_type=F32) * scale
        pr = softmax_with_sink(s, sink)
        return jnp.einsum('bkgqs,bskd->bqkgd', pr.astype(v.dtype), v)

    out = lax.map(one_block, qb)
    return out.transpose(1, 0, 2, 3, 4, 5).reshape(B, Lq, H, HD)


def banded_attention(q, k, v, ck, cv, sink):
    B, L, H, HD = q.shape
    KV = k.shape[2]
    G = H // KV
    W = WINDOW
    nb = L // W
    qb = q.reshape(B, nb, W, KV, G, HD)

    def band(t):
        tp = jnp.pad(t, ((0, 0), (W, W), (0, 0), (0, 0))).reshape(B, nb + 2, W, KV, HD)
        return jnp.concatenate([tp[:, :-2], tp[:, 1:-1], tp[:, 2:]], axis=2)

    kb, vb = band(k), band(v)
    qi = jnp.arange(W)[None, :, None]
    kj = jnp.arange(3 * W)[None, None, :]
    blk = jnp.arange(nb)[:, None, None]
    keypos = blk * W - W + kj
    valid = (jnp.abs(kj - W - qi) <= W) & (keypos >= 0) & (keypos < L)
    scale = HD ** -0.5
    s_loc = jnp.einsum('bnqkgd,bnskd->bnkgqs', qb, kb, preferred_element_type=F32) * scale
    s_loc = jnp.where(valid[None, :, None, None], s_loc, NEG_BIG)
    s_ctx = jnp.einsum('bnqkgd,bskd->bnkgqs', qb, ck, preferred_element_type=F32) * scale
    pr = softmax_with_sink(jnp.concatenate([s_loc, s_ctx], axis=-1), sink)
    out = (jnp.einsum('bnkgqs,bnskd->bnqkgd', pr[..., :3 * W].astype(v.dtype), vb)
           + jnp.einsum('bnkgqs,bskd->bnqkgd', pr[..., 3 * W:].astype(v.dtype), cv))
    return out.reshape(B, L, H, HD)


def odd_mixer(h, p, j, ctx, rope):
    B, L, _ = h.shape
    proj = h @ p['od_w_in'][j]
    cuts = [int(x) * HEAD_DIM for x in np.cumsum([C_HEADS, C_KV, C_KV, D_HEADS, D_KV])]
    qc, kc, vc, qd, kd, vd = jnp.split(proj, cuts, axis=-1)
    heads = lambda t: t.reshape(B, L, -1, HEAD_DIM)
    qc = rmsnorm(heads(qc), p['c_q_norm'][j])
    kc = rmsnorm(heads(kc), p['c_k_norm'][j])
    vc, qd, kd, vd = heads(vc), heads(qd), heads(kd), heads(vd)
    sink = p['d_sink'][j]
    if ctx is None:
        yc = dense_attention(qc, kc, vc, None)
        yd = dense_attention(qd, kd, vd, sink)
        new = (kc, vc, kd, vd)
    else:
        cos, sin = rope
        qc, kc, qd, kd = [apply_rope_2d(t, cos, sin) for t in (qc, kc, qd, kd)]
        ck_c, cv_c, ck_d, cv_d = ctx
        yc = dense_attention(qc, jnp.concatenate([ck_c, kc], axis=1), jnp.concatenate([cv_c, vc], axis=1), None)
        yd = banded_attention(qd, kd, vd, ck_d, cv_d, sink)
        new = None
    mix = jnp.concatenate([yc.reshape(B, L, -1), yd.reshape(B, L, -1)], axis=-1)
    return mix.astype(h.dtype), new


def conv_ffn(h, w_up, conv_w, conv_b, w_down):
    a, b = jnp.split(h @ w_up, 2, axis=-1)
    a = dwconv(a, conv_w, conv_b)
    return (jax.nn.silu(a) * b) @ w_down


def run_trunk(x, cvec, p, past):
    is_ctx = past is None
    B, L, _ = x.shape
    rope = None if is_ctx else rope_2d_tables(L)
    dn_states, k_c, v_c, k_d, v_d = [], [], [], [], []
    for i in range(DEPTH):
        j = i // 2
        sh1, sc1, g1, sh2, sc2, g2 = modulation(cvec, p['w_mod'][i], p['b_mod'][i])
        h = rmsnorm(x, p['norm_mix'][i]) * (1.0 + sc1) + sh1
        if i % 2 == 0:
            if is_ctx:
                s0 = jnp.zeros((B, 2, DN_HEADS, HEAD_DIM, HEAD_DIM), F32)
            else:
                s0 = past[0][:, j]
            mix, s_fin = even_mixer(h, p, j, s0)
            if is_ctx:
                dn_states.append(s_fin.astype(x.dtype))
        else:
            ctx = None if is_ctx else (past[1][:, j], past[2][:, j], past[3][:, j], past[4][:, j])
            mix, kv = odd_mixer(h, p, j, ctx, rope)
            if is_ctx:
                k_c.append(kv[0])
                v_c.append(kv[1])
                k_d.append(kv[2])
                v_d.append(kv[3])
        x = x + g1 * (mix @ p['w_out'][i])
        h = rmsnorm(x, p['norm_ffn'][i]) * (1.0 + sc2) + sh2
        x = x + g2 * conv_ffn(h, p['ffn_w_up'][i], p['ffn_conv_w'][i], p['ffn_conv_b'][i], p['ffn_w_down'][i])
    y = rmsnorm(x, p['final_norm'])
    if not is_ctx:
        return y, None
    return y, (jnp.stack(dn_states, axis=1), jnp.stack(k_c, axis=1), jnp.stack(v_c, axis=1),
               jnp.stack(k_d, axis=1), jnp.stack(v_d, axis=1))


def setup_inputs(seed: int = 0) -> dict:
    key = jax.random.key(seed)
    ks = jax.random.split(key, 64)
    counter = [0]

    def nk():
        counter[0] += 1
        return ks[counter[0] - 1]

    def nrm(shape, scale):
        return jax.random.normal(nk(), shape, F32) * scale

    def gain(shape):
        return 1.0 + nrm(shape, 0.02)

    D = D_MODEL
    ha = DN_HEADS * HEAD_DIM
    dt = jnp.exp(jax.random.uniform(nk(), (N_EVEN, 2, DN_HEADS), F32, math.log(1e-3), math.log(1e-1)))
    a_log = jnp.log(jax.random.uniform(nk(), (N_EVEN, 2, DN_HEADS), F32, 1.0, 16.0))
    return {
        'x_prompt': nrm((BATCH, SEQ, D), 1.0),
        'x_sample': nrm((DEC_BATCH, DEC_SEQ, D), 1.0),
        'state_dn': nrm((DEC_BATCH, N_EVEN, 2, DN_HEADS, HEAD_DIM, HEAD_DIM), 0.1),
        'cache_k_c': nrm((DEC_BATCH, N_ODD, PAST_LEN, C_KV, HEAD_DIM), 1.0),
        'cache_v_c': nrm((DEC_BATCH, N_ODD, PAST_LEN, C_KV, HEAD_DIM), 1.0),
        'cache_k_d': nrm((DEC_BATCH, N_ODD, PAST_LEN, D_KV, HEAD_DIM), 1.0),
        'cache_v_d': nrm((DEC_BATCH, N_ODD, PAST_LEN, D_KV, HEAD_DIM), 1.0),
        'c': nrm((DEC_BATCH, D), 1.0),
        'c_ctx': nrm((D,), 1.0),
        'final_norm': gain((D,)),
        'w_mod': nrm((DEPTH, D, 6 * D), 0.5 * D ** -0.5),
        'b_mod': nrm((DEPTH, 6 * D), 0.02),
        'norm_mix': gain((DEPTH, D)),
        'norm_ffn': gain((DEPTH, D)),
        'w_out': nrm((DEPTH, D, D), D ** -0.5),
        'ffn_w_up': nrm((DEPTH, D, 2 * D_FF), D ** -0.5),
        'ffn_conv_w': nrm((DEPTH, FFN_CONV, D_FF), FFN_CONV ** -0.5),
        'ffn_conv_b': nrm((DEPTH, D_FF), 0.02),
        'ffn_w_down': nrm((DEPTH, D_FF, D), D_FF ** -0.5),
        'ev_w_in': nrm((N_EVEN, D, EVEN_IN), D ** -0.5),
        'dn_conv_w': nrm((N_EVEN, SHORT_CONV, 3 * ha), SHORT_CONV ** -0.5),
        'dn_a_log': a_log,
        'dn_dt_bias': dt + jnp.log(-jnp.expm1(-dt)),
        'dn_norm': gain((N_EVEN, HEAD_DIM)),
        'hy_conv_w': nrm((N_EVEN, 3, 3 * HY_WIDTH), 3 ** -0.5),
        'hy_conv_b': nrm((N_EVEN, 3 * HY_WIDTH), 0.02),
        'hy_w1': nrm((N_EVEN, HY_EMB_DIM, HY_FFN), HY_EMB_DIM ** -0.5),
        'hy_b1': nrm((N_EVEN, HY_FFN), 0.1),
        'hy_freq1': gain((N_EVEN, HY_FFN)),
        'hy_w2': nrm((N_EVEN, HY_FFN, HY_FFN), HY_FFN ** -0.5),
        'hy_b2': nrm((N_EVEN, HY_FFN), 0.1),
        'hy_freq2': gain((N_EVEN, HY_FFN)),
        'hy_w3': nrm((N_EVEN, HY_FFN, HY_ORDER * 2 * HY_WIDTH), 0.1 * HY_FFN ** -0.5),
        'hy_bias': nrm((N_EVEN, HY_ORDER, HY_WIDTH), 0.1),
        'od_w_in': nrm((N_ODD, D, ODD_IN), D ** -0.5),
        'c_q_norm': gain((N_ODD, HEAD_DIM)),
        'c_k_norm': gain((N_ODD, HEAD_DIM)),
        'd_sink': nrm((N_ODD, D_HEADS), 0.5),
    }


def reference(x_prompt, x_sample, state_dn, cache_k_c, cache_v_c, cache_k_d, cache_v_d, c,
              c_ctx, final_norm, w_mod, b_mod, norm_mix, norm_ffn, w_out, ffn_w_up, ffn_conv_w,
              ffn_conv_b, ffn_w_down, ev_w_in, dn_conv_w, dn_a_log, dn_dt_bias, dn_norm,
              hy_conv_w, hy_conv_b, hy_w1, hy_b1, hy_freq1, hy_w2, hy_b2, hy_freq2, hy_w3, hy_bias,
              od_w_in, c_q_norm, c_k_norm, d_sink):
    p = {
        'final_norm': final_norm, 'w_mod': w_mod, 'b_mod': b_mod, 'norm_mix': norm_mix,
        'norm_ffn': norm_ffn, 'w_out': w_out, 'ffn_w_up': ffn_w_up, 'ffn_conv_w': ffn_conv_w,
        'ffn_conv_b': ffn_conv_b, 'ffn_w_down': ffn_w_down, 'ev_w_in': ev_w_in,
        'dn_conv_w': dn_conv_w, 'dn_a_log': dn_a_log, 'dn_dt_bias': dn_dt_bias, 'dn_norm': dn_norm,
        'hy_conv_w': hy_conv_w, 'hy_conv_b': hy_conv_b, 'hy_w1': hy_w1, 'hy_b1': hy_b1,
        'hy_freq1': hy_freq1, 'hy_w2': hy_w2, 'hy_b2': hy_b2, 'hy_freq2': hy_freq2,
        'hy_w3': hy_w3, 'hy_bias': hy_bias, 'od_w_in': od_w_in, 'c_q_norm': c_q_norm,
        'c_k_norm': c_k_norm, 'd_sink': d_sink,
    }
    y_prompt, ctx_state = run_trunk(x_prompt, c_ctx[None, :], p, None)
    new_state_dn, new_k_c, new_v_c, new_k_d, new_v_d = ctx_state
    y_sample, _ = run_trunk(x_sample, c, p, (state_dn, cache_k_c, cache_v_c, cache_k_d, cache_v_d))
    return (y_prompt, y_sample, new_state_dn, new_k_c, new_v_c, new_k_d, new_v_d)
```

```python
import math
import contextlib
import numpy as np
import ml_dtypes
import concourse.bass as bass
import concourse.mybir as mybir
from concourse.bass_utils import run_bass_kernel_spmd

F32 = mybir.dt.float32
BF16 = mybir.dt.bfloat16
AF = mybir.ActivationFunctionType
ALU = mybir.AluOpType
AX = mybir.AxisListType

NCORES = 8
D = 1024
T = 1536
SEQS = [(0, 256, 0), (256, 256, 0), (512, 1024, 1)]
DEPTH = 4
DFF = 2816
EPS = 1e-6
NEG = -1.0e30
DBG = {"layers": DEPTH, "dump": False}


class Buf:
    __slots__ = ("name", "lw", "rs", "excl", "lwx")

    carry = {}

    def __init__(self, name="b", excl=False):
        self.name = name
        self.lw = None
        self.lwx = {}
        self.rs = dict(Buf.carry)
        self.excl = excl


class K:
    def __init__(self, nc):
        self.nc = nc
        self.eng = {"pe": nc.tensor, "act": nc.scalar, "dve": nc.vector,
                    "pool": nc.gpsimd, "sp": nc.sync}
        self.sem = {}
        self.cnt = {}
        self.seen = {e: {} for e in self.eng}
        self._ctx = []
        for e in self.eng:
            self._newsem(e)
        self.n_instr = 0
        self.n_wait = 0
        Buf.carry = {}

    def _newsem(self, key):
        cm = self.nc.semaphore("s_" + key)
        s = cm.__enter__()
        self._ctx.append(cm)
        self.sem[key] = s
        self.cnt[key] = 0

    def _deps(self, e, reads, writes):
        deps = {}

        def add(k, v):
            if deps.get(k, 0) < v:
                deps[k] = v
        for b in reads:
            if b.lw is not None:
                add(*b.lw)
            for k, v in b.lwx.items():
                add(k, v)
            if b.excl:
                for k, v in b.rs.items():
                    if k != e:
                        add(k, v)
        for b in writes:
            if b.lw is not None:
                add(*b.lw)
            for k, v in b.lwx.items():
                add(k, v)
            for k, v in b.rs.items():
                add(k, v)
        if e == "pe":
            deps.pop("pe", None)
        return deps

    def _wait(self, e, deps):
        eng = self.eng[e]
        seen = self.seen[e]
        for k, v in deps.items():
            if seen.get(k, 0) < v:
                eng.wait_ge(self.sem[k], v)
                seen[k] = v
                self.n_wait += 1

    def op(self, e, fn, reads=(), writes=()):
        self._wait(e, self._deps(e, reads, writes))
        ins = fn(self.eng[e])
        self.cnt[e] += 1
        ins.then_inc(self.sem[e], 1)
        v = self.cnt[e]
        for b in writes:
            b.lw = (e, v)
            b.lwx = {}
            b.rs = {}
        for b in reads:
            if b.lw is None or b.lw != (e, v):
                b.rs[e] = v
        self.n_instr += 1
        return ins

    NDSEM = 56

    def dma(self, q, out, in_, reads=(), writes=(), stream="x"):
        if not hasattr(self, "_dn"):
            self._dn = {"p": 0, "h": 0}
        cls_, npool = ("p", 24) if q == "pool" else ("h", 32)
        key = "d_%s%d" % (cls_, self._dn[cls_] % npool)
        self._dn[cls_] += 1
        if key not in self.sem:
            self._newsem(key)
        deps = self._deps(key, reads, writes)
        if self.cnt[key] > 0:
            deps[key] = max(deps.get(key, 0), self.cnt[key])
        self._wait(q, deps)
        ins = self.eng[q].dma_start(out=out, in_=in_)
        self.cnt[key] += 16
        ins.then_inc(self.sem[key], 16)
        v = self.cnt[key]
        for b in writes:
            b.lwx[key] = v
            b.rs = {}
        for b in reads:
            b.rs[key] = v
        self.n_instr += 1
        return ins

    def barrier(self):
        Buf.carry = {k: c for k, c in self.cnt.items() if c > 0}

    def hard_barrier(self):
        for e in self.eng:
            deps = {k: c for k, c in self.cnt.items() if c > 0 and k != e}
            self._wait(e, deps)

    def finish(self):
        sp = self.eng["sp"]
        for k, s in self.sem.items():
            if self.cnt[k] > 0 and k != "sp":
                sp.wait_ge(s, self.cnt[k])
        for cm in reversed(self._ctx):
            cm.__exit__(None, None, None)


def _bf(a):
    return np.ascontiguousarray(a.astype(ml_dtypes.bfloat16))


def make_consts():
    c = {}
    eye = np.eye(128, dtype=np.float32)
    c["ident_f"] = eye
    c["ident_b"] = _bf(eye)
    c["ones_b"] = _bf(np.ones((128, 128), np.float32))
    c["ones_f"] = np.ones((128, 128), np.float32)
    j = np.arange(128)[:, None]
    i = np.arange(128)[None, :]
    c["tri_f"] = (j <= i).astype(np.float32)
    c["tri_b"] = (j >= i).astype(np.float32)
    c["m1_f"] = np.where(i < j, 0.0, NEG).astype(np.float32)
    c["m2_f"] = np.where(i >= j, 0.0, NEG).astype(np.float32)
    c["m1_b"] = np.where(i > j, 0.0, NEG).astype(np.float32)
    c["m2_b"] = np.where(i <= j, 0.0, NEG).astype(np.float32)
    rm = np.zeros((128, 128), np.float32)
    for d in range(128):
        if (d % 64) < 32:
            rm[d + 32, d] = -1.0
        else:
            rm[d - 32, d] = 1.0
    c["rope_rm"] = _bf(rm)
    L = 1024
    rows = L // 64
    row = np.repeat(np.arange(rows), 64).astype(np.float32)
    col = (np.arange(L) % 64).astype(np.float32)
    inv = (10000.0 ** (-np.arange(0, 64, 2, dtype=np.float32) / 64)).astype(np.float32)
    ang = np.concatenate([row[:, None] * inv, col[:, None] * inv], axis=-1)
    dd = np.arange(128)
    idx = (dd // 64) * 32 + (dd % 32)
    c["rope_c"] = np.ascontiguousarray(np.cos(ang)[:, idx].T.astype(np.float32))
    c["rope_s"] = np.ascontiguousarray(np.sin(ang)[:, idx].T.astype(np.float32))
    s = np.arange(128)[:, None]
    q = np.arange(512)[None, :]
    bm = np.stack([(np.abs(128 * rel + s - q) <= 128) for rel in range(-1, 5)]).astype(np.float32)
    c["band"] = _bf(bm)
    for Ls in (256, 1024):
        N = 2 * Ls
        t = np.arange(Ls, dtype=np.float64)[:, None]
        f = np.arange(Ls, dtype=np.float64)[None, :]
        th = 2.0 * np.pi / N
        Fc = np.cos(th * t * f)
        Fs = np.sin(th * t * f)
        Fs[:, 0] = (-1.0) ** np.arange(Ls)
        c[f"dft_f{Ls}"] = _bf(np.concatenate([Fc, Fs], axis=1))
        wf = np.full((Ls, 1), 2.0 / N)
        wf[0, 0] = 1.0 / N
        Ic = wf * Fc.T
        Is = (2.0 / N) * Fs.T
        Is[0, :] = (1.0 / N) * ((-1.0) ** np.arange(Ls))
        c[f"dft_i{Ls}"] = _bf(np.concatenate([Ic, Is], axis=0))
        tl = np.linspace(0.0, 1.0, Ls, dtype=np.float32)[:, None]
        w = ((2.0 * math.pi / Ls) * np.arange(Ls, dtype=np.float32))[:, None]
        fb = np.linspace(1e-4, 15, 16, dtype=np.float32)[None, :]
        feats = np.concatenate([tl, np.cos(fb * w), -np.sin(fb * w)], axis=-1).astype(np.float32)
        c[f"featsT{Ls}"] = np.ascontiguousarray(feats.T)
        deltas = np.abs(np.linspace(math.log(1e-2) / 1.5, math.log(1e-2) / 0.3, 512, dtype=np.float32))
        c[f"hydec{Ls}"] = np.exp(-tl * deltas[None, :]).astype(np.float32)
    return c


def fm(v, n=None):
    v = np.asarray(v, np.float32)
    return np.ascontiguousarray(v.reshape(-1, 128).T)


class Gen:
    def __init__(self, specs):
        self.nc = bass.Bass("TRN2", target_bir_lowering=False)
        nc = self.nc
        self.dr = {}
        for name, (shape, dt, kind) in specs.items():
            self.dr[name] = nc.dram_tensor(name, list(shape), dt, kind=kind).ap()
        self.es = contextlib.ExitStack()
        self.k = K(nc)
        self._uid = 0

    def sb(self, shape, dt, name=None, es=None):
        self._uid += 1
        nm = f"{name or 't'}_{self._uid}"
        if DBG.get("trace_alloc"):
            print("ALLOC", nm, shape, dt, int(np.prod(shape[1:])) * (2 if dt == BF16 else 4))
        t = (es or self.es).enter_context(self.nc.sbuf_tensor(nm, list(shape), dt))
        return t, Buf(nm)

    def setup_rings(self):
        nc = self.nc
        self.psf_ring = []
        for i in range(4):
            t = self.es.enter_context(nc.psum_tensor(f"psf{i}", [128, 512], F32))
            self.psf_ring.append((t, Buf(f"psf{i}", True)))
        self.psacc = []
        for i in range(2):
            t = self.es.enter_context(nc.psum_tensor(f"psacc{i}", [128, 512], F32))
            self.psacc.append((t, Buf(f"psacc{i}", True)))
        self.psb_ring = []
        for i in range(2):
            t = self.es.enter_context(nc.psum_tensor(f"psb{i}", [128, 1024], BF16))
            self.psb_ring.append((t, Buf(f"psb{i}", True)))
        self.slab_ring = [self.sb([128, 4096], BF16, "slab") for _ in range(4)]
        self._pf = self._pb = self._sl = 0

    def psf(self):
        r = self.psf_ring[self._pf % len(self.psf_ring)]
        self._pf += 1
        return r

    def psb(self):
        r = self.psb_ring[self._pb % len(self.psb_ring)]
        self._pb += 1
        return r

    def wload(self, w2d, c0, ncols, kc, segs=None):
        t, b = self.slab_ring[self._sl % len(self.slab_ring)]
        self._sl += 1
        view = t[:, 0:kc * ncols].rearrange("p (k n) -> p k n", k=kc)
        src = w2d.rearrange("(k p) n -> p k n", p=128)
        if segs is None:
            segs = [(c0, 0, ncols)]
        for (sc, dc, n) in segs:
            self.k.dma("pool", view[:, :, dc:dc + n], src[:, :, sc:sc + n], writes=[b], stream="w%d" % ((self._sl - 1) % 4))
        return view, b

    def dump(self, name, ap, buf, shape, dt=F32):
        if name not in DBG.get("extra_outputs", {}):
            return
        if dt != F32:
            with contextlib.ExitStack() as es3:
                t, b = self.sb(list(shape), F32, "dmp", es3)
                self.k.op("dve", lambda e: e.tensor_copy(out=t[:], in_=ap), reads=[buf], writes=[b])
                self.k.dma("sp", self.dr[name], t[:], reads=[b], stream="dump")
                self.k.barrier()
        else:
            self.k.dma("sp", self.dr[name], ap, reads=[buf], stream="dump")

    def op(self, e, fn, reads=(), writes=()):
        return self.k.op(e, fn, reads, writes)

    def load_const(self, name, shape, dt, q="sp"):
        t, b = self.sb(shape, dt, name)
        self.k.dma(q, t[:], self.dr[name], writes=[b], stream="c")
        return t, b


def build_program(specs):
    g = Gen(specs)
    nc, k, dr, op = g.nc, g.k, g.dr, g.op
    with g.es:
        g.setup_rings()
        ident_f, Bc = g.load_const("ident_f", [128, 128], F32)
        CB = Bc

        def lc(name, shape, dt):
            t, b = g.sb(shape, dt, name)
            k.dma("sp", t[:], dr[name], writes=[CB], stream="c")
            return t
        ident_b = lc("ident_b", [128, 128], BF16)
        ones_b = lc("ones_b", [128, 128], BF16)
        ones_f = lc("ones_f", [128, 128], F32)
        tri = [lc("tri_f", [128, 128], F32), lc("tri_b", [128, 128], F32)]
        m1 = [lc("m1_f", [128, 128], F32), lc("m1_b", [128, 128], F32)]
        m2 = [lc("m2_f", [128, 128], F32), lc("m2_b", [128, 128], F32)]
        pv = {}
        for nm in ("norm_mix", "norm_ffn"):
            pv[nm] = lc(nm, [128, DEPTH, 8], F32)
        pv["final_norm"] = lc("final_norm", [128, 8], F32)
        pv["b_mod"] = lc("b_mod", [128, DEPTH, 48, 2], F32)
        pv["ffn_conv_w"] = lc("ffn_conv_w", [128, DEPTH, 3, 22], F32)
        pv["ffn_conv_b"] = lc("ffn_conv_b", [128, DEPTH, 22], F32)
        pv["csil"] = lc("cvec", [128, 8, 2], F32)

        xT, Bx = g.sb([128, 8, T], F32, "xT")
        hT, Bh = g.sb([128, 8, T], BF16, "hT")
        mixT, Bmix = g.sb([128, 8, T], BF16, "mixT")
        modv, Bmod = g.sb([128, 48, 2], F32, "modv")
        modA, BmodA = g.sb([128, 2, 8, 2], F32, "modA")
        cbf, Bcbf = g.sb([128, 8, 2], BF16, "cbf")
        rstd, Brstd = g.sb([128, 512], F32, "rstd")
        sqb = [g.sb([128, 512], BF16, "sq") for _ in range(2)]
        tmpf = [g.sb([128, 512], F32, "tmpf") for _ in range(3)]
        _rr = {"sq": 0, "tmp": 0}

        def nsq():
            _rr["sq"] += 1
            return sqb[_rr["sq"] % 2]

        def ntmp():
            _rr["tmp"] += 1
            return tmpf[_rr["tmp"] % 3]

        op("act", lambda e: e.activation(out=cbf[:], in_=pv["csil"][:], func=AF.Silu), reads=[CB], writes=[Bcbf])

        with contextlib.ExitStack() as es2:
            xin = [g.sb([128, D], F32, "xin", es2) for _ in range(2)]
            for tt in range(T // 128):
                xt, bx = xin[tt % 2]
                k.dma("sp", xt[:], dr["x_all"][tt * 128:(tt + 1) * 128, :], writes=[bx], stream="xin%d" % (tt % 2))
                for half in range(2):
                    pt, bp = g.psf()
                    for jj in range(4):
                        kc = half * 4 + jj
                        op("pe", lambda e: e.transpose(out=pt[:, jj * 128:(jj + 1) * 128], in_=xt[:, kc * 128:(kc + 1) * 128], identity=ident_f[:]), reads=[bx, CB], writes=[bp])
                    op("dve" if half == 0 else "act",
                       (lambda e: e.tensor_copy(out=xT[:, half * 4:half * 4 + 4, tt * 128:(tt + 1) * 128], in_=pt[:].rearrange("p (a b) -> p a b", a=4))) if half == 0 else
                       (lambda e: e.activation(out=xT[:, half * 4:half * 4 + 4, tt * 128:(tt + 1) * 128], in_=pt[:].rearrange("p (a b) -> p a b", a=4), func=AF.Copy)),
                       reads=[bp], writes=[Bx])
            k.barrier()

        def modulation(i):
            pm, bpm = g.psacc[0]
            for sl in range(12):
                wv, bw = g.wload(dr["w_mod"][i], sl * 512, 512, 8)
                for cc in range(4):
                    ch = sl * 4 + cc
                    for kc in range(8):
                        op("pe", lambda e: e.matmul(pm[:, ch * 2:ch * 2 + 2], lhsT=wv[:, kc, cc * 128:(cc + 1) * 128], rhs=cbf[:, kc, :], start=(kc == 0), stop=(kc == 7)), reads=[bw, Bcbf], writes=[bpm])
            op("dve", lambda e: e.tensor_tensor(out=modv[:].rearrange("p a b -> p (a b)"), in0=pm[:, 0:96], in1=pv["b_mod"][:, i].rearrange("p a b -> p (a b)"), op=ALU.add), reads=[bpm, CB], writes=[Bmod])
            for which, (gname, sc0) in enumerate((("norm_mix", 8), ("norm_ffn", 32))):
                for r in range(2):
                    op("dve", lambda e: e.tensor_scalar(out=modA[:, which, :, r], in0=modv[:, sc0:sc0 + 8, r], scalar1=1.0, scalar2=math.sqrt(D), op0=ALU.add, op1=ALU.mult), reads=[Bmod], writes=[BmodA])
                    op("dve", lambda e: e.tensor_tensor(out=modA[:, which, :, r], in0=modA[:, which, :, r], in1=pv[gname][:, i, :], op=ALU.mult), reads=[BmodA, CB], writes=[BmodA])

        epsc = {}

        def eps_col(val):
            if val not in epsc:
                t, b = g.sb([128, 1], F32, "epsc")
                op("pool", lambda e: e.memset(t[:], float(val)), writes=[b])
                epsc[val] = (t, b)
            return epsc[val]

        for _v in (D * EPS, EPS, 128 * EPS):
            eps_col(float(_v))

        def ss_to_rstd(ps_ap, out_ap, bps, bout, n, epsn, scale=1.0):
            et, eb = eps_col(float(n * epsn))
            op("act", lambda e: e.activation(out=out_ap, in_=ps_ap, func=AF.Sqrt, bias=et[:, 0:1], scale=float(scale)), reads=[bps, eb], writes=[bout])
            op("dve", lambda e: e.reciprocal(out=out_ap, in_=out_ap), reads=[bout], writes=[bout])

        def adaln(which, sh0):
            for tg in range(3):
                r = 0 if tg == 0 else 1
                ts = slice(tg * 512, (tg + 1) * 512)
                pss, bpss = g.psf()
                for kc in range(8):
                    sq, bsq = nsq()
                    op("act", lambda e: e.activation(out=sq[:], in_=xT[:, kc, ts], func=AF.Square), reads=[Bx], writes=[bsq])
                    op("pe", lambda e: e.matmul(pss[:], lhsT=ones_b[:], rhs=sq[:], start=(kc == 0), stop=(kc == 7)), reads=[bsq, CB], writes=[bpss])
                ss_to_rstd(pss[:], rstd[:], bpss, Brstd, D, EPS)
                for kc in range(8):
                    tm, btm = ntmp()
                    op("dve", lambda e: e.tensor_tensor(out=tm[:], in0=xT[:, kc, ts], in1=rstd[:], op=ALU.mult), reads=[Bx, Brstd], writes=[btm])
                    op("act", lambda e: e.activation(out=hT[:, kc, ts], in_=tm[:], func=AF.Identity, scale=modA[:, which, kc, r:r + 1], bias=modv[:, sh0 + kc, r:r + 1]), reads=[btm, BmodA, Bmod], writes=[Bh])

        def proj_residual(w2d, kcn, src, bsrc, g0):
            ncol = 512 if kcn <= 8 else 128
            for c0 in range(0, D, ncol):
                wv, bw = g.wload(w2d, c0, ncol, kcn)
                for mm in range(ncol // 128):
                    m = c0 // 128 + mm
                    for tg in range(3):
                        r = 0 if tg == 0 else 1
                        ts = slice(tg * 512, (tg + 1) * 512)
                        pp, bpp = g.psf()
                        for kc in range(kcn):
                            op("pe", lambda e: e.matmul(pp[:], lhsT=wv[:, kc, mm * 128:(mm + 1) * 128], rhs=src[:, kc, ts], start=(kc == 0), stop=(kc == kcn - 1)), reads=[bw, bsrc], writes=[bpp])
                        op("dve", lambda e: e.scalar_tensor_tensor(out=xT[:, m, ts], in0=pp[:], scalar=modv[:, g0 + m, r:r + 1], in1=xT[:, m, ts], op0=ALU.mult, op1=ALU.add), reads=[bpp, Bmod, Bx], writes=[Bx])

        def ffn(i):
            with contextlib.ExitStack() as es2:
                gbuf, Bg = g.sb([128, 11, T], BF16, "gbuf", es2)
                abuf = [g.sb([128, T], F32, "abuf", es2) for _ in range(2)]
                cbuf = [g.sb([128, T], F32, "cbuf", es2) for _ in range(2)]
                cw = pv["ffn_conv_w"]
                for half in range(2):
                    slabs = {}

                    def part_a(cp):
                        c = half * 11 + cp
                        wv, bw = g.wload(dr["ffn_w_up"][i], 0, 256, 8, segs=[(c * 128, 0, 128), (DFF + c * 128, 128, 128)])
                        slabs[cp] = (wv, bw)
                        ab, bab = abuf[c % 2]
                        cb_, bcb = cbuf[c % 2]
                        for tg in range(3):
                            ts = slice(tg * 512, (tg + 1) * 512)
                            pa, bpa = g.psf()
                            for kc in range(8):
                                op("pe", lambda e: e.matmul(pa[:], lhsT=wv[:, kc, 0:128], rhs=hT[:, kc, ts], start=(kc == 0), stop=(kc == 7)), reads=[bw, Bh], writes=[bpa])
                            op("act", lambda e: e.activation(out=ab[:, ts], in_=pa[:], func=AF.Copy), reads=[bpa], writes=[bab])
                        for (s0, ln, _) in SEQS:
                            op("dve", lambda e: e.tensor_scalar(out=cb_[:, s0:s0 + ln], in0=ab[:, s0:s0 + ln], scalar1=cw[:, i, 1, c:c + 1], scalar2=pv["ffn_conv_b"][:, i, c:c + 1], op0=ALU.mult, op1=ALU.add), reads=[bab, CB], writes=[bcb])
                            op("dve", lambda e: e.scalar_tensor_tensor(out=cb_[:, s0 + 1:s0 + ln], in0=ab[:, s0:s0 + ln - 1], scalar=cw[:, i, 0, c:c + 1], in1=cb_[:, s0 + 1:s0 + ln], op0=ALU.mult, op1=ALU.add), reads=[bab, CB, bcb], writes=[bcb])
                            op("dve", lambda e: e.scalar_tensor_tensor(out=cb_[:, s0:s0 + ln - 1], in0=ab[:, s0 + 1:s0 + ln], scalar=cw[:, i, 2, c:c + 1], in1=cb_[:, s0:s0 + ln - 1], op0=ALU.mult, op1=ALU.add), reads=[bab, CB, bcb], writes=[bcb])
                        op("act", lambda e: e.activation(out=ab[:], in_=cb_[:], func=AF.Silu), reads=[bcb], writes=[bab])

                    def part_b(cp):
                        c = half * 11 + cp
                        wv, bw = slabs.pop(cp)
                        ab, bab = abuf[c % 2]
                        for tg in range(3):
                            ts = slice(tg * 512, (tg + 1) * 512)
                            pb_, bpb = g.psf()
                            for kc in range(8):
                                op("pe", lambda e: e.matmul(pb_[:], lhsT=wv[:, kc, 128:256], rhs=hT[:, kc, ts], start=(kc == 0), stop=(kc == 7)), reads=[bw, Bh], writes=[bpb])
                            op("dve", lambda e: e.tensor_tensor(out=gbuf[:, cp, ts], in0=pb_[:], in1=ab[:, ts], op=ALU.mult), reads=[bpb, bab], writes=[Bg])

                    part_a(0)
                    for cp in range(11):
                        if cp + 1 < 11:
                            part_a(cp + 1)
                        part_b(cp)
                    wd = dr["ffn_w_down"][i][half * 11 * 128:(half + 1) * 11 * 128, :]
                    proj_residual(wd, 11, gbuf, Bg, 40)
                k.barrier()

        import sys
        nl = DBG["layers"]
        for i in range(nl):
            modulation(i)
            adaln(0, 0)
            if i % 2 == 0:
                EVEN_MIXER(g, locals(), i)
            else:
                ODD_MIXER(g, locals(), i)
            proj_residual(dr["w_out"][i], 8, mixT, Bmix, 16)
            adaln(1, 24)
            ffn(i)

        with contextlib.ExitStack() as es2:
            yout = [g.sb([128, D], F32, "yout", es2) for _ in range(2)]
            fin, Bfin = g.sb([128, 8, 512], F32, "fin", es2)
            for tg in range(3):
                ts = slice(tg * 512, (tg + 1) * 512)
                pss, bpss = g.psf()
                for kc in range(8):
                    sq, bsq = nsq()
                    op("act", lambda e: e.activation(out=sq[:], in_=xT[:, kc, ts], func=AF.Square), reads=[Bx], writes=[bsq])
                    op("pe", lambda e: e.matmul(pss[:], lhsT=ones_b[:], rhs=sq[:], start=(kc == 0), stop=(kc == 7)), reads=[bsq, CB], writes=[bpss])
                ss_to_rstd(pss[:], rstd[:], bpss, Brstd, 1, EPS, scale=1.0 / D)
                for kc in range(8):
                    op("dve", lambda e: e.scalar_tensor_tensor(out=fin[:, kc, :], in0=xT[:, kc, ts], scalar=pv["final_norm"][:, kc:kc + 1], in1=rstd[:], op0=ALU.mult, op1=ALU.mult), reads=[Bx, CB, Brstd], writes=[Bfin])
                for t4 in range(4):
                    tt = tg * 4 + t4
                    yo, byo = yout[tt % 2]
                    for half in range(2):
                        pt, bp = g.psf()
                        for jj in range(4):
                            kc = half * 4 + jj
                            op("pe", lambda e: e.transpose(out=pt[:, jj * 128:(jj + 1) * 128], in_=fin[:, kc, t4 * 128:(t4 + 1) * 128], identity=ident_f[:]), reads=[Bfin, CB], writes=[bp])
                        op("act" if half else "dve",
                           (lambda e: e.activation(out=yo[:, half * 512:(half + 1) * 512], in_=pt[:], func=AF.Copy)) if half else
                           (lambda e: e.tensor_copy(out=yo[:, half * 512:(half + 1) * 512], in_=pt[:])),
                           reads=[bp], writes=[byo])
                    k.dma("sp", dr["y_all"][tt * 128:(tt + 1) * 128, :], yo[:], reads=[byo], stream="yo%d" % (tt % 2))
            k.barrier()
        k.finish()
    print("program: instr", k.n_instr, "waits", k.n_wait)
    return nc


def _zero_fill(g, L, names):
    k, dr, op = L["k"], L["dr"], L["op"]
    with contextlib.ExitStack() as es2:
        z, bz = g.sb([128, 2048], F32, "zfill", es2)
        op("pool", lambda e: e.memset(z[:], 0.0), writes=[bz])
        for nm, view in names:
            k.dma("sp", view, z[:, 0:view.shape[-1]] if len(view.shape) == 2 else z[:].rearrange("p (a b) -> p a b", b=view.shape[-1])[:, 0:view.shape[1], :], reads=[bz], stream="zf")
        k.barrier()


def EVEN_MIXER(g, L, i):
    k, dr, op = L["k"], L["dr"], L["op"]
    hT, Bh, mixT, Bmix, CB = L["hT"], L["Bh"], L["mixT"], L["Bmix"], L["CB"]
    ones_b, ones_f, ident_b = L["ones_b"], L["ones_f"], L["ident_b"]
    tri, m1, m2 = L["tri"], L["m1"], L["m2"]
    ss_to_rstd = L["ss_to_rstd"]
    sqb, tmpf = L["sqb"], L["tmpf"]
    j = i // 2
    W = dr["ev_w_in"][j]
    with contextlib.ExitStack() as es:
        def sb(shape, dt, nm, es_=None):
            return g.sb(shape, dt, nm, es_ or es)
        cw, Bcw = sb([128, 3, 12], F32, "dncw")
        dtb, Bdtb = sb([128, 8], F32, "dtb")
        negA, BnegA = sb([128, 8], F32, "negA")
        dnn, Bdnn = sb([128, 1], F32, "dnn")
        beta, Bbeta = sb([128, 12, 8], F32, "beta")
        gg, Bgg = sb([128, 12, 8], F32, "gg")
        gcum, Bgcum = sb([128, 12, 8], F32, "gcum")
        bg, Bbg = sb([128, 12, 8], F32, "bg")
        k.dma("sp", cw[:], dr["dn_conv_w_fm"][j], writes=[Bcw], stream="ec")
        k.dma("sp", dtb[:], dr["dn_dt_bias_bc"][:, j, :], writes=[Bdtb], stream="ec")
        k.dma("sp", negA[:], dr["dn_a_log_bc"][:, j, :], writes=[BnegA], stream="ec")
        k.dma("sp", dnn[:], dr["dn_norm_fm"][j], writes=[Bdnn], stream="ec")
        op("act", lambda e: e.activation(out=negA[:], in_=negA[:], func=AF.Exp), reads=[BnegA], writes=[BnegA])
        op("dve", lambda e: e.tensor_scalar(out=negA[:], in0=negA[:], scalar1=-1.0, scalar2=None, op0=ALU.mult), reads=[BnegA], writes=[BnegA])
        op("dve", lambda e: e.tensor_scalar(out=dnn[:], in0=dnn[:], scalar1=math.sqrt(128.0), scalar2=None, op0=ALU.mult), reads=[Bdnn], writes=[Bdnn])
        wba, bwba = g.wload(W, 2048, 16, 8)
        for tt in range(12):
            pp, bpp = g.psf()
            for kc in range(8):
                op("pe", lambda e: e.matmul(pp[:, 0:16], lhsT=hT[:, kc, tt * 128:(tt + 1) * 128], rhs=wba[:, kc, :], start=(kc == 0), stop=(kc == 7)), reads=[bwba, Bh], writes=[bpp])
            op("act", lambda e: e.activation(out=beta[:, tt, :], in_=pp[:, 0:8], func=AF.Sigmoid), reads=[bpp], writes=[Bbeta])
            op("dve", lambda e: e.tensor_tensor(out=gg[:, tt, :], in0=pp[:, 8:16], in1=dtb[:], op=ALU.add), reads=[bpp, Bdtb], writes=[Bgg])
        op("act", lambda e: e.activation(out=gg[:], in_=gg[:], func=AF.Exp), reads=[Bgg], writes=[Bgg])
        op("act", lambda e: e.activation(out=gg[:], in_=gg[:], func=AF.Ln, bias=ones_f[:, 0:1]), reads=[Bgg, CB], writes=[Bgg])
        for tt in range(12):
            op("dve", lambda e: e.tensor_tensor(out=gg[:, tt, :], in0=gg[:, tt, :], in1=negA[:], op=ALU.mult), reads=[Bgg, BnegA], writes=[Bgg])
        for tt in range(12):
            pp, bpp = g.psf()
            for d_ in range(2):
                op("pe", lambda e: e.matmul(pp[:, d_ * 4:d_ * 4 + 4], lhsT=tri[d_][:], rhs=gg[:, tt, d_ * 4:d_ * 4 + 4], start=True, stop=True), reads=[CB, Bgg], writes=[bpp])
            op("dve", lambda e: e.tensor_copy(out=gcum[:, tt, :], in_=pp[:, 0:8]), reads=[bpp], writes=[Bgcum])
        op("act", lambda e: e.activation(out=bg[:], in_=gcum[:], func=AF.Exp), reads=[Bgcum], writes=[Bbg])
        op("dve", lambda e: e.tensor_tensor(out=bg[:], in0=bg[:], in1=beta[:], op=ALU.mult), reads=[Bbg, Bbeta], writes=[Bbg])

        g.dump("d_beta", beta[:], Bbeta, [128, 12, 8])
        g.dump("d_gg", gg[:], Bgg, [128, 12, 8])
        g.dump("d_gcum", gcum[:], Bgcum, [128, 12, 8])
        for h in range(4):
            with contextlib.ExitStack() as eh:
                qT, Bq = sb([128, T], BF16, "dq", eh)
                kT, Bk = sb([128, T], BF16, "dk", eh)
                vT, Bv = sb([128, T], BF16, "dv", eh)
                zs, Bzs = sb([128, T], F32, "dz", eh)
                oacc, Bo = sb([128, T], F32, "doa", eh)
                ktok, Bkt = sb([128, 12, 128], BF16, "dkt", eh)
                vtok, Bvt = sb([128, 12, 128], BF16, "dvt", eh)
                wv, bw = g.wload(W, 0, 512, 8, segs=[(h * 128, 0, 128), (512 + h * 128, 128, 128), (1024 + h * 128, 256, 128), (1536 + h * 128, 384, 128)])
                with contextlib.ExitStack() as e1:
                    raw, Braw = sb([128, T], F32, "draw", e1)
                    cv, Bcv = sb([128, T], F32, "dcv", e1)
                    for comp in range(4):
                        for tg in range(3):
                            ts = slice(tg * 512, (tg + 1) * 512)
                            pp, bpp = g.psf()
                            for kc in range(8):
                                op("pe", lambda e: e.matmul(pp[:], lhsT=wv[:, kc, comp * 128:(comp + 1) * 128], rhs=hT[:, kc, ts], start=(kc == 0), stop=(kc == 7)), reads=[bw, Bh], writes=[bpp])
                            if comp == 3:
                                op("act", lambda e: e.activation(out=zs[:, ts], in_=pp[:], func=AF.Silu), reads=[bpp], writes=[Bzs])
                            else:
                                op("act", lambda e: e.activation(out=raw[:, ts], in_=pp[:], func=AF.Copy), reads=[bpp], writes=[Braw])
                        if comp == 3:
                            continue
                        cc = comp * 4 + h
                        for (s0, ln, _) in SEQS:
                            op("dve", lambda e: e.tensor_scalar(out=cv[:, s0:s0 + ln], in0=raw[:, s0:s0 + ln], scalar1=cw[:, 1, cc:cc + 1], scalar2=None, op0=ALU.mult), reads=[Braw, Bcw], writes=[Bcv])
                            op("dve", lambda e: e.scalar_tensor_tensor(out=cv[:, s0 + 1:s0 + ln], in0=raw[:, s0:s0 + ln - 1], scalar=cw[:, 0, cc:cc + 1], in1=cv[:, s0 + 1:s0 + ln], op0=ALU.mult, op1=ALU.add), reads=[Braw, Bcw, Bcv], writes=[Bcv])
                            op("dve", lambda e: e.scalar_tensor_tensor(out=cv[:, s0:s0 + ln - 1], in0=raw[:, s0 + 1:s0 + ln], scalar=cw[:, 2, cc:cc + 1], in1=cv[:, s0:s0 + ln - 1], op0=ALU.mult, op1=ALU.add), reads=[Braw, Bcw, Bcv], writes=[Bcv])
                        op("act", lambda e: e.activation(out=cv[:], in_=cv[:], func=AF.Silu), reads=[Bcv], writes=[Bcv])
                        if comp == 2:
                            op("dve", lambda e: e.tensor_copy(out=vT[:], in_=cv[:]), reads=[Bcv], writes=[Bv])
                            continue
                        dst, bdst = (qT, Bq) if comp == 0 else (kT, Bk)
                        for tg in range(3):
                            ts = slice(tg * 512, (tg + 1) * 512)
                            sq, bsq = sqb[tg % 2]
                            op("act", lambda e: e.activation(out=sq[:], in_=cv[:, ts], func=AF.Square), reads=[Bcv], writes=[bsq])
                            p2, bp2 = g.psf()
                            op("pe", lambda e: e.matmul(p2[:], lhsT=ones_b[:], rhs=sq[:], start=True, stop=True), reads=[bsq, CB], writes=[bp2])
                            rs_, brs_ = tmpf[tg % 3]
                            ss_to_rstd(p2[:], rs_[:], bp2, brs_, 1, EPS)
                            if comp == 0:
                                op("dve", lambda e: e.scalar_tensor_tensor(out=dst[:, ts], in0=cv[:, ts], scalar=128.0 ** -0.5, in1=rs_[:], op0=ALU.mult, op1=ALU.mult), reads=[Bcv, brs_], writes=[bdst])
                            else:
                                op("dve", lambda e: e.tensor_tensor(out=dst[:, ts], in0=cv[:, ts], in1=rs_[:], op=ALU.mult), reads=[Bcv, brs_], writes=[bdst])
                    k.barrier()
                if h == 0:
                    g.dump("d_qT", qT[:], Bq, [128, T], BF16)
                    g.dump("d_kT", kT[:], Bk, [128, T], BF16)
                    g.dump("d_vT", vT[:], Bv, [128, T], BF16)
                for tt in range(12):
                    ptb, bptb = g.psb()
                    op("pe", lambda e: e.transpose(out=ptb[:, 0:128], in_=kT[:, tt * 128:(tt + 1) * 128], identity=ident_b[:]), reads=[Bk, CB], writes=[bptb])
                    op("pe", lambda e: e.transpose(out=ptb[:, 128:256], in_=vT[:, tt * 128:(tt + 1) * 128], identity=ident_b[:]), reads=[Bv, CB], writes=[bptb])
                    op("dve", lambda e: e.tensor_copy(out=ktok[:, tt, :], in_=ptb[:, 0:128]), reads=[bptb], writes=[Bkt])
                    op("act", lambda e: e.activation(out=vtok[:, tt, :], in_=ptb[:, 128:256], func=AF.Copy), reads=[bptb], writes=[Bvt])

                groups = []
                groups.append(([(0, 0), (1, 0), (2, 0), (3, 0)], [(0, 0, [0, 1], True, True), (1, 0, [2, 3], True, True)]))
                groups.append(([(0, 1), (1, 1), (2, 1), (3, 1)], [(0, 1, [1, 0], True, True), (1, 1, [3, 2], True, True)]))
                groups.append(([(4 + c, 0) for c in range(4)], [(2, 0, [0, 1, 2, 3], True, False)]))
                groups.append(([(8 + c, 0) for c in range(4)], [(2, 0, [0, 1, 2, 3], False, True)]))
                groups.append(([(8 + c, 1) for c in range(4)], [(2, 1, [3, 2, 1, 0], True, False)]))
                groups.append(([(4 + c, 1) for c in range(4)], [(2, 1, [3, 2, 1, 0], False, True)]))
                Ss = [sb([128, 128], F32, "S", eh) for _ in range(2)]
                Sbs = [sb([128, 128], BF16, "Sb", eh) for _ in range(2)]
                for chains, scans in groups:
                    with contextlib.ExitStack() as e2:
                        NCH = len(chains)
                        Az, BAz = sb([128, NCH, 2, 128], F32, "Az", e2)
                        PP = [sb([128, NCH, 2, 128], F32, "PP", e2) for _ in range(2)]
                        RTf, BRTf = sb([128, NCH, 128], F32, "RTf", e2)
                        RT, BRT = sb([128, NCH, 128], BF16, "RT", e2)
                        attnT, Bat = sb([128, NCH, 128], BF16, "attnT", e2)
                        kbg, Bkbg = sb([128, NCH, 128], BF16, "kbg", e2)
                        vb, Bvb = sb([128, NCH, 128], BF16, "vb", e2)
                        kdec, Bkd = sb([128, NCH, 128], BF16, "kdec", e2)
                        qgT, Bqg = sb([128, NCH, 128], BF16, "qgT", e2)
                        glb, Bglb = sb([128, NCH], F32, "glb", e2)
                        kde, Bkde = sb([128, NCH], F32, "kde", e2)
                        Lg = [sb([128, 128], F32, "Lg", e2) for _ in range(2)]
                        D1 = [sb([128, 128], F32, "D1", e2) for _ in range(2)]
                        D2 = [sb([128, 128], F32, "D2", e2) for _ in range(2)]
                        EG = [sb([128, 128], F32, "EG", e2) for _ in range(2)]
                        nwTa, BnwT = sb([128, NCH, 128], BF16, "nwTa", e2)
                        vnews = [sb([128, 128], BF16, "vnew", e2) for _ in range(2)]
                        BSZ = 2
                        for b0 in range(0, NCH, BSZ):
                            batch = [(c, chains[c][0], chains[c][1]) for c in range(b0, min(NCH, b0 + BSZ))]
                            st = {}
                            for (c, tt, d_) in batch:
                                col = d_ * 4 + h
                                lg, blg = Lg[c % 2]
                                op("act", lambda e: e.activation(out=lg[:], in_=ones_f[:], func=AF.Identity, scale=gg[:, tt, col:col + 1]), reads=[CB, Bgg], writes=[blg])
                            for (c, tt, d_) in batch:
                                tsl = slice(tt * 128, (tt + 1) * 128)
                                lg, blg = Lg[c % 2]
                                pG, bpG = g.psf()
                                op("pe", lambda e: e.matmul(pG[:, 0:128], lhsT=lg[:], rhs=tri[d_][:], start=True, stop=True), reads=[blg, CB], writes=[bpG])
                                pK, bpK = g.psf()
                                op("pe", lambda e: e.matmul(pK[:, 0:128], lhsT=kT[:, tsl], rhs=kT[:, tsl], start=True, stop=True), reads=[Bk], writes=[bpK])
                                op("pe", lambda e: e.matmul(pK[:, 128:256], lhsT=kT[:, tsl], rhs=qT[:, tsl], start=True, stop=True), reads=[Bk, Bq], writes=[bpK])
                                st[c] = (pG, bpG, pK, bpK)
                            for (c, tt, d_) in batch:
                                col = d_ * 4 + h
                                gcol = gcum[:, tt, col:col + 1]
                                pG, bpG, pK, bpK = st[c]
                                d1, bd1 = D1[c % 2]
                                d2, bd2 = D2[c % 2]
                                eg, beg = EG[c % 2]
                                lastc = 127 if d_ == 0 else 0
                                op("dve", lambda e: e.scalar_tensor_tensor(out=d1[:], in0=pG[:, 0:128], scalar=gcol, in1=m1[d_][:], op0=ALU.subtract, op1=ALU.subtract), reads=[bpG, Bgcum, CB], writes=[bd1])
                                op("dve", lambda e: e.scalar_tensor_tensor(out=d2[:], in0=pG[:, 0:128], scalar=gcol, in1=m2[d_][:], op0=ALU.subtract, op1=ALU.add), reads=[bpG, Bgcum, CB], writes=[bd2])
                                op("dve", lambda e: e.tensor_scalar(out=kde[:, c:c + 1], in0=pG[:, lastc:lastc + 1], scalar1=gcol, scalar2=None, op0=ALU.subtract), reads=[bpG, Bgcum], writes=[Bkde])
                                op("act", lambda e: e.activation(out=eg[:], in_=pG[:, 0:128], func=AF.Exp), reads=[bpG], writes=[beg])
                                op("act", lambda e: e.activation(out=d1[:], in_=d1[:], func=AF.Exp, scale=-1.0), reads=[bd1], writes=[bd1])
                                op("act", lambda e: e.activation(out=d2[:], in_=d2[:], func=AF.Exp), reads=[bd2], writes=[bd2])
                                op("act", lambda e: e.activation(out=kde[:, c:c + 1], in_=kde[:, c:c + 1], func=AF.Exp), reads=[Bkde], writes=[Bkde])
                            for (c, tt, d_) in batch:
                                col = d_ * 4 + h
                                tsl = slice(tt * 128, (tt + 1) * 128)
                                pG, bpG, pK, bpK = st[c]
                                d1, bd1 = D1[c % 2]
                                d2, bd2 = D2[c % 2]
                                eg, beg = EG[c % 2]
                                lastc = 127 if d_ == 0 else 0
                                op("dve", lambda e: e.scalar_tensor_tensor(out=Az[:, c, 0, :], in0=pK[:, 0:128], scalar=beta[:, tt, col:col + 1], in1=d1[:], op0=ALU.mult, op1=ALU.mult), reads=[bpK, Bbeta, bd1], writes=[BAz])
                                op("dve", lambda e: e.tensor_tensor(out=attnT[:, c, :], in0=pK[:, 128:256], in1=d2[:], op=ALU.mult), reads=[bpK, bd2], writes=[Bat])
                                op("dve", lambda e: e.tensor_copy(out=glb[:, c:c + 1], in_=eg[:, lastc:lastc + 1]), reads=[beg], writes=[Bglb])
                                op("pool", lambda e: e.tensor_tensor(out=qgT[:, c, :], in0=qT[:, tsl], in1=eg[:], op=ALU.mult), reads=[Bq, beg], writes=[Bqg])
                            for (c, tt, d_) in batch:
                                ptb, bptb = g.psf()
                                op("pe", lambda e: e.transpose(out=ptb[:, 0:128], in_=Az[:, c, 0, :], identity=L["ident_f"][:]), reads=[BAz, CB], writes=[bptb])
                                op("dve", lambda e: e.tensor_copy(out=Az[:, c, 1, :], in_=ptb[:, 0:128]), reads=[bptb], writes=[BAz])
                                op("dve", lambda e: e.tensor_tensor(out=RTf[:, c, :], in0=L["ident_f"][:], in1=ptb[:, 0:128], op=ALU.subtract), reads=[bptb, CB], writes=[BRTf])
                            for (c, tt, d_) in batch:
                                col = d_ * 4 + h
                                op("dve", lambda e: e.tensor_scalar(out=kbg[:, c, :], in0=ktok[:, tt, :], scalar1=bg[:, tt, col:col + 1], scalar2=None, op0=ALU.mult), reads=[Bkt, Bbg], writes=[Bkbg])
                                op("act", lambda e: e.activation(out=vb[:, c, :], in_=vtok[:, tt, :], func=AF.Identity, scale=beta[:, tt, col:col + 1]), reads=[Bvt, Bbeta], writes=[Bvb])
                                op("dve", lambda e: e.tensor_scalar(out=kdec[:, c, :], in0=ktok[:, tt, :], scalar1=kde[:, c:c + 1], scalar2=None, op0=ALU.mult), reads=[Bkt, Bkde], writes=[Bkd])
                        NP = NCH // 2
                        BPPp = [[Buf("PPp") for _ in range(NP)] for _ in range(2)]
                        BRp = [Buf("RTp") for _ in range(NP)]
                        cur = Az
                        bcur = [[BAz] for _ in range(NP)]
                        for step in range(6):
                            nxt = PP[step % 2][0]
                            bnx = BPPp[step % 2]
                            last = step == 5
                            for p in range(NP):
                                pp, bpp = g.psf()
                                for cc_ in range(2):
                                    c = 2 * p + cc_
                                    op("pe", lambda e: e.matmul(pp[:, cc_ * 256:cc_ * 256 + 128], lhsT=cur[:, c, 1, :], rhs=cur[:, c, 0, :], start=True, stop=True), reads=bcur[p], writes=[bpp])
                                    if not last:
                                        op("pe", lambda e: e.matmul(pp[:, cc_ * 256 + 128:cc_ * 256 + 256], lhsT=cur[:, c, 0, :], rhs=cur[:, c, 1, :], start=True, stop=True), reads=bcur[p], writes=[bpp])
                                if last:
                                    op("dve" if p % 2 == 0 else "act",
                                       (lambda e: e.tensor_copy(out=nxt[:, 2 * p:2 * p + 2, 0, :], in_=pp[:].rearrange("p (a b c) -> p a b c", a=2, b=2)[:, :, 0, :])) if p % 2 == 0 else
                                       (lambda e: e.activation(out=nxt[:, 2 * p:2 * p + 2, 0, :], in_=pp[:].rearrange("p (a b c) -> p a b c", a=2, b=2)[:, :, 0, :], func=AF.Copy)),
                                       reads=[bpp], writes=[bnx[p]])
                                elif p % 2 == 0:
                                    op("dve", lambda e: e.tensor_copy(out=nxt[:, 2 * p:2 * p + 2].rearrange("p a b c -> p (a b c)"), in_=pp[:]), reads=[bpp], writes=[bnx[p]])
                                else:
                                    op("act", lambda e: e.activation(out=nxt[:, 2 * p:2 * p + 2].rearrange("p a b c -> p (a b c)"), in_=pp[:], func=AF.Copy), reads=[bpp], writes=[bnx[p]])
                            for p in range(NP):
                                pr, bpr = g.psf()
                                for cc_ in range(2):
                                    c = 2 * p + cc_
                                    op("pe", lambda e: e.matmul(pr[:, cc_ * 128:(cc_ + 1) * 128], lhsT=nxt[:, c, 0, :], rhs=RTf[:, c, :], start=True, stop=True), reads=[bnx[p], BRTf, BRp[p]], writes=[bpr])
                                op("dve", lambda e: e.tensor_tensor(out=RTf[:, 2 * p:2 * p + 2].rearrange("p a b -> p (a b)"), in0=pr[:, 0:256], in1=RTf[:, 2 * p:2 * p + 2].rearrange("p a b -> p (a b)"), op=ALU.add), reads=[bpr, BRTf, BRp[p]], writes=[BRp[p]])
                            cur = nxt
                            bcur = [[bnx[p]] for p in range(NP)]
                        op("act", lambda e: e.activation(out=RT[:], in_=RTf[:], func=AF.Copy), reads=[BRTf] + BRp, writes=[BRT])
                        if h == 0 and chains[0] == (4, 0):
                            g.dump("d_Az", Az[:, 0], BAz, [128, 2, 128])
                            g.dump("d_RT", RT[:, 0], BRT, [128, 128], BF16)
                            g.dump("d_attnT", attnT[:, 0], Bat, [128, 128], BF16)
                            g.dump("d_kbg", kbg[:, 0], Bkbg, [128, 128], BF16)
                            g.dump("d_qgT", qgT[:, 0], Bqg, [128, 128], BF16)
                            g.dump("d_kdec", kdec[:, 0], Bkd, [128, 128], BF16)
                        for c in range(NCH):
                            pw, bpw = g.psf()
                            op("pe", lambda e: e.matmul(pw[:, 0:128], lhsT=kbg[:, c, :], rhs=RT[:, c, :], start=True, stop=True), reads=[Bkbg, BRT], writes=[bpw])
                            op("act", lambda e: e.activation(out=nwTa[:, c, :], in_=pw[:, 0:128], func=AF.Identity, scale=-1.0), reads=[bpw], writes=[BnwT])
                        for si, (sqi, d_, order, s_init, s_final) in enumerate(scans):
                            S, BS = Ss[si]
                            Sb, BSb = Sbs[si]
                            if s_init:
                                if SEQS[sqi][2]:
                                    k.dma("sp", S[:], dr["state_dn"][j, d_, h], writes=[BS], stream="st")
                                else:
                                    op("pool", lambda e: e.memset(S[:], 0.0), writes=[BS])
                                op("act", lambda e: e.activation(out=Sb[:], in_=S[:], func=AF.Copy), reads=[BS], writes=[BSb])
                        for st_ in range(max(len(sc[2]) for sc in scans)):
                            for si, (sqi, d_, order, s_init, s_final) in enumerate(scans):
                                if st_ >= len(order):
                                    continue
                                S, BS = Ss[si]
                                Sb, BSb = Sbs[si]
                                vnew, Bvn = vnews[si]
                                c = order[st_]
                                tt = chains[c][0]
                                tsl = slice(tt * 128, (tt + 1) * 128)
                                pv_, bpv = g.psf()
                                op("pe", lambda e: e.matmul(pv_[:, 0:128], lhsT=RT[:, c, :], rhs=vb[:, c, :], start=True, stop=False), reads=[BRT, Bvb], writes=[bpv])
                                op("pe", lambda e: e.matmul(pv_[:, 0:128], lhsT=nwTa[:, c, :], rhs=Sb[:], start=False, stop=True), reads=[BnwT, BSb], writes=[bpv])
                                op("dve", lambda e: e.tensor_copy(out=vnew[:], in_=pv_[:, 0:128]), reads=[bpv], writes=[Bvn])
                                pS, bpS = g.psf()
                                op("pe", lambda e: e.matmul(pS[:, 0:128], lhsT=kdec[:, c, :], rhs=vnew[:], start=True, stop=True), reads=[Bkd, Bvn], writes=[bpS])
                                po_, bpo = g.psf()
                                op("pe", lambda e: e.matmul(po_[:, 0:128], lhsT=Sb[:], rhs=qgT[:, c, :], start=True, stop=False), reads=[BSb, Bqg], writes=[bpo])
                                op("pe", lambda e: e.matmul(po_[:, 0:128], lhsT=vnew[:], rhs=attnT[:, c, :], start=False, stop=True), reads=[Bvn, Bat], writes=[bpo])
                                op("dve", lambda e: e.scalar_tensor_tensor(out=Sb[:], in0=S[:], scalar=glb[:, c:c + 1], in1=pS[:, 0:128], op0=ALU.mult, op1=ALU.add), reads=[BS, Bglb, bpS], writes=[BSb])
                                op("dve", lambda e: e.scalar_tensor_tensor(out=S[:], in0=S[:], scalar=glb[:, c:c + 1], in1=pS[:, 0:128], op0=ALU.mult, op1=ALU.add), reads=[BS, Bglb, bpS], writes=[BS])
                                if d_ == 0:
                                    op("act", lambda e: e.activation(out=oacc[:, tsl], in_=po_[:, 0:128], func=AF.Copy), reads=[bpo], writes=[Bo])
                                else:
                                    op("dve", lambda e: e.tensor_tensor(out=oacc[:, tsl], in0=po_[:, 0:128], in1=oacc[:, tsl], op=ALU.add), reads=[bpo, Bo], writes=[Bo])
                        for si, (sqi, d_, order, s_init, s_final) in enumerate(scans):
                            if s_final and not SEQS[sqi][2]:
                                k.dma("sp", dr["st_out"][sqi, j, d_, h], Ss[si][0][:], reads=[Ss[si][1]], stream="sto")
                        k.barrier()
                if h == 0:
                    g.dump("d_oacc", oacc[:], Bo, [128, T])
                for tg in range(3):
                    ts = slice(tg * 512, (tg + 1) * 512)
                    sq, bsq = sqb[tg % 2]
                    op("act", lambda e: e.activation(out=sq[:], in_=oacc[:, ts], func=AF.Square), reads=[Bo], writes=[bsq])
                    p2, bp2 = g.psf()
                    op("pe", lambda e: e.matmul(p2[:], lhsT=ones_b[:], rhs=sq[:], start=True, stop=True), reads=[bsq, CB], writes=[bp2])
                    rs_, brs_ = tmpf[tg % 3]
                    ss_to_rstd(p2[:], rs_[:], bp2, brs_, 128, EPS)
                    op("dve", lambda e: e.scalar_tensor_tensor(out=oacc[:, ts], in0=oacc[:, ts], scalar=dnn[:, 0:1], in1=rs_[:], op0=ALU.mult, op1=ALU.mult), reads=[Bo, Bdnn, brs_], writes=[Bo])
                    op("pool", lambda e: e.tensor_tensor(out=mixT[:, h, ts], in0=oacc[:, ts], in1=zs[:, ts], op=ALU.mult), reads=[Bo, Bzs], writes=[Bmix])
                k.barrier()
        HYENA(g, L, i)
        k.barrier()


def HYENA(g, L, i):
    k, dr, op = L["k"], L["dr"], L["op"]
    hT, Bh, mixT, Bmix, CB = L["hT"], L["Bh"], L["mixT"], L["Bmix"], L["CB"]
    ident_b = L["ident_b"]
    j = i // 2
    W = dr["ev_w_in"][j]
    CW = 256
    I32 = mybir.dt.int32
    with contextlib.ExitStack() as es:
        def sb(shape, dt, nm, es_=None):
            return g.sb(shape, dt, nm, es_ or es)
        cw, Bcw = sb([128, 3, 12], F32, "hycw")
        cbv, Bcbv = sb([128, 12], F32, "hycb")
        w1, Bw1 = sb([33, 64], F32, "hyw1")
        w2, Bw2 = sb([64, 64], F32, "hyw2")
        fr, Bfr = sb([64, 4], F32, "hyfr")
        k.dma("sp", cw[:], dr["hy_conv_w_fm"][j], writes=[Bcw], stream="hc")
        k.dma("sp", cbv[:], dr["hy_conv_b_fm"][j], writes=[Bcbv], stream="hc")
        k.dma("sp", w1[:], dr["hy_w1"][j], writes=[Bw1], stream="hc")
        k.dma("sp", w2[:], dr["hy_w2"][j], writes=[Bw2], stream="hc")
        k.dma("sp", fr[:, 0:1], dr["hy_freq1_fm"][j], writes=[Bfr], stream="hc")
        k.dma("sp", fr[:, 1:2], dr["hy_b1_fm"][j], writes=[Bfr], stream="hc")
        k.dma("sp", fr[:, 2:3], dr["hy_freq2_fm"][j], writes=[Bfr], stream="hc")
        k.dma("sp", fr[:, 3:4], dr["hy_b2_fm"][j], writes=[Bfr], stream="hc")
        op("dve", lambda e: e.tensor_tensor(out=fr[:, 1:2], in0=fr[:, 1:2], in1=fr[:, 0:1], op=ALU.mult), reads=[Bfr], writes=[Bfr])
        op("dve", lambda e: e.tensor_tensor(out=fr[:, 3:4], in0=fr[:, 3:4], in1=fr[:, 2:3], op=ALU.mult), reads=[Bfr], writes=[Bfr])

        for ch in range(512 // CW):
            with contextlib.ExitStack() as ec:
                x2T, Bx2 = sb([128, 2, T], BF16, "x2T", ec)
                ztok, Bzt = sb([128, 12, CW], BF16, "ztok", ec)
                x1tok, Bx1t = sb([128, 12, CW], BF16, "x1tok", ec)
                zm2, Bzm2 = sb([64, 1024], F32, "zm2", ec)
                with contextlib.ExitStack() as e1:
                    raw, Braw = sb([128, T], F32, "hraw", e1)
                    cv, Bcv = sb([128, T], F32, "hcv", e1)
                    cvb, Bcvb = sb([128, T], BF16, "hcvb", e1)
                    for comp in range(3):
                        wv, bw = g.wload(W, 2064 + comp * 512 + ch * CW, CW, 8)
                        for c2 in range(CW // 128):
                            cc = comp * 4 + ch * (CW // 128) + c2
                            for tg in range(3):
                                ts = slice(tg * 512, (tg + 1) * 512)
                                pp, bpp = g.psf()
                                for kc in range(8):
                                    op("pe", lambda e: e.matmul(pp[:], lhsT=wv[:, kc, c2 * 128:(c2 + 1) * 128], rhs=hT[:, kc, ts], start=(kc == 0), stop=(kc == 7)), reads=[bw, Bh], writes=[bpp])
                                op("act", lambda e: e.activation(out=raw[:, ts], in_=pp[:], func=AF.Copy), reads=[bpp], writes=[Braw])
                            for (s0, ln, _) in SEQS:
                                op("dve", lambda e: e.tensor_scalar(out=cv[:, s0:s0 + ln], in0=raw[:, s0:s0 + ln], scalar1=cw[:, 1, cc:cc + 1], scalar2=cbv[:, cc:cc + 1], op0=ALU.mult, op1=ALU.add), reads=[Braw, Bcw, Bcbv], writes=[Bcv])
                                op("dve", lambda e: e.scalar_tensor_tensor(out=cv[:, s0 + 1:s0 + ln], in0=raw[:, s0:s0 + ln - 1], scalar=cw[:, 0, cc:cc + 1], in1=cv[:, s0 + 1:s0 + ln], op0=ALU.mult, op1=ALU.add), reads=[Braw, Bcw, Bcv], writes=[Bcv])
                                op("dve", lambda e: e.scalar_tensor_tensor(out=cv[:, s0:s0 + ln - 1], in0=raw[:, s0 + 1:s0 + ln], scalar=cw[:, 2, cc:cc + 1], in1=cv[:, s0:s0 + ln - 1], op0=ALU.mult, op1=ALU.add), reads=[Braw, Bcw, Bcv], writes=[Bcv])
                            if comp == 1:
                                op("act", lambda e: e.activation(out=x2T[:, c2, :], in_=cv[:], func=AF.Copy), reads=[Bcv], writes=[Bx2])
                            else:
                                dst, bdst = (x1tok, Bx1t) if comp == 0 else (ztok, Bzt)
                                op("act", lambda e: e.activation(out=cvb[:], in_=cv[:], func=AF.Copy), reads=[Bcv], writes=[Bcvb])
                                for t4 in range(3):
                                    ptb, bptb = g.psb()
                                    for q4 in range(4):
                                        tt = t4 * 4 + q4
                                        op("pe", lambda e: e.transpose(out=ptb[:, q4 * 128:(q4 + 1) * 128], in_=cvb[:, tt * 128:(tt + 1) * 128], identity=ident_b[:]), reads=[Bcvb, CB], writes=[bptb])
                                    op("dve", lambda e: e.tensor_copy(out=dst[:, t4 * 4:t4 * 4 + 4, c2 * 128:(c2 + 1) * 128], in_=ptb[:, 0:512].rearrange("p (a b) -> p a b", a=4)), reads=[bptb], writes=[bdst])
                    k.barrier()

                for (Ls, seqs) in ((256, [SEQS[0], SEQS[1]]), (1024, [SEQS[2]])):
                    nt = Ls // 128
                    Fd, Fi = dr[f"dft_f{Ls}"], dr[f"dft_i{Ls}"]
                    with contextlib.ExitStack() as eL:
                        with contextlib.ExitStack() as em:
                            ft, Bft = sb([33, Ls], F32, "feats", em)
                            k.dma("sp", ft[:], dr[f"featsT{Ls}"], writes=[Bft], stream="hc")
                            zmlp, Bzm = sb([64, Ls], F32, "zmlp", em)
                            ti, Bti = sb([64, Ls], I32, "ti", em)
                            tf, Btf = sb([64, Ls], F32, "tf", em)

                            def sin_layer(dst, bdst, wmat, bwm, src, bsrc, kdim, fcol):
                                for c0 in range(0, Ls, 512):
                                    n = min(512, Ls - c0)
                                    pp, bpp = g.psf()
                                    op("pe", lambda e: e.matmul(pp[0:64, 0:n], lhsT=wmat[0:kdim, :], rhs=src[0:kdim, c0:c0 + n], start=True, stop=True), reads=[bwm, bsrc], writes=[bpp])
                                    op("dve", lambda e: e.tensor_scalar(out=dst[:, c0:c0 + n], in0=pp[0:64, 0:n], scalar1=fr[:, fcol:fcol + 1], scalar2=fr[:, fcol + 1:fcol + 2], op0=ALU.mult, op1=ALU.add), reads=[bpp, Bfr], writes=[bdst])
                                d_ = dst[:, 0:Ls]
                                op("dve", lambda e: e.tensor_scalar(out=d_, in0=d_, scalar1=1.0 / (2 * math.pi), scalar2=None, op0=ALU.mult), reads=[bdst], writes=[bdst])
                                op("dve", lambda e: e.tensor_copy(out=ti[:], in_=d_), reads=[bdst], writes=[Bti])
                                op("dve", lambda e: e.tensor_copy(out=tf[:], in_=ti[:]), reads=[Bti], writes=[Btf])
                                op("dve", lambda e: e.tensor_tensor(out=d_, in0=d_, in1=tf[:], op=ALU.subtract), reads=[bdst, Btf], writes=[bdst])
                                op("dve", lambda e: e.tensor_single_scalar(out=tf[:], in_=d_, scalar=0.5, op=ALU.is_gt), reads=[bdst], writes=[Btf])
                                op("dve", lambda e: e.tensor_tensor(out=d_, in0=d_, in1=tf[:], op=ALU.subtract), reads=[bdst, Btf], writes=[bdst])
                                op("dve", lambda e: e.tensor_single_scalar(out=tf[:], in_=d_, scalar=-0.5, op=ALU.is_lt), reads=[bdst], writes=[Btf])
                                op("dve", lambda e: e.tensor_tensor(out=d_, in0=d_, in1=tf[:], op=ALU.add), reads=[bdst, Btf], writes=[bdst])
                                op("act", lambda e: e.activation(out=d_, in_=d_, func=AF.Sin, scale=2 * math.pi), reads=[bdst], writes=[bdst])
                            sin_layer(zmlp, Bzm, w1, Bw1, ft, Bft, 33, 0)
                            sin_layer(zm2, Bzm2, w2, Bw2, zmlp, Bzm, 64, 2)
                            k.barrier()
                        z2tok, Bz2 = sb([128, len(seqs), nt, CW], BF16, "z2tok", eL)
                        Hc, BHc = sb([128, nt, CW], F32, "Hc", eL)
                        Hs, BHs = sb([128, nt, CW], F32, "Hs", eL)
                        brow, Bbrow = sb([128, CW], F32, "brow", eL)
                        w3s, Bw3 = sb([64, 2, CW], F32, "w3s", eL)
                        tq = [sb([128, CW], F32, "tq", eL) for _ in range(4)]

                        def fslabs(grp):
                            if Ls == 256:
                                v_, b_ = g.wload(Fd, 0, 512, nt)
                                return (v_, b_, 0), (v_, b_, 256)
                            vc, bc = g.wload(Fd, grp * 512, 512, nt)
                            vs, bs = g.wload(Fd, Ls + grp * 512, 512, nt)
                            return (vc, bc, 0), (vs, bs, 0)

                        for o in range(2):
                            k.dma("sp", brow[:], dr["hy_bias_bc"][:, j, o, ch * CW:(ch + 1) * CW], writes=[Bbrow], stream="hc")
                            for sd in range(2):
                                c0 = o * 1024 + sd * 512 + ch * CW
                                k.dma("sp", w3s[:, sd, :], dr["hy_w3"][j][:, c0:c0 + CW], writes=[Bw3], stream="hc")
                            with contextlib.ExitStack() as eS:
                                Sd, BSd = sb([128, nt, CW], BF16, "Sd", eS)
                                Dd, BDd = sb([128, nt, CW], BF16, "Dd", eS)
                                for tt in range(nt):
                                    pp, bpp = g.psf()
                                    for sd in range(2):
                                        op("pe", lambda e: e.matmul(pp[:, sd * CW:(sd + 1) * CW], lhsT=zm2[:, tt * 128:(tt + 1) * 128], rhs=w3s[:, sd, :], start=True, stop=True), reads=[Bzm2, Bw3], writes=[bpp])
                                    dec, bdec = tq[tt % 2]
                                    k.dma("sp", dec[:], dr[f"hydec{Ls}"][tt * 128:(tt + 1) * 128, ch * CW:(ch + 1) * CW], writes=[bdec], stream="hd%d" % (tt % 2))
                                    tfw, btfw = tq[2]
                                    tbw, btbw = tq[3]
                                    op("dve", lambda e: e.tensor_tensor(out=tfw[:], in0=pp[:, 0:CW], in1=dec[:], op=ALU.mult), reads=[bpp, bdec], writes=[btfw])
                                    op("dve", lambda e: e.tensor_tensor(out=tbw[:], in0=pp[:, CW:2 * CW], in1=dec[:], op=ALU.mult), reads=[bpp, bdec], writes=[btbw])
                                    if tt == 0:
                                        op("dve", lambda e: e.memset(tbw[0:1, :], 0.0), reads=[btbw], writes=[btbw])
                                    op("pool", lambda e: e.tensor_tensor(out=Sd[:, tt, :], in0=tfw[:], in1=tbw[:], op=ALU.add), reads=[btfw, btbw], writes=[BSd])
                                    op("pool", lambda e: e.tensor_tensor(out=Dd[:, tt, :], in0=tfw[:], in1=tbw[:], op=ALU.subtract), reads=[btfw, btbw], writes=[BDd])
                                for grp in range(max(1, nt // 4)):
                                    (vc, bc, oc), (vs, bs, os_) = fslabs(grp)
                                    for f4 in range(min(4, nt)):
                                        fc = grp * 4 + f4
                                        pc, bpc = g.psf()
                                        for tc in range(nt):
                                            op("pe", lambda e: e.matmul(pc[:, 0:CW], lhsT=vc[:, tc, oc + f4 * 128:oc + (f4 + 1) * 128], rhs=Sd[:, tc, :], start=(tc == 0), stop=(tc == nt - 1)), reads=[bc, BSd], writes=[bpc])
                                        op("dve", lambda e: e.tensor_tensor(out=Hc[:, fc, :], in0=pc[:, 0:CW], in1=brow[:], op=ALU.add), reads=[bpc, Bbrow], writes=[BHc])
                                        ps_, bps = g.psf()
                                        for tc in range(nt):
                                            op("pe", lambda e: e.matmul(ps_[:, 0:CW], lhsT=vs[:, tc, os_ + f4 * 128:os_ + (f4 + 1) * 128], rhs=Dd[:, tc, :], start=(tc == 0), stop=(tc == nt - 1)), reads=[bs, BDd], writes=[bps])
                                        op("act", lambda e: e.activation(out=Hs[:, fc, :], in_=ps_[:, 0:CW], func=AF.Copy), reads=[bps], writes=[BHs])
                                        if fc == 0:
                                            pn, bpn = g.psf()
                                            for tc in range(nt):
                                                op("pe", lambda e: e.matmul(pn[0:1, 0:CW], lhsT=vs[:, tc, os_:os_ + 1], rhs=Sd[:, tc, :], start=(tc == 0), stop=(tc == nt - 1)), reads=[bs, BSd], writes=[bpn])
                                            op("dve", lambda e: e.tensor_tensor(out=Hs[0:1, 0, :], in0=pn[0:1, 0:CW], in1=brow[0:1, :], op=ALU.add), reads=[bpn, Bbrow, BHs], writes=[BHs])
                                k.barrier()
                            eY = contextlib.ExitStack()
                            Y, BY = sb([128, 2 * nt, CW], BF16, "Y", eY)
                            for si, (s0, ln, smp) in enumerate(seqs):
                                t0 = s0 // 128
                                for grp in range(max(1, nt // 4)):
                                    (vc, bc, oc), (vs, bs, os_) = fslabs(grp)
                                    for f4 in range(min(4, nt)):
                                        fc = grp * 4 + f4
                                        pc, bpc = g.psf()
                                        ps_, bps = g.psf()
                                        for tc in range(nt):
                                            rhs = ztok[:, t0 + tc, :] if o == 0 else z2tok[:, si, tc, :]
                                            brhs = Bzt if o == 0 else Bz2
                                            op("pe", lambda e: e.matmul(pc[:, 0:CW], lhsT=vc[:, tc, oc + f4 * 128:oc + (f4 + 1) * 128], rhs=rhs, start=(tc == 0), stop=(tc == nt - 1)), reads=[bc, brhs], writes=[bpc])
                                        for tc in range(nt):
                                            rhs = ztok[:, t0 + tc, :] if o == 0 else z2tok[:, si, tc, :]
                                            brhs = Bzt if o == 0 else Bz2
                                            op("pe", lambda e: e.matmul(ps_[:, 0:CW], lhsT=vs[:, tc, os_ + f4 * 128:os_ + (f4 + 1) * 128], rhs=rhs, start=(tc == 0), stop=(tc == nt - 1)), reads=[bs, brhs], writes=[bps])
                                        (a1, ba1), (a2, ba2), (a3, ba3), (a4, ba4) = tq
                                        op("dve", lambda e: e.tensor_tensor(out=a1[:], in0=pc[:, 0:CW], in1=Hc[:, fc, :], op=ALU.mult), reads=[bpc, BHc], writes=[ba1])
                                        op("dve", lambda e: e.tensor_tensor(out=a3[:], in0=pc[:, 0:CW], in1=Hs[:, fc, :], op=ALU.mult), reads=[bpc, BHs], writes=[ba3])
                                        op("dve", lambda e: e.tensor_tensor(out=a2[:], in0=ps_[:, 0:CW], in1=Hs[:, fc, :], op=ALU.mult), reads=[bps, BHs], writes=[ba2])
                                        op("dve", lambda e: e.tensor_tensor(out=a4[:], in0=ps_[:, 0:CW], in1=Hc[:, fc, :], op=ALU.mult), reads=[bps, BHc], writes=[ba4])
                                        op("pool", lambda e: e.tensor_tensor(out=Y[:, fc, :], in0=a1[:], in1=a2[:], op=ALU.subtract), reads=[ba1, ba2], writes=[BY])
                                        op("pool", lambda e: e.tensor_tensor(out=Y[:, nt + fc, :], in0=a3[:], in1=a4[:], op=ALU.add), reads=[ba3, ba4], writes=[BY])
                                        if fc == 0:
                                            op("pool", lambda e: e.tensor_copy(out=Y[0:1, 0, :], in_=a1[0:1, :]), reads=[ba1, BY], writes=[BY])
                                            op("pool", lambda e: e.tensor_copy(out=Y[0:1, nt, :], in_=a2[0:1, :]), reads=[ba2, BY], writes=[BY])
                                ncol = 256 if Ls == 1024 else 256
                                for c0 in range(0, Ls, ncol):
                                    vi, bi = g.wload(Fi, c0, ncol, 2 * nt)
                                    if o == 0:
                                        for t2 in range(ncol // 128):
                                            tl = c0 // 128 + t2
                                            pp, bpp = g.psf()
                                            for fc in range(2 * nt):
                                                op("pe", lambda e: e.matmul(pp[:, 0:CW], lhsT=vi[:, fc, t2 * 128:(t2 + 1) * 128], rhs=Y[:, fc, :], start=(fc == 0), stop=(fc == 2 * nt - 1)), reads=[bi, BY], writes=[bpp])
                                            op("dve", lambda e: e.tensor_tensor(out=z2tok[:, si, tl, :], in0=pp[:, 0:CW], in1=x1tok[:, t0 + tl, :], op=ALU.mult), reads=[bpp, Bx1t], writes=[Bz2])
                                    else:
                                        for c2 in range(CW // 128):
                                            pp, bpp = g.psf()
                                            for fc in range(2 * nt):
                                                op("pe", lambda e: e.matmul(pp[:, 0:ncol], lhsT=Y[:, fc, c2 * 128:(c2 + 1) * 128], rhs=vi[:, fc, :], start=(fc == 0), stop=(fc == 2 * nt - 1)), reads=[bi, BY], writes=[bpp])
                                            op("dve", lambda e: e.tensor_tensor(out=mixT[:, 4 + ch * (CW // 128) + c2, s0 + c0:s0 + c0 + ncol], in0=pp[:, 0:ncol], in1=x2T[:, c2, s0 + c0:s0 + c0 + ncol], op=ALU.mult), reads=[bpp, Bx2], writes=[Bmix])
                            k.barrier()
                            eY.close()
                        k.barrier()
                k.barrier()
        k.barrier()


def ODD_MIXER(g, L, i):
    k, dr, op = L["k"], L["dr"], L["op"]
    hT, Bh, mixT, Bmix, CB = L["hT"], L["Bh"], L["mixT"], L["Bmix"], L["CB"]
    ones_b, ident_b = L["ones_b"], L["ident_b"]
    ss_to_rstd = L["ss_to_rstd"]
    j = i // 2
    W = dr["od_w_in"][j]
    scale = 128.0 ** -0.5
    with contextlib.ExitStack() as es:
        def sb(shape, dt, nm):
            return g.sb(shape, dt, nm, es)
        qT, Bq = sb([128, 4, T], BF16, "qT")
        kT, Bk = sb([128, 2, T], BF16, "kT")
        vtok, Bv = sb([128, 12, 2, 128], BF16, "vtok")
        ckT, Bck = sb([128, 2, 256], BF16, "ckT")
        cktok, Bckt = sb([128, 2, 2, 128], BF16, "cktok")
        cvtok, Bcv = sb([128, 2, 2, 128], BF16, "cvtok")
        ropec, Brc = sb([128, 1024], F32, "ropec")
        ropes, Brs = sb([128, 1024], F32, "ropes")
        k.dma("sp", ropec[:], dr["rope_c"], writes=[Brc])
        k.dma("sp", ropes[:], dr["rope_s"], writes=[Brs])
        rrm, Brm = sb([128, 128], BF16, "rrm")
        band, Bband = sb([128, 6, 512], BF16, "band")
        gq, Bgq = sb([128, 2], F32, "gq")
        grow, Bgrow = sb([128, 128], F32, "grow")
        sinkc, Bsink = sb([128, 4], F32, "sinkc")
        qf = [L["tmpf"][0], L["tmpf"][1]]
        qb = [sb([128, 512], BF16, "qb") for _ in range(2)]
        sqs = L["sqb"]
        t1 = [sb([128, 512], F32, "t1") for _ in range(2)]
        rs_, Brs_ = L["tmpf"][2]
        ebuf = [sb([128, 512], BF16, "ebuf") for _ in range(3)]
        rinv, Brinv = sb([128, 512], F32, "rinv")
        nq, Bnq = sb([128, 4, 3], F32, "nq")
        nk, Bnk = sb([128, 2, 4], F32, "nk")
        negM, BnegM = sb([128, 2, 4], F32, "negM")
        esink, Bes = sb([128, 2, 4], F32, "esink")
        kvf, Bkvf = sb([128, 512], F32, "kvf")
        kvo, Bkvo = sb([128, 2, 128], F32, "kvo")
        ssk, Bssk = sb([128, 2], F32, "ssk")
        sqk, Bsqk = sb([128, 2, 128], F32, "sqk")
        cnt = {"q": 0, "e": 0}

        k.dma("sp", rrm[:], dr["rope_rm"], writes=[Brm], stream="oc")
        k.dma("sp", band[:], dr["band"].rearrange("r s q -> s r q"), writes=[Bband], stream="oc")
        k.dma("sp", gq[:, 0:1], dr["c_q_norm_fm"][j], writes=[Bgq], stream="oc")
        k.dma("sp", gq[:, 1:2], dr["c_k_norm_fm"][j], writes=[Bgq], stream="oc")
        k.dma("sp", grow[:], dr["c_k_norm_row"][:, j, :], writes=[Bgrow], stream="oc")
        k.dma("sp", sinkc[:], dr["d_sink_bc"][:, j, :], writes=[Bsink], stream="oc")
        op("dve", lambda e: e.tensor_scalar(out=gq[:], in0=gq[:], scalar1=math.sqrt(128.0), scalar2=None, op0=ALU.mult), reads=[Bgq], writes=[Bgq])

        for grp in range(2):
            qc0 = grp * 1024
            kc0 = grp * 1024 + 512
            wq, bwq = g.wload(W, qc0, 512, 8)
            wk, bwk = g.wload(W, kc0, 256, 8)
            def stage0(ci, tg):
                isq = ci < 4
                wv, bw, cc = (wq, bwq, ci) if isq else (wk, bwk, ci - 4)
                dst, bdst, dc = (qT, Bq, ci) if isq else (kT, Bk, ci - 4)
                ts = slice(tg * 512, (tg + 1) * 512)
                pp, bpp = g.psf()
                for kc in range(8):
                    op("pe", lambda e: e.matmul(pp[:], lhsT=wv[:, kc, cc * 128:(cc + 1) * 128], rhs=hT[:, kc, ts], start=(kc == 0), stop=(kc == 7)), reads=[bw, Bh], writes=[bpp])
                cnt["q"] += 1
                par = cnt["q"]
                f_, bf_ = qf[par % 2]
                if grp == 1 and tg == 0:
                    op("act", lambda e: e.activation(out=dst[:, dc, ts], in_=pp[:], func=AF.Copy), reads=[bpp], writes=[bdst])
                else:
                    op("act", lambda e: e.activation(out=f_[:], in_=pp[:], func=AF.Copy), reads=[bpp], writes=[bf_])
                return (isq, dst, bdst, dc, ts, par, f_, bf_)

            def rest(ci, tg, ctx):
                isq, dst, bdst, dc, ts, par, f_, bf_ = ctx
                if grp == 0:
                    sq, bsq = sqs[par % 2]
                    op("act", lambda e: e.activation(out=sq[:], in_=f_[:], func=AF.Square), reads=[bf_], writes=[bsq])
                    p2, bp2 = g.psf()
                    op("pe", lambda e: e.matmul(p2[:], lhsT=ones_b[:], rhs=sq[:], start=True, stop=True), reads=[bsq, CB], writes=[bp2])
                    ss_to_rstd(p2[:], rs_[:], bp2, Brs_, 128, EPS)
                    gcol = gq[:, 0:1] if isq else gq[:, 1:2]
                    if tg == 0:
                        op("dve", lambda e: e.scalar_tensor_tensor(out=dst[:, dc, ts], in0=f_[:], scalar=gcol, in1=rs_[:], op0=ALU.mult, op1=ALU.mult), reads=[bf_, Bgq, Brs_], writes=[bdst])
                    else:
                        op("dve", lambda e: e.scalar_tensor_tensor(out=f_[:], in0=f_[:], scalar=gcol, in1=rs_[:], op0=ALU.mult, op1=ALU.mult), reads=[bf_, Bgq, Brs_], writes=[bf_])
                if tg > 0:
                    ps_ = slice((tg - 1) * 512, tg * 512)
                    b_, bb_ = qb[par % 2]
                    op("act", lambda e: e.activation(out=b_[:], in_=f_[:], func=AF.Copy), reads=[bf_], writes=[bb_])
                    p3, bp3 = g.psf()
                    op("pe", lambda e: e.matmul(p3[:], lhsT=rrm[:], rhs=b_[:], start=True, stop=True), reads=[Brm, bb_], writes=[bp3])
                    t_, bt_ = t1[par % 2]
                    op("pool", lambda e: e.tensor_tensor(out=t_[:], in0=f_[:], in1=ropec[:, ps_], op=ALU.mult), reads=[bf_, Brc], writes=[bt_])
                    op("dve", lambda e: e.tensor_tensor(out=f_[:], in0=p3[:], in1=ropes[:, ps_], op=ALU.mult), reads=[bp3, Brs, bf_], writes=[bf_])
                    op("pool", lambda e: e.tensor_tensor(out=dst[:, dc, ts], in0=f_[:], in1=t_[:], op=ALU.add), reads=[bf_, bt_], writes=[bdst])
                sq, bsq = sqs[(par + 1) % 2]
                op("act", lambda e: e.activation(out=sq[:], in_=dst[:, dc, ts], func=AF.Square), reads=[bdst], writes=[bsq])
                p4, bp4 = g.psf()
                op("pe", lambda e: e.matmul(p4[:], lhsT=ones_b[:], rhs=sq[:], start=True, stop=True), reads=[bsq, CB], writes=[bp4])
                if isq:
                    op("dve", lambda e: e.tensor_reduce(out=nq[:, dc, tg:tg + 1], in_=p4[:], axis=AX.X, op=ALU.max), reads=[bp4], writes=[Bnq])
                else:
                    op("dve", lambda e: e.tensor_reduce(out=nk[:, dc, tg:tg + 1], in_=p4[:], axis=AX.X, op=ALU.max), reads=[bp4], writes=[Bnk])

            items = [(ci, tg) for ci in range(6) for tg in range(3)]
            ctx = stage0(*items[0])
            for n_, it in enumerate(items):
                nctx = stage0(*items[n_ + 1]) if n_ + 1 < len(items) else None
                rest(it[0], it[1], ctx)
                ctx = nctx
            if DBG.get("odd_stop") == "A":
                continue
            wkv, bwkv = g.wload(W, kc0, 512, 8)
            kname, vname = ("kc_out", "vc_out") if grp == 0 else ("kd_out", "vd_out")
            for tt in range(12):
                pp, bpp = g.psf()
                for kc in range(8):
                    op("pe", lambda e: e.matmul(pp[:], lhsT=hT[:, kc, tt * 128:(tt + 1) * 128], rhs=wkv[:, kc, :], start=(kc == 0), stop=(kc == 7)), reads=[bwkv, Bh], writes=[bpp])
                op("act", lambda e: e.activation(out=vtok[:, tt], in_=pp[:, 256:512].rearrange("p (a b) -> p a b", a=2), func=AF.Copy), reads=[bpp], writes=[Bv])
                if tt < 4 and DBG.get("odd_stop") != "B1":
                    sq_, tb = tt // 2, tt % 2
                    op("dve", lambda e: e.tensor_copy(out=kvf[:], in_=pp[:]), reads=[bpp], writes=[Bkvf])
                    if DBG.get("odd_stop") == "B2":
                        continue
                    k.dma("sp", dr[vname][sq_, j, tb * 128:(tb + 1) * 128], kvf[:, 256:512].rearrange("p (a b) -> p a b", a=2), reads=[Bkvf], stream="kvo")
                    if grp == 1:
                        k.dma("sp", dr[kname][sq_, j, tb * 128:(tb + 1) * 128], kvf[:, 0:256].rearrange("p (a b) -> p a b", a=2), reads=[Bkvf], stream="kvo")
                    else:
                        kv3 = kvf[:, 0:256].rearrange("p (a b) -> p a b", a=2)
                        op("pool", lambda e: e.tensor_tensor(out=sqk[:], in0=kv3, in1=kv3, op=ALU.mult), reads=[Bkvf], writes=[Bsqk])
                        op("dve", lambda e: e.tensor_reduce(out=ssk[:], in_=sqk[:], axis=AX.X, op=ALU.add), reads=[Bsqk], writes=[Bssk])
                        ss_to_rstd(ssk[:], ssk[:], Bssk, Bssk, 1, EPS, scale=1.0 / 128.0)
                        for kv in range(2):
                            op("dve", lambda e: e.scalar_tensor_tensor(out=kvo[:, kv, :], in0=kvf[:, kv * 128:(kv + 1) * 128], scalar=ssk[:, kv:kv + 1], in1=grow[:], op0=ALU.mult, op1=ALU.mult), reads=[Bkvf, Bssk, Bgrow], writes=[Bkvo])
                        k.dma("sp", dr[kname][sq_, j, tb * 128:(tb + 1) * 128], kvo[:], reads=[Bkvo], stream="kvo")
            if DBG.get("odd_stop") in ("B", "B1", "B2"):
                continue
            ckn, cvn = ("cache_k_c", "cache_v_c") if grp == 0 else ("cache_k_d", "cache_v_d")
            k.dma("pool", cktok[:], dr[ckn][j].rearrange("(sb p) k d -> p sb k d", p=128), writes=[Bckt], stream="cch")
            k.dma("pool", cvtok[:], dr[cvn][j].rearrange("(sb p) k d -> p sb k d", p=128), writes=[Bcv], stream="cch")
            ptb, bptb = g.psb()
            for kv in range(2):
                for sbk in range(2):
                    o_ = (kv * 2 + sbk) * 128
                    op("pe", lambda e: e.transpose(out=ptb[:, o_:o_ + 128], in_=cktok[:, sbk, kv, :], identity=ident_b[:]), reads=[Bckt, CB], writes=[bptb])
            op("dve", lambda e: e.tensor_copy(out=ckT[:].rearrange("p a b -> p (a b)"), in_=ptb[:, 0:512]), reads=[bptb], writes=[Bck])
            for kv in range(2):
                sq, bsq = sqs[kv]
                op("act", lambda e: e.activation(out=sq[:, 0:256], in_=ckT[:, kv, :], func=AF.Square), reads=[Bck], writes=[bsq])
                p4, bp4 = g.psf()
                op("pe", lambda e: e.matmul(p4[:, 0:256], lhsT=ones_b[:], rhs=sq[:, 0:256], start=True, stop=True), reads=[bsq, CB], writes=[bp4])
                op("dve", lambda e: e.tensor_reduce(out=nk[:, kv, 3:4], in_=p4[:, 0:256], axis=AX.X, op=ALU.max), reads=[bp4], writes=[Bnk])
            if DBG.get("odd_stop") == "C":
                continue
            op("dve", lambda e: e.tensor_tensor(out=nq[:, :, 1], in0=nq[:, :, 1], in1=nq[:, :, 2], op=ALU.max), reads=[Bnq], writes=[Bnq])
            op("dve", lambda e: e.tensor_tensor(out=nk[:, :, 1], in0=nk[:, :, 1], in1=nk[:, :, 2], op=ALU.max), reads=[Bnk], writes=[Bnk])
            op("dve", lambda e: e.tensor_tensor(out=nk[:, :, 1], in0=nk[:, :, 1], in1=nk[:, :, 3], op=ALU.max), reads=[Bnk], writes=[Bnk])
            for r in range(2):
                for h in range(4):
                    op("dve", lambda e: e.tensor_tensor(out=negM[:, r, h:h + 1], in0=nq[:, h, r:r + 1], in1=nk[:, h // 2, r:r + 1], op=ALU.mult), reads=[Bnq, Bnk], writes=[BnegM])
            op("act", lambda e: e.activation(out=negM[:], in_=negM[:], func=AF.Sqrt), reads=[BnegM], writes=[BnegM])
            op("dve", lambda e: e.tensor_scalar(out=negM[:], in0=negM[:], scalar1=-scale, scalar2=None, op0=ALU.mult), reads=[BnegM], writes=[BnegM])
            if grp == 1:
                for r in range(2):
                    op("dve", lambda e: e.tensor_tensor(out=esink[:, r, :], in0=sinkc[:], in1=negM[:, r, :], op=ALU.add), reads=[Bsink, BnegM], writes=[Bes])
                op("act", lambda e: e.activation(out=esink[:], in_=esink[:], func=AF.Exp), reads=[Bes], writes=[Bes])
            if DBG.get("odd_stop") == "M":
                continue
            for (s0, ln, smp) in SEQS:
                for h in range(4):
                    kvh = h // 2
                    mcol = negM[:, smp, h:h + 1]
                    qgs = [(s0 + a, min(512, ln)) for a in range(0, ln, 512)]
                    for qi, (q0, n) in enumerate(qgs):
                        blocks = []
                        if smp:
                            for sbk in range(2):
                                blocks.append((ckT[:, kvh, sbk * 128:(sbk + 1) * 128], Bck, cvtok[:, sbk, kvh, :], Bcv, None))
                        for kb in range(ln // 128):
                            msk = None
                            if smp and grp == 1:
                                rel = kb - 4 * qi
                                if rel < -1 or rel > 4:
                                    continue
                                msk = rel + 1
                            t0 = s0 + kb * 128
                            blocks.append((kT[:, kvh, t0:t0 + 128], Bk, vtok[:, t0 // 128, kvh, :], Bv, msk))
                        po, bpo = g.psacc[0]
                        pm, bpm = g.psacc[1]

                        def score(bi_):
                            kap_, bk__ = blocks[bi_][0], blocks[bi_][1]
                            pst_, bpst_ = g.psf()
                            op("pe", lambda e: e.matmul(pst_[:, 0:n], lhsT=kap_, rhs=qT[:, h, q0:q0 + n], start=True, stop=True), reads=[bk__, Bq], writes=[bpst_])
                            return pst_, bpst_
                        nxt_score = score(0)
                        for bi, (kap, bk_, vap, bv_, msk) in enumerate(blocks):
                            pst, bpst = nxt_score
                            if bi + 1 < len(blocks):
                                nxt_score = score(bi + 1)
                            cnt["e"] += 1
                            eb, beb = ebuf[cnt["e"] % 3]
                            op("act", lambda e: e.activation(out=eb[:, 0:n], in_=pst[:, 0:n], func=AF.Exp, bias=mcol, scale=scale), reads=[bpst, BnegM], writes=[beb])
                            if msk is not None:
                                op("pool", lambda e: e.tensor_tensor(out=eb[:, 0:n], in0=eb[:, 0:n], in1=band[:, msk, 0:n], op=ALU.mult), reads=[beb, Bband], writes=[beb])
                            first, last = bi == 0, bi == len(blocks) - 1
                            op("pe", lambda e: e.matmul(po[:, 0:n], lhsT=vap, rhs=eb[:, 0:n], start=first, stop=last), reads=[bv_, beb], writes=[bpo])
                            op("pe", lambda e: e.matmul(pm[:, 0:n], lhsT=ones_b[:], rhs=eb[:, 0:n], start=first, stop=last), reads=[CB, beb], writes=[bpm])
                        if grp == 1:
                            op("dve", lambda e: e.tensor_scalar(out=rinv[:, 0:n], in0=pm[:, 0:n], scalar1=esink[:, smp, h:h + 1], scalar2=None, op0=ALU.add), reads=[bpm, Bes], writes=[Brinv])
                            op("dve", lambda e: e.reciprocal(out=rinv[:, 0:n], in_=rinv[:, 0:n]), reads=[Brinv], writes=[Brinv])
                        else:
                            op("dve", lambda e: e.reciprocal(out=rinv[:, 0:n], in_=pm[:, 0:n]), reads=[bpm], writes=[Brinv])
                        op("dve", lambda e: e.tensor_tensor(out=mixT[:, grp * 4 + h, q0:q0 + n], in0=po[:, 0:n], in1=rinv[:, 0:n], op=ALU.mult), reads=[bpo, Brinv], writes=[Bmix])
        k.barrier()


_CONSTS = None
_PROG = {}


def _specs(consts, percore, shared):
    sp = {}
    for nm, a in {**consts, **shared, **percore}.items():
        dt = BF16 if a.dtype == ml_dtypes.bfloat16 else F32
        sp[nm] = (a.shape, dt, "ExternalInput")
    return sp


def host_prepare(inp, core):
    pc = {}
    pc["x_all"] = np.ascontiguousarray(np.concatenate(
        [inp["x_prompt"][2 * core], inp["x_prompt"][2 * core + 1], inp["x_sample"][core]], axis=0))
    cv = np.stack([fm(inp["c_ctx"]), fm(inp["c"][core])], axis=-1)
    pc["cvec"] = np.ascontiguousarray(cv)
    pc["state_dn"] = np.ascontiguousarray(inp["state_dn"][core])
    for nm in ("cache_k_c", "cache_v_c", "cache_k_d", "cache_v_d"):
        pc[nm] = np.ascontiguousarray(inp[nm][core])
    return pc


def host_shared(inp):
    sh = {}
    for nm in ("w_mod", "w_out", "ffn_w_up", "ffn_w_down", "ev_w_in", "od_w_in", "hy_w3", "hy_w1", "hy_w2"):
        sh[nm] = inp[nm]
    sh["norm_mix"] = np.ascontiguousarray(np.stack([fm(inp["norm_mix"][i]) for i in range(DEPTH)], axis=1))
    sh["norm_ffn"] = np.ascontiguousarray(np.stack([fm(inp["norm_ffn"][i]) for i in range(DEPTH)], axis=1))
    sh["final_norm"] = fm(inp["final_norm"])
    bm = np.stack([fm(inp["b_mod"][i]) for i in range(DEPTH)], axis=1)
    sh["b_mod"] = np.ascontiguousarray(np.repeat(bm[..., None], 2, axis=-1))
    cw = np.stack([np.stack([fm(inp["ffn_conv_w"][i, t]) for t in range(3)], axis=1) for i in range(DEPTH)], axis=1)
    sh["ffn_conv_w"] = np.ascontiguousarray(cw)
    sh["c_q_norm_fm"] = np.ascontiguousarray(inp["c_q_norm"][:, :, None])
    sh["c_k_norm_fm"] = np.ascontiguousarray(inp["c_k_norm"][:, :, None])
    sh["c_k_norm_row"] = np.ascontiguousarray(np.broadcast_to(inp["c_k_norm"][None], (128, 2, 128)))
    sh["d_sink_bc"] = np.ascontiguousarray(np.broadcast_to(inp["d_sink"][None], (128, 2, 4)))
    cwd = np.stack([np.stack([fm(inp["dn_conv_w"][jj, t]) for t in range(3)], axis=1) for jj in range(2)], axis=0)
    sh["dn_conv_w_fm"] = np.ascontiguousarray(cwd)
    sh["dn_dt_bias_bc"] = np.ascontiguousarray(np.broadcast_to(inp["dn_dt_bias"].reshape(1, 2, 8), (128, 2, 8)))
    sh["dn_a_log_bc"] = np.ascontiguousarray(np.broadcast_to(inp["dn_a_log"].reshape(1, 2, 8), (128, 2, 8)))
    sh["dn_norm_fm"] = np.ascontiguousarray(inp["dn_norm"][:, :, None])
    cwh = np.stack([np.stack([fm(inp["hy_conv_w"][jj, t]) for t in range(3)], axis=1) for jj in range(2)], axis=0)
    sh["hy_conv_w_fm"] = np.ascontiguousarray(cwh)
    sh["hy_conv_b_fm"] = np.ascontiguousarray(np.stack([fm(inp["hy_conv_b"][jj]) for jj in range(2)], axis=0))
    for nm in ("hy_freq1", "hy_b1", "hy_freq2", "hy_b2"):
        sh[nm + "_fm"] = np.ascontiguousarray(inp[nm][:, :, None])
    sh["hy_bias_bc"] = np.ascontiguousarray(np.broadcast_to(inp["hy_bias"][None], (128, 2, 2, 512)))
    sh["ffn_conv_b"] = np.ascontiguousarray(np.stack([fm(inp["ffn_conv_b"][i]) for i in range(DEPTH)], axis=1))
    return sh


def kernel(**inp):
    global _CONSTS
    inp = {k_: np.asarray(v) for k_, v in inp.items()}
    if _CONSTS is None:
        _CONSTS = make_consts()
    consts = _CONSTS
    shared = host_shared(inp)
    per = [host_prepare(inp, c) for c in range(NCORES)]
    extra = DBG.get("extra_inputs")
    if extra:
        for c in range(NCORES):
            per[c].update(extra(c))
    specs = _specs(consts, per[0], shared)
    outs = {
        "y_all": ((T, D), F32, "ExternalOutput"),
    }
    outs["st_out"] = ((2, 2, 2, 4, 128, 128), F32, "ExternalOutput")
    for nm in ("kc_out", "vc_out", "kd_out", "vd_out"):
        outs[nm] = ((2, 2, 256, 2, 128), F32, "ExternalOutput")
    for nm, shp in DBG.get("extra_outputs", {}).items():
        outs[nm] = (shp, F32, "ExternalOutput")
    specs.update(outs)
    nc = build_program(specs)
    in_maps = [{**consts, **shared, **per[c]} for c in range(NCORES)]
    if DBG.get("trace"):
        res = run_bass_kernel_spmd(nc, in_maps, core_ids=list(range(NCORES)), trace=True)
        print("EXEC_NS", res.exec_time_ns)
    else:
        res = run_bass_kernel_spmd(nc, in_maps, core_ids=list(range(NCORES)))
    R = res.results
    DBG["last_results"] = R
    y_prompt = np.stack([R[c // 2]["y_all"][(c % 2) * 256:(c % 2) * 256 + 256] for c in range(16)])
    y_sample = np.stack([R[c]["y_all"][512:1536] for c in range(NCORES)])
    st = np.concatenate([R[c]["st_out"] for c in range(NCORES)], axis=0)
    kv = [np.concatenate([R[c][nm] for c in range(NCORES)], axis=0) for nm in ("kc_out", "vc_out", "kd_out", "vd_out")]
    return (y_prompt, y_sample, st, kv[0], kv[1], kv[2], kv[3])
```

```python
import math
import contextlib
import numpy as np
import ml_dtypes
import concourse.bass as bass
import concourse.mybir as mybir
from concourse.bass_utils import run_bass_kernel_spmd

F32 = mybir.dt.float32
BF16 = mybir.dt.bfloat16
AF = mybir.ActivationFunctionType
ALU = mybir.AluOpType
AX = mybir.AxisListType

NCORES = 8
D = 1024
T = 1536
SEQS = [(0, 256, 0), (256, 256, 0), (512, 1024, 1)]
DEPTH = 4
DFF = 2816
EPS = 1e-6
NEG = -1.0e30
DBG = {"layers": DEPTH, "dump": False}


class Buf:
    __slots__ = ("name", "lw", "rs", "excl", "lwx")

    carry = {}

    def __init__(self, name="b", excl=False):
        self.name = name
        self.lw = None
        self.lwx = {}
        self.rs = dict(Buf.carry)
        self.excl = excl


class K:
    def __init__(self, nc):
        self.nc = nc
        self.eng = {"pe": nc.tensor, "act": nc.scalar, "dve": nc.vector,
                    "pool": nc.gpsimd, "sp": nc.sync}
        self.sem = {}
        self.cnt = {}
        self.seen = {e: {} for e in self.eng}
        self._ctx = []
        for e in self.eng:
            self._newsem(e)
        self.n_instr = 0
        self.n_wait = 0
        Buf.carry = {}

    def _newsem(self, key):
        cm = self.nc.semaphore("s_" + key)
        s = cm.__enter__()
        self._ctx.append(cm)
        self.sem[key] = s
        self.cnt[key] = 0

    def _deps(self, e, reads, writes):
        deps = {}

        def add(k, v):
            if deps.get(k, 0) < v:
                deps[k] = v
        for b in reads:
            if b.lw is not None:
                add(*b.lw)
            for k, v in b.lwx.items():
                add(k, v)
            if b.excl:
                for k, v in b.rs.items():
                    if k != e:
                        add(k, v)
        for b in writes:
            if b.lw is not None:
                add(*b.lw)
            for k, v in b.lwx.items():
                add(k, v)
            for k, v in b.rs.items():
                if k != e:
                    add(k, v)
        if e == "pe":
            deps.pop("pe", None)
        return deps

    def _wait(self, e, deps):
        eng = self.eng[e]
        seen = self.seen[e]
        for k, v in deps.items():
            if seen.get(k, 0) < v:
                eng.wait_ge(self.sem[k], v)
                seen[k] = v
                self.n_wait += 1

    def op(self, e, fn, reads=(), writes=()):
        self._wait(e, self._deps(e, reads, writes))
        ins = fn(self.eng[e])
        self.cnt[e] += 1
        ins.then_inc(self.sem[e], 1)
        v = self.cnt[e]
        for b in writes:
            b.lw = (e, v)
            b.lwx = {}
            b.rs = {}
        for b in reads:
            if b.lw is None or b.lw != (e, v):
                b.rs[e] = v
        self.n_instr += 1
        return ins

    NDSEM = 56

    def dma(self, q, out, in_, reads=(), writes=(), stream="x"):
        if not hasattr(self, "_dn"):
            self._dn = {"p": 0, "h": 0}
        cls_, npool = ("p", 24) if q == "pool" else ("h", 32)
        key = "d_%s%d" % (cls_, self._dn[cls_] % npool)
        self._dn[cls_] += 1
        if key not in self.sem:
            self._newsem(key)
        deps = self._deps(key, reads, writes)
        if self.cnt[key] > 0:
            deps[key] = max(deps.get(key, 0), self.cnt[key])
        self._wait(q, deps)
        ins = self.eng[q].dma_start(out=out, in_=in_)
        self.cnt[key] += 16
        ins.then_inc(self.sem[key], 16)
        v = self.cnt[key]
        for b in writes:
            b.lwx[key] = v
            b.rs = {}
        for b in reads:
            b.rs[key] = v
        self.n_instr += 1
        return ins

    def barrier(self):
        Buf.carry = {k: c for k, c in self.cnt.items() if c > 0}

    def hard_barrier(self):
        for e in self.eng:
            deps = {k: c for k, c in self.cnt.items() if c > 0 and k != e}
            self._wait(e, deps)

    def finish(self):
        sp = self.eng["sp"]
        for k, s in self.sem.items():
            if self.cnt[k] > 0 and k != "sp":
                sp.wait_ge(s, self.cnt[k])
        for cm in reversed(self._ctx):
            cm.__exit__(None, None, None)


def _bf(a):
    return np.ascontiguousarray(a.astype(ml_dtypes.bfloat16))


def make_consts():
    c = {}
    eye = np.eye(128, dtype=np.float32)
    c["ident_f"] = eye
    c["ident_b"] = _bf(eye)
    c["ones_b"] = _bf(np.ones((128, 128), np.float32))
    c["ones_f"] = np.ones((128, 128), np.float32)
    j = np.arange(128)[:, None]
    i = np.arange(128)[None, :]
    c["tri_f"] = (j <= i).astype(np.float32)
    c["tri_b"] = (j >= i).astype(np.float32)
    c["m1_f"] = np.where(i < j, 0.0, NEG).astype(np.float32)
    c["m2_f"] = np.where(i >= j, 0.0, NEG).astype(np.float32)
    c["m1_b"] = np.where(i > j, 0.0, NEG).astype(np.float32)
    c["m2_b"] = np.where(i <= j, 0.0, NEG).astype(np.float32)
    rm = np.zeros((128, 128), np.float32)
    for d in range(128):
        if (d % 64) < 32:
            rm[d + 32, d] = -1.0
        else:
            rm[d - 32, d] = 1.0
    c["rope_rm"] = _bf(rm)
    L = 1024
    rows = L // 64
    row = np.repeat(np.arange(rows), 64).astype(np.float32)
    col = (np.arange(L) % 64).astype(np.float32)
    inv = (10000.0 ** (-np.arange(0, 64, 2, dtype=np.float32) / 64)).astype(np.float32)
    ang = np.concatenate([row[:, None] * inv, col[:, None] * inv], axis=-1)
    dd = np.arange(128)
    idx = (dd // 64) * 32 + (dd % 32)
    c["rope_c"] = np.ascontiguousarray(np.cos(ang)[:, idx].T.astype(np.float32))
    c["rope_s"] = np.ascontiguousarray(np.sin(ang)[:, idx].T.astype(np.float32))
    s = np.arange(128)[:, None]
    q = np.arange(512)[None, :]
    bm = np.stack([(np.abs(128 * rel + s - q) <= 128) for rel in range(-1, 5)]).astype(np.float32)
    c["band"] = _bf(bm)
    for Ls in (256, 1024):
        N = 2 * Ls
        t = np.arange(Ls, dtype=np.float64)[:, None]
        f = np.arange(Ls, dtype=np.float64)[None, :]
        th = 2.0 * np.pi / N
        Fc = np.cos(th * t * f)
        Fs = np.sin(th * t * f)
        Fs[:, 0] = (-1.0) ** np.arange(Ls)
        c[f"dft_f{Ls}"] = _bf(np.concatenate([Fc, Fs], axis=1))
        wf = np.full((Ls, 1), 2.0 / N)
        wf[0, 0] = 1.0 / N
        Ic = wf * Fc.T
        Is = (2.0 / N) * Fs.T
        Is[0, :] = (1.0 / N) * ((-1.0) ** np.arange(Ls))
        c[f"dft_i{Ls}"] = _bf(np.concatenate([Ic, Is], axis=0))
        tl = np.linspace(0.0, 1.0, Ls, dtype=np.float32)[:, None]
        w = ((2.0 * math.pi / Ls) * np.arange(Ls, dtype=np.float32))[:, None]
        fb = np.linspace(1e-4, 15, 16, dtype=np.float32)[None, :]
        feats = np.concatenate([tl, np.cos(fb * w), -np.sin(fb * w)], axis=-1).astype(np.float32)
        c[f"featsT{Ls}"] = np.ascontiguousarray(feats.T)
        deltas = np.abs(np.linspace(math.log(1e-2) / 1.5, math.log(1e-2) / 0.3, 512, dtype=np.float32))
        c[f"hydec{Ls}"] = np.exp(-tl * deltas[None, :]).astype(np.float32)
    return c


def fm(v, n=None):
    v = np.asarray(v, np.float32)
    return np.ascontiguousarray(v.reshape(-1, 128).T)


class Gen:
    def __init__(self, specs):
        self.nc = bass.Bass("TRN2", target_bir_lowering=False)
        nc = self.nc
        self.dr = {}
        for name, (shape, dt, kind) in specs.items():
            self.dr[name] = nc.dram_tensor(name, list(shape), dt, kind=kind).ap()
        self.es = contextlib.ExitStack()
        self.k = K(nc)
        self._uid = 0

    def sb(self, shape, dt, name=None, es=None):
        self._uid += 1
        nm = f"{name or 't'}_{self._uid}"
        if DBG.get("trace_alloc"):
            print("ALLOC", nm, shape, dt, int(np.prod(shape[1:])) * (2 if dt == BF16 else 4))
        t = (es or self.es).enter_context(self.nc.sbuf_tensor(nm, list(shape), dt))
        return t, Buf(nm)

    def setup_rings(self):
        nc = self.nc
        self.psf_ring = []
        for i in range(4):
            t = self.es.enter_context(nc.psum_tensor(f"psf{i}", [128, 512], F32))
            self.psf_ring.append((t, Buf(f"psf{i}", True)))
        self.psacc = []
        for i in range(2):
            t = self.es.enter_context(nc.psum_tensor(f"psacc{i}", [128, 512], F32))
            self.psacc.append((t, Buf(f"psacc{i}", True)))
        self.psb_ring = []
        for i in range(2):
            t = self.es.enter_context(nc.psum_tensor(f"psb{i}", [128, 1024], BF16))
            self.psb_ring.append((t, Buf(f"psb{i}", True)))
        self.slab_ring = [self.sb([128, 4096], BF16, "slab") for _ in range(4)]
        self._pf = self._pb = self._sl = 0

    def psf(self):
        r = self.psf_ring[self._pf % len(self.psf_ring)]
        self._pf += 1
        return r

    def psb(self):
        r = self.psb_ring[self._pb % len(self.psb_ring)]
        self._pb += 1
        return r

    def wload(self, w2d, c0, ncols, kc, segs=None):
        t, b = self.slab_ring[self._sl % len(self.slab_ring)]
        self._sl += 1
        view = t[:, 0:kc * ncols].rearrange("p (k n) -> p k n", k=kc)
        src = w2d.rearrange("(k p) n -> p k n", p=128)
        if segs is None:
            segs = [(c0, 0, ncols)]
        for (sc, dc, n) in segs:
            self.k.dma("pool", view[:, :, dc:dc + n], src[:, :, sc:sc + n], writes=[b], stream="w%d" % ((self._sl - 1) % 4))
        return view, b

    def dump(self, name, ap, buf, shape, dt=F32):
        if name not in DBG.get("extra_outputs", {}):
            return
        if dt != F32:
            with contextlib.ExitStack() as es3:
                t, b = self.sb(list(shape), F32, "dmp", es3)
                self.k.op("dve", lambda e: e.tensor_copy(out=t[:], in_=ap), reads=[buf], writes=[b])
                self.k.dma("sp", self.dr[name], t[:], reads=[b], stream="dump")
                self.k.barrier()
        else:
            self.k.dma("sp", self.dr[name], ap, reads=[buf], stream="dump")

    def op(self, e, fn, reads=(), writes=()):
        return self.k.op(e, fn, reads, writes)

    def load_const(self, name, shape, dt, q="sp"):
        t, b = self.sb(shape, dt, name)
        self.k.dma(q, t[:], self.dr[name], writes=[b], stream="c")
        return t, b


def build_program(specs):
    g = Gen(specs)
    nc, k, dr, op = g.nc, g.k, g.dr, g.op
    with g.es:
        g.setup_rings()
        ident_f, Bc = g.load_const("ident_f", [128, 128], F32)
        CB = Bc

        def lc(name, shape, dt):
            t, b = g.sb(shape, dt, name)
            k.dma("sp", t[:], dr[name], writes=[CB], stream="c")
            return t
        ident_b = lc("ident_b", [128, 128], BF16)
        ones_b = lc("ones_b", [128, 128], BF16)
        ones_f = lc("ones_f", [128, 128], F32)
        tri = [lc("tri_f", [128, 128], F32), lc("tri_b", [128, 128], F32)]
        m1 = [lc("m1_f", [128, 128], F32), lc("m1_b", [128, 128], F32)]
        m2 = [lc("m2_f", [128, 128], F32), lc("m2_b", [128, 128], F32)]
        pv = {}
        for nm in ("norm_mix", "norm_ffn"):
            pv[nm] = lc(nm, [128, DEPTH, 8], F32)
        pv["final_norm"] = lc("final_norm", [128, 8], F32)
        pv["b_mod"] = lc("b_mod", [128, DEPTH, 48, 2], F32)
        pv["ffn_conv_w"] = lc("ffn_conv_w", [128, DEPTH, 3, 22], F32)
        pv["ffn_conv_b"] = lc("ffn_conv_b", [128, DEPTH, 22], F32)
        pv["csil"] = lc("cvec", [128, 8, 2], F32)

        xT, Bx = g.sb([128, 8, T], F32, "xT")
        hT, Bh = g.sb([128, 8, T], BF16, "hT")
        mixT, Bmix = g.sb([128, 8, T], BF16, "mixT")
        modv, Bmod = g.sb([128, 48, 2], F32, "modv")
        modA, BmodA = g.sb([128, 2, 8, 2], F32, "modA")
        cbf, Bcbf = g.sb([128, 8, 2], BF16, "cbf")
        rstd, Brstd = g.sb([128, 512], F32, "rstd")
        sqb = [g.sb([128, 512], BF16, "sq") for _ in range(2)]
        tmpf = [g.sb([128, 512], F32, "tmpf") for _ in range(3)]
        _rr = {"sq": 0, "tmp": 0}

        def nsq():
            _rr["sq"] += 1
            return sqb[_rr["sq"] % 2]

        def ntmp():
            _rr["tmp"] += 1
            return tmpf[_rr["tmp"] % 3]

        op("act", lambda e: e.activation(out=cbf[:], in_=pv["csil"][:], func=AF.Silu), reads=[CB], writes=[Bcbf])

        with contextlib.ExitStack() as es2:
            xin = [g.sb([128, D], F32, "xin", es2) for _ in range(2)]
            for tt in range(T // 128):
                xt, bx = xin[tt % 2]
                k.dma("sp", xt[:], dr["x_all"][tt * 128:(tt + 1) * 128, :], writes=[bx], stream="xin%d" % (tt % 2))
                for half in range(2):
                    pt, bp = g.psf()
                    for jj in range(4):
                        kc = half * 4 + jj
                        op("pe", lambda e: e.transpose(out=pt[:, jj * 128:(jj + 1) * 128], in_=xt[:, kc * 128:(kc + 1) * 128], identity=ident_f[:]), reads=[bx, CB], writes=[bp])
                    op("dve" if half == 0 else "act",
                       (lambda e: e.tensor_copy(out=xT[:, half * 4:half * 4 + 4, tt * 128:(tt + 1) * 128], in_=pt[:].rearrange("p (a b) -> p a b", a=4))) if half == 0 else
                       (lambda e: e.activation(out=xT[:, half * 4:half * 4 + 4, tt * 128:(tt + 1) * 128], in_=pt[:].rearrange("p (a b) -> p a b", a=4), func=AF.Copy)),
                       reads=[bp], writes=[Bx])
            k.barrier()

        def modulation(i):
            pm, bpm = g.psacc[0]
            for sl in range(12):
                wv, bw = g.wload(dr["w_mod"][i], sl * 512, 512, 8)
                for cc in range(4):
                    ch = sl * 4 + cc
                    for kc in range(8):
                        op("pe", lambda e: e.matmul(pm[:, ch * 2:ch * 2 + 2], lhsT=wv[:, kc, cc * 128:(cc + 1) * 128], rhs=cbf[:, kc, :], start=(kc == 0), stop=(kc == 7)), reads=[bw, Bcbf], writes=[bpm])
            op("dve", lambda e: e.tensor_tensor(out=modv[:].rearrange("p a b -> p (a b)"), in0=pm[:, 0:96], in1=pv["b_mod"][:, i].rearrange("p a b -> p (a b)"), op=ALU.add), reads=[bpm, CB], writes=[Bmod])
            for which, (gname, sc0) in enumerate((("norm_mix", 8), ("norm_ffn", 32))):
                for r in range(2):
                    op("dve", lambda e: e.tensor_scalar(out=modA[:, which, :, r], in0=modv[:, sc0:sc0 + 8, r], scalar1=1.0, scalar2=math.sqrt(D), op0=ALU.add, op1=ALU.mult), reads=[Bmod], writes=[BmodA])
                    op("dve", lambda e: e.tensor_tensor(out=modA[:, which, :, r], in0=modA[:, which, :, r], in1=pv[gname][:, i, :], op=ALU.mult), reads=[BmodA, CB], writes=[BmodA])

        epsc = {}

        def eps_col(val):
            if val not in epsc:
                t, b = g.sb([128, 1], F32, "epsc")
                op("pool", lambda e: e.memset(t[:], float(val)), writes=[b])
                epsc[val] = (t, b)
            return epsc[val]

        for _v in (D * EPS, EPS, 128 * EPS):
            eps_col(float(_v))

        def ss_to_rstd(ps_ap, out_ap, bps, bout, n, epsn, scale=1.0):
            et, eb = eps_col(float(n * epsn))
            op("act", lambda e: e.activation(out=out_ap, in_=ps_ap, func=AF.Sqrt, bias=et[:, 0:1], scale=float(scale)), reads=[bps, eb], writes=[bout])
            op("dve", lambda e: e.reciprocal(out=out_ap, in_=out_ap), reads=[bout], writes=[bout])

        def adaln(which, sh0):
            for tg in range(3):
                r = 0 if tg == 0 else 1
                ts = slice(tg * 512, (tg + 1) * 512)
                pss, bpss = g.psf()
                for kc in range(8):
                    sq, bsq = nsq()
                    op("act", lambda e: e.activation(out=sq[:], in_=xT[:, kc, ts], func=AF.Square), reads=[Bx], writes=[bsq])
                    op("pe", lambda e: e.matmul(pss[:], lhsT=ones_b[:], rhs=sq[:], start=(kc == 0), stop=(kc == 7)), reads=[bsq, CB], writes=[bpss])
                ss_to_rstd(pss[:], rstd[:], bpss, Brstd, D, EPS)
                for kc in range(8):
                    tm, btm = ntmp()
                    op("dve", lambda e: e.tensor_tensor(out=tm[:], in0=xT[:, kc, ts], in1=rstd[:], op=ALU.mult), reads=[Bx, Brstd], writes=[btm])
                    op("act", lambda e: e.activation(out=hT[:, kc, ts], in_=tm[:], func=AF.Identity, scale=modA[:, which, kc, r:r + 1], bias=modv[:, sh0 + kc, r:r + 1]), reads=[btm, BmodA, Bmod], writes=[Bh])

        def proj_residual(w2d, kcn, src, bsrc, g0):
            ncol = 512 if kcn <= 8 else 128
            for c0 in range(0, D, ncol):
                wv, bw = g.wload(w2d, c0, ncol, kcn)
                for mm in range(ncol // 128):
                    m = c0 // 128 + mm
                    for tg in range(3):
                        r = 0 if tg == 0 else 1
                        ts = slice(tg * 512, (tg + 1) * 512)
                        pp, bpp = g.psf()
                        for kc in range(kcn):
                            op("pe", lambda e: e.matmul(pp[:], lhsT=wv[:, kc, mm * 128:(mm + 1) * 128], rhs=src[:, kc, ts], start=(kc == 0), stop=(kc == kcn - 1)), reads=[bw, bsrc], writes=[bpp])
                        op("dve", lambda e: e.scalar_tensor_tensor(out=xT[:, m, ts], in0=pp[:], scalar=modv[:, g0 + m, r:r + 1], in1=xT[:, m, ts], op0=ALU.mult, op1=ALU.add), reads=[bpp, Bmod, Bx], writes=[Bx])

        def ffn(i):
            with contextlib.ExitStack() as es2:
                gbuf, Bg = g.sb([128, 11, T], BF16, "gbuf", es2)
                abuf = [g.sb([128, T], F32, "abuf", es2) for _ in range(2)]
                cbuf = [g.sb([128, T], F32, "cbuf", es2) for _ in range(2)]
                cw = pv["ffn_conv_w"]
                for half in range(2):
                    slabs = {}

                    def part_a(cp):
                        c = half * 11 + cp
                        wv, bw = g.wload(dr["ffn_w_up"][i], 0, 256, 8, segs=[(c * 128, 0, 128), (DFF + c * 128, 128, 128)])
                        slabs[cp] = (wv, bw)
                        ab, bab = abuf[c % 2]
                        cb_, bcb = cbuf[c % 2]
                        for tg in range(3):
                            ts = slice(tg * 512, (tg + 1) * 512)
                            pa, bpa = g.psf()
                            for kc in range(8):
                                op("pe", lambda e: e.matmul(pa[:], lhsT=wv[:, kc, 0:128], rhs=hT[:, kc, ts], start=(kc == 0), stop=(kc == 7)), reads=[bw, Bh], writes=[bpa])
                            op("act", lambda e: e.activation(out=ab[:, ts], in_=pa[:], func=AF.Copy), reads=[bpa], writes=[bab])
                        for (s0, ln, _) in SEQS:
                            op("dve", lambda e: e.tensor_scalar(out=cb_[:, s0:s0 + ln], in0=ab[:, s0:s0 + ln], scalar1=cw[:, i, 1, c:c + 1], scalar2=pv["ffn_conv_b"][:, i, c:c + 1], op0=ALU.mult, op1=ALU.add), reads=[bab, CB], writes=[bcb])
                            op("dve", lambda e: e.scalar_tensor_tensor(out=cb_[:, s0 + 1:s0 + ln], in0=ab[:, s0:s0 + ln - 1], scalar=cw[:, i, 0, c:c + 1], in1=cb_[:, s0 + 1:s0 + ln], op0=ALU.mult, op1=ALU.add), reads=[bab, CB, bcb], writes=[bcb])
                            op("dve", lambda e: e.scalar_tensor_tensor(out=cb_[:, s0:s0 + ln - 1], in0=ab[:, s0 + 1:s0 + ln], scalar=cw[:, i, 2, c:c + 1], in1=cb_[:, s0:s0 + ln - 1], op0=ALU.mult, op1=ALU.add), reads=[bab, CB, bcb], writes=[bcb])
                        op("act", lambda e: e.activation(out=ab[:], in_=cb_[:], func=AF.Silu), reads=[bcb], writes=[bab])

                    def part_b(cp):
                        c = half * 11 + cp
                        wv, bw = slabs.pop(cp)
                        ab, bab = abuf[c % 2]
                        for tg in range(3):
                            ts = slice(tg * 512, (tg + 1) * 512)
                            pb_, bpb = g.psf()
                            for kc in range(8):
                                op("pe", lambda e: e.matmul(pb_[:], lhsT=wv[:, kc, 128:256], rhs=hT[:, kc, ts], start=(kc == 0), stop=(kc == 7)), reads=[bw, Bh], writes=[bpb])
                            op("dve", lambda e: e.tensor_tensor(out=gbuf[:, cp, ts], in0=pb_[:], in1=ab[:, ts], op=ALU.mult), reads=[bpb, bab], writes=[Bg])

                    part_a(0)
                    for cp in range(11):
                        if cp + 1 < 11:
                            part_a(cp + 1)
                        part_b(cp)
                    wd = dr["ffn_w_down"][i][half * 11 * 128:(half + 1) * 11 * 128, :]
                    proj_residual(wd, 11, gbuf, Bg, 40)
                k.barrier()

        import sys
        nl = DBG["layers"]
        for i in range(nl):
            modulation(i)
            adaln(0, 0)
            if i % 2 == 0:
                EVEN_MIXER(g, locals(), i)
            else:
                ODD_MIXER(g, locals(), i)
            proj_residual(dr["w_out"][i], 8, mixT, Bmix, 16)
            adaln(1, 24)
            ffn(i)

        with contextlib.ExitStack() as es2:
            yout = [g.sb([128, D], F32, "yout", es2) for _ in range(2)]
            fin, Bfin = g.sb([128, 8, 512], F32, "fin", es2)
            for tg in range(3):
                ts = slice(tg * 512, (tg + 1) * 512)
                pss, bpss = g.psf()
                for kc in range(8):
                    sq, bsq = nsq()
                    op("act", lambda e: e.activation(out=sq[:], in_=xT[:, kc, ts], func=AF.Square), reads=[Bx], writes=[bsq])
                    op("pe", lambda e: e.matmul(pss[:], lhsT=ones_b[:], rhs=sq[:], start=(kc == 0), stop=(kc == 7)), reads=[bsq, CB], writes=[bpss])
                ss_to_rstd(pss[:], rstd[:], bpss, Brstd, 1, EPS, scale=1.0 / D)
                for kc in range(8):
                    op("dve", lambda e: e.scalar_tensor_tensor(out=fin[:, kc, :], in0=xT[:, kc, ts], scalar=pv["final_norm"][:, kc:kc + 1], in1=rstd[:], op0=ALU.mult, op1=ALU.mult), reads=[Bx, CB, Brstd], writes=[Bfin])
                for t4 in range(4):
                    tt = tg * 4 + t4
                    yo, byo = yout[tt % 2]
                    for half in range(2):
                        pt, bp = g.psf()
                        for jj in range(4):
                            kc = half * 4 + jj
                            op("pe", lambda e: e.transpose(out=pt[:, jj * 128:(jj + 1) * 128], in_=fin[:, kc, t4 * 128:(t4 + 1) * 128], identity=ident_f[:]), reads=[Bfin, CB], writes=[bp])
                        op("act" if half else "dve",
                           (lambda e: e.activation(out=yo[:, half * 512:(half + 1) * 512], in_=pt[:], func=AF.Copy)) if half else
                           (lambda e: e.tensor_copy(out=yo[:, half * 512:(half + 1) * 512], in_=pt[:])),
                           reads=[bp], writes=[byo])
                    k.dma("sp", dr["y_all"][tt * 128:(tt + 1) * 128, :], yo[:], reads=[byo], stream="yo%d" % (tt % 2))
            k.barrier()
        k.finish()
    print("program: instr", k.n_instr, "waits", k.n_wait)
    return nc


def _zero_fill(g, L, names):
    k, dr, op = L["k"], L["dr"], L["op"]
    with contextlib.ExitStack() as es2:
        z, bz = g.sb([128, 2048], F32, "zfill", es2)
        op("pool", lambda e: e.memset(z[:], 0.0), writes=[bz])
        for nm, view in names:
            k.dma("sp", view, z[:, 0:view.shape[-1]] if len(view.shape) == 2 else z[:].rearrange("p (a b) -> p a b", b=view.shape[-1])[:, 0:view.shape[1], :], reads=[bz], stream="zf")
        k.barrier()


def EVEN_MIXER(g, L, i):
    k, dr, op = L["k"], L["dr"], L["op"]
    hT, Bh, mixT, Bmix, CB = L["hT"], L["Bh"], L["mixT"], L["Bmix"], L["CB"]
    ones_b, ones_f, ident_b = L["ones_b"], L["ones_f"], L["ident_b"]
    tri, m1, m2 = L["tri"], L["m1"], L["m2"]
    ss_to_rstd = L["ss_to_rstd"]
    sqb, tmpf = L["sqb"], L["tmpf"]
    j = i // 2
    W = dr["ev_w_in"][j]
    with contextlib.ExitStack() as es:
        def sb(shape, dt, nm, es_=None):
            return g.sb(shape, dt, nm, es_ or es)
        cw, Bcw = sb([128, 3, 12], F32, "dncw")
        dtb, Bdtb = sb([128, 8], F32, "dtb")
        negA, BnegA = sb([128, 8], F32, "negA")
        dnn, Bdnn = sb([128, 1], F32, "dnn")
        beta, Bbeta = sb([128, 12, 8], F32, "beta")
        gg, Bgg = sb([128, 12, 8], F32, "gg")
        gcum, Bgcum = sb([128, 12, 8], F32, "gcum")
        bg, Bbg = sb([128, 12, 8], F32, "bg")
        k.dma("sp", cw[:], dr["dn_conv_w_fm"][j], writes=[Bcw], stream="ec")
        k.dma("sp", dtb[:], dr["dn_dt_bias_bc"][:, j, :], writes=[Bdtb], stream="ec")
        k.dma("sp", negA[:], dr["dn_a_log_bc"][:, j, :], writes=[BnegA], stream="ec")
        k.dma("sp", dnn[:], dr["dn_norm_fm"][j], writes=[Bdnn], stream="ec")
        op("act", lambda e: e.activation(out=negA[:], in_=negA[:], func=AF.Exp), reads=[BnegA], writes=[BnegA])
        op("dve", lambda e: e.tensor_scalar(out=negA[:], in0=negA[:], scalar1=-1.0, scalar2=None, op0=ALU.mult), reads=[BnegA], writes=[BnegA])
        op("dve", lambda e: e.tensor_scalar(out=dnn[:], in0=dnn[:], scalar1=math.sqrt(128.0), scalar2=None, op0=ALU.mult), reads=[Bdnn], writes=[Bdnn])
        wba, bwba = g.wload(W, 2048, 16, 8)
        for tt in range(12):
            pp, bpp = g.psf()
            for kc in range(8):
                op("pe", lambda e: e.matmul(pp[:, 0:16], lhsT=hT[:, kc, tt * 128:(tt + 1) * 128], rhs=wba[:, kc, :], start=(kc == 0), stop=(kc == 7)), reads=[bwba, Bh], writes=[bpp])
            op("act", lambda e: e.activation(out=beta[:, tt, :], in_=pp[:, 0:8], func=AF.Sigmoid), reads=[bpp], writes=[Bbeta])
            op("dve", lambda e: e.tensor_tensor(out=gg[:, tt, :], in0=pp[:, 8:16], in1=dtb[:], op=ALU.add), reads=[bpp, Bdtb], writes=[Bgg])
        op("act", lambda e: e.activation(out=gg[:], in_=gg[:], func=AF.Exp), reads=[Bgg], writes=[Bgg])
        op("act", lambda e: e.activation(out=gg[:], in_=gg[:], func=AF.Ln, bias=ones_f[:, 0:1]), reads=[Bgg, CB], writes=[Bgg])
        for tt in range(12):
            op("dve", lambda e: e.tensor_tensor(out=gg[:, tt, :], in0=gg[:, tt, :], in1=negA[:], op=ALU.mult), reads=[Bgg, BnegA], writes=[Bgg])
        for tt in range(12):
            pp, bpp = g.psf()
            for d_ in range(2):
                op("pe", lambda e: e.matmul(pp[:, d_ * 4:d_ * 4 + 4], lhsT=tri[d_][:], rhs=gg[:, tt, d_ * 4:d_ * 4 + 4], start=True, stop=True), reads=[CB, Bgg], writes=[bpp])
            op("dve", lambda e: e.tensor_copy(out=gcum[:, tt, :], in_=pp[:, 0:8]), reads=[bpp], writes=[Bgcum])
        op("act", lambda e: e.activation(out=bg[:], in_=gcum[:], func=AF.Exp), reads=[Bgcum], writes=[Bbg])
        op("dve", lambda e: e.tensor_tensor(out=bg[:], in0=bg[:], in1=beta[:], op=ALU.mult), reads=[Bbg, Bbeta], writes=[Bbg])

        g.dump("d_beta", beta[:], Bbeta, [128, 12, 8])
        g.dump("d_gg", gg[:], Bgg, [128, 12, 8])
        g.dump("d_gcum", gcum[:], Bgcum, [128, 12, 8])
        for h in range(4):
            with contextlib.ExitStack() as eh:
                qT, Bq = sb([128, T], BF16, "dq", eh)
                kT, Bk = sb([128, T], BF16, "dk", eh)
                vT, Bv = sb([128, T], BF16, "dv", eh)
                zs, Bzs = sb([128, T], F32, "dz", eh)
                oacc, Bo = sb([128, T], F32, "doa", eh)
                ktok, Bkt = sb([128, 12, 128], BF16, "dkt", eh)
                vtok, Bvt = sb([128, 12, 128], BF16, "dvt", eh)
                wv, bw = g.wload(W, 0, 512, 8, segs=[(h * 128, 0, 128), (512 + h * 128, 128, 128), (1024 + h * 128, 256, 128), (1536 + h * 128, 384, 128)])
                with contextlib.ExitStack() as e1:
                    raw, Braw = sb([128, T], F32, "draw", e1)
                    cv, Bcv = sb([128, T], F32, "dcv", e1)
                    for comp in range(4):
                        for tg in range(3):
                            ts = slice(tg * 512, (tg + 1) * 512)
                            pp, bpp = g.psf()
                            for kc in range(8):
                                op("pe", lambda e: e.matmul(pp[:], lhsT=wv[:, kc, comp * 128:(comp + 1) * 128], rhs=hT[:, kc, ts], start=(kc == 0), stop=(kc == 7)), reads=[bw, Bh], writes=[bpp])
                            if comp == 3:
                                op("act", lambda e: e.activation(out=zs[:, ts], in_=pp[:], func=AF.Silu), reads=[bpp], writes=[Bzs])
                            else:
                                op("act", lambda e: e.activation(out=raw[:, ts], in_=pp[:], func=AF.Copy), reads=[bpp], writes=[Braw])
                        if comp == 3:
                            continue
                        cc = comp * 4 + h
                        for (s0, ln, _) in SEQS:
                            op("dve", lambda e: e.tensor_scalar(out=cv[:, s0:s0 + ln], in0=raw[:, s0:s0 + ln], scalar1=cw[:, 1, cc:cc + 1], scalar2=None, op0=ALU.mult), reads=[Braw, Bcw], writes=[Bcv])
                            op("dve", lambda e: e.scalar_tensor_tensor(out=cv[:, s0 + 1:s0 + ln], in0=raw[:, s0:s0 + ln - 1], scalar=cw[:, 0, cc:cc + 1], in1=cv[:, s0 + 1:s0 + ln], op0=ALU.mult, op1=ALU.add), reads=[Braw, Bcw, Bcv], writes=[Bcv])
                            op("dve", lambda e: e.scalar_tensor_tensor(out=cv[:, s0:s0 + ln - 1], in0=raw[:, s0 + 1:s0 + ln], scalar=cw[:, 2, cc:cc + 1], in1=cv[:, s0:s0 + ln - 1], op0=ALU.mult, op1=ALU.add), reads=[Braw, Bcw, Bcv], writes=[Bcv])
                        op("act", lambda e: e.activation(out=cv[:], in_=cv[:], func=AF.Silu), reads=[Bcv], writes=[Bcv])
                        if comp == 2:
                            op("dve", lambda e: e.tensor_copy(out=vT[:], in_=cv[:]), reads=[Bcv], writes=[Bv])
                            continue
                        dst, bdst = (qT, Bq) if comp == 0 else (kT, Bk)
                        for tg in range(3):
                            ts = slice(tg * 512, (tg + 1) * 512)
                            sq, bsq = sqb[tg % 2]
                            op("act", lambda e: e.activation(out=sq[:], in_=cv[:, ts], func=AF.Square), reads=[Bcv], writes=[bsq])
                            p2, bp2 = g.psf()
                            op("pe", lambda e: e.matmul(p2[:], lhsT=ones_b[:], rhs=sq[:], start=True, stop=True), reads=[bsq, CB], writes=[bp2])
                            rs_, brs_ = tmpf[tg % 3]
                            ss_to_rstd(p2[:], rs_[:], bp2, brs_, 1, EPS)
                            if comp == 0:
                                op("dve", lambda e: e.scalar_tensor_tensor(out=dst[:, ts], in0=cv[:, ts], scalar=128.0 ** -0.5, in1=rs_[:], op0=ALU.mult, op1=ALU.mult), reads=[Bcv, brs_], writes=[bdst])
                            else:
                                op("dve", lambda e: e.tensor_tensor(out=dst[:, ts], in0=cv[:, ts], in1=rs_[:], op=ALU.mult), reads=[Bcv, brs_], writes=[bdst])
                    k.barrier()
                if h == 0:
                    g.dump("d_qT", qT[:], Bq, [128, T], BF16)
                    g.dump("d_kT", kT[:], Bk, [128, T], BF16)
                    g.dump("d_vT", vT[:], Bv, [128, T], BF16)
                for tt in range(12):
                    ptb, bptb = g.psb()
                    op("pe", lambda e: e.transpose(out=ptb[:, 0:128], in_=kT[:, tt * 128:(tt + 1) * 128], identity=ident_b[:]), reads=[Bk, CB], writes=[bptb])
                    op("pe", lambda e: e.transpose(out=ptb[:, 128:256], in_=vT[:, tt * 128:(tt + 1) * 128], identity=ident_b[:]), reads=[Bv, CB], writes=[bptb])
                    op("dve", lambda e: e.tensor_copy(out=ktok[:, tt, :], in_=ptb[:, 0:128]), reads=[bptb], writes=[Bkt])
                    op("act", lambda e: e.activation(out=vtok[:, tt, :], in_=ptb[:, 128:256], func=AF.Copy), reads=[bptb], writes=[Bvt])

                groups = []
                groups.append(([(0, 0), (1, 0), (2, 0), (3, 0)], [(0, 0, [0, 1], True, True), (1, 0, [2, 3], True, True)]))
                groups.append(([(0, 1), (1, 1), (2, 1), (3, 1)], [(0, 1, [1, 0], True, True), (1, 1, [3, 2], True, True)]))
                groups.append(([(4 + c, 0) for c in range(4)], [(2, 0, [0, 1, 2, 3], True, False)]))
                groups.append(([(8 + c, 0) for c in range(4)], [(2, 0, [0, 1, 2, 3], False, True)]))
                groups.append(([(8 + c, 1) for c in range(4)], [(2, 1, [3, 2, 1, 0], True, False)]))
                groups.append(([(4 + c, 1) for c in range(4)], [(2, 1, [3, 2, 1, 0], False, True)]))
                Ss = [sb([128, 128], F32, "S", eh) for _ in range(2)]
                Sbs = [sb([128, 128], BF16, "Sb", eh) for _ in range(2)]
                for chains, scans in groups:
                    with contextlib.ExitStack() as e2:
                        NCH = len(chains)
                        Az, BAz = sb([128, NCH, 2, 128], F32, "Az", e2)
                        PP = [sb([128, NCH, 2, 128], F32, "PP", e2) for _ in range(2)]
                        RTf, BRTf = sb([128, NCH, 128], F32, "RTf", e2)
                        RT, BRT = sb([128, NCH, 128], BF16, "RT", e2)
                        attnT, Bat = sb([128, NCH, 128], BF16, "attnT", e2)
                        kbg, Bkbg = sb([128, NCH, 128], BF16, "kbg", e2)
                        vb, Bvb = sb([128, NCH, 128], BF16, "vb", e2)
                        kdec, Bkd = sb([128, NCH, 128], BF16, "kdec", e2)
                        qgT, Bqg = sb([128, NCH, 128], BF16, "qgT", e2)
                        glb, Bglb = sb([128, NCH], F32, "glb", e2)
                        kde, Bkde = sb([128, NCH], F32, "kde", e2)
                        Lg = [sb([128, 128], F32, "Lg", e2) for _ in range(2)]
                        D1 = [sb([128, 128], F32, "D1", e2) for _ in range(2)]
                        D2 = [sb([128, 128], F32, "D2", e2) for _ in range(2)]
                        EG = [sb([128, 128], F32, "EG", e2) for _ in range(2)]
                        nwTa, BnwT = sb([128, NCH, 128], BF16, "nwTa", e2)
                        vnews = [sb([128, 128], BF16, "vnew", e2) for _ in range(2)]
                        BSZ = 2
                        for b0 in range(0, NCH, BSZ):
                            batch = [(c, chains[c][0], chains[c][1]) for c in range(b0, min(NCH, b0 + BSZ))]
                            st = {}
                            for (c, tt, d_) in batch:
                                col = d_ * 4 + h
                                lg, blg = Lg[c % 2]
                                op("act", lambda e: e.activation(out=lg[:], in_=ones_f[:], func=AF.Identity, scale=gg[:, tt, col:col + 1]), reads=[CB, Bgg], writes=[blg])
                            for (c, tt, d_) in batch:
                                tsl = slice(tt * 128, (tt + 1) * 128)
                                lg, blg = Lg[c % 2]
                                pG, bpG = g.psf()
                                op("pe", lambda e: e.matmul(pG[:, 0:128], lhsT=lg[:], rhs=tri[d_][:], start=True, stop=True), reads=[blg, CB], writes=[bpG])
                                pK, bpK = g.psf()
                                op("pe", lambda e: e.matmul(pK[:, 0:128], lhsT=kT[:, tsl], rhs=kT[:, tsl], start=True, stop=True), reads=[Bk], writes=[bpK])
                                op("pe", lambda e: e.matmul(pK[:, 128:256], lhsT=kT[:, tsl], rhs=qT[:, tsl], start=True, stop=True), reads=[Bk, Bq], writes=[bpK])
                                st[c] = (pG, bpG, pK, bpK)
                            for (c, tt, d_) in batch:
                                col = d_ * 4 + h
                                gcol = gcum[:, tt, col:col + 1]
                                pG, bpG, pK, bpK = st[c]
                                d1, bd1 = D1[c % 2]
                                d2, bd2 = D2[c % 2]
                                eg, beg = EG[c % 2]
                                lastc = 127 if d_ == 0 else 0
                                op("dve", lambda e: e.scalar_tensor_tensor(out=d1[:], in0=pG[:, 0:128], scalar=gcol, in1=m1[d_][:], op0=ALU.subtract, op1=ALU.subtract), reads=[bpG, Bgcum, CB], writes=[bd1])
                                op("dve", lambda e: e.scalar_tensor_tensor(out=d2[:], in0=pG[:, 0:128], scalar=gcol, in1=m2[d_][:], op0=ALU.subtract, op1=ALU.add), reads=[bpG, Bgcum, CB], writes=[bd2])
                                op("dve", lambda e: e.tensor_scalar(out=kde[:, c:c + 1], in0=pG[:, lastc:lastc + 1], scalar1=gcol, scalar2=None, op0=ALU.subtract), reads=[bpG, Bgcum], writes=[Bkde])
                                op("act", lambda e: e.activation(out=eg[:], in_=pG[:, 0:128], func=AF.Exp), reads=[bpG], writes=[beg])
                                op("act", lambda e: e.activation(out=d1[:], in_=d1[:], func=AF.Exp, scale=-1.0), reads=[bd1], writes=[bd1])
                                op("act", lambda e: e.activation(out=d2[:], in_=d2[:], func=AF.Exp), reads=[bd2], writes=[bd2])
                                op("act", lambda e: e.activation(out=kde[:, c:c + 1], in_=kde[:, c:c + 1], func=AF.Exp), reads=[Bkde], writes=[Bkde])
                            for (c, tt, d_) in batch:
                                col = d_ * 4 + h
                                tsl = slice(tt * 128, (tt + 1) * 128)
                                pG, bpG, pK, bpK = st[c]
                                d1, bd1 = D1[c % 2]
                                d2, bd2 = D2[c % 2]
                                eg, beg = EG[c % 2]
                                lastc = 127 if d_ == 0 else 0
                                op("dve", lambda e: e.scalar_tensor_tensor(out=Az[:, c, 0, :], in0=pK[:, 0:128], scalar=beta[:, tt, col:col + 1], in1=d1[:], op0=ALU.mult, op1=ALU.mult), reads=[bpK, Bbeta, bd1], writes=[BAz])
                                op("dve", lambda e: e.tensor_tensor(out=attnT[:, c, :], in0=pK[:, 128:256], in1=d2[:], op=ALU.mult), reads=[bpK, bd2], writes=[Bat])
                                op("dve", lambda e: e.tensor_copy(out=glb[:, c:c + 1], in_=eg[:, lastc:lastc + 1]), reads=[beg], writes=[Bglb])
                                op("pool", lambda e: e.tensor_tensor(out=qgT[:, c, :], in0=qT[:, tsl], in1=eg[:], op=ALU.mult), reads=[Bq, beg], writes=[Bqg])
                            for (c, tt, d_) in batch:
                                ptb, bptb = g.psf()
                                op("pe", lambda e: e.transpose(out=ptb[:, 0:128], in_=Az[:, c, 0, :], identity=L["ident_f"][:]), reads=[BAz, CB], writes=[bptb])
                                op("dve", lambda e: e.tensor_copy(out=Az[:, c, 1, :], in_=ptb[:, 0:128]), reads=[bptb], writes=[BAz])
                                op("dve", lambda e: e.tensor_tensor(out=RTf[:, c, :], in0=L["ident_f"][:], in1=ptb[:, 0:128], op=ALU.subtract), reads=[bptb, CB], writes=[BRTf])
                            for (c, tt, d_) in batch:
                                col = d_ * 4 + h
                                op("dve", lambda e: e.tensor_scalar(out=kbg[:, c, :], in0=ktok[:, tt, :], scalar1=bg[:, tt, col:col + 1], scalar2=None, op0=ALU.mult), reads=[Bkt, Bbg], writes=[Bkbg])
                                op("act", lambda e: e.activation(out=vb[:, c, :], in_=vtok[:, tt, :], func=AF.Identity, scale=beta[:, tt, col:col + 1]), reads=[Bvt, Bbeta], writes=[Bvb])
                                op("dve", lambda e: e.tensor_scalar(out=kdec[:, c, :], in0=ktok[:, tt, :], scalar1=kde[:, c:c + 1], scalar2=None, op0=ALU.mult), reads=[Bkt, Bkde], writes=[Bkd])
                        NP = NCH // 2
                        BPPp = [[Buf("PPp") for _ in range(NP)] for _ in range(2)]
                        BRp = [Buf("RTp") for _ in range(NP)]
                        cur = Az
                        bcur = [[BAz] for _ in range(NP)]
                        for step in range(6):
                            nxt = PP[step % 2][0]
                            bnx = BPPp[step % 2]
                            last = step == 5
                            for p in range(NP):
                                pp, bpp = g.psf()
                                for cc_ in range(2):
                                    c = 2 * p + cc_
                                    op("pe", lambda e: e.matmul(pp[:, cc_ * 256:cc_ * 256 + 128], lhsT=cur[:, c, 1, :], rhs=cur[:, c, 0, :], start=True, stop=True), reads=bcur[p], writes=[bpp])
                                    if not last:
                                        op("pe", lambda e: e.matmul(pp[:, cc_ * 256 + 128:cc_ * 256 + 256], lhsT=cur[:, c, 0, :], rhs=cur[:, c, 1, :], start=True, stop=True), reads=bcur[p], writes=[bpp])
                                if last:
                                    op("dve" if p % 2 == 0 else "act",
                                       (lambda e: e.tensor_copy(out=nxt[:, 2 * p:2 * p + 2, 0, :], in_=pp[:].rearrange("p (a b c) -> p a b c", a=2, b=2)[:, :, 0, :])) if p % 2 == 0 else
                                       (lambda e: e.activation(out=nxt[:, 2 * p:2 * p + 2, 0, :], in_=pp[:].rearrange("p (a b c) -> p a b c", a=2, b=2)[:, :, 0, :], func=AF.Copy)),
                                       reads=[bpp], writes=[bnx[p]])
                                elif p % 2 == 0:
                                    op("dve", lambda e: e.tensor_copy(out=nxt[:, 2 * p:2 * p + 2].rearrange("p a b c -> p (a b c)"), in_=pp[:]), reads=[bpp], writes=[bnx[p]])
                                else:
                                    op("act", lambda e: e.activation(out=nxt[:, 2 * p:2 * p + 2].rearrange("p a b c -> p (a b c)"), in_=pp[:], func=AF.Copy), reads=[bpp], writes=[bnx[p]])
                            for p in range(NP):
                                pr, bpr = g.psf()
                                for cc_ in range(2):
                                    c = 2 * p + cc_
                                    op("pe", lambda e: e.matmul(pr[:, cc_ * 128:(cc_ + 1) * 128], lhsT=nxt[:, c, 0, :], rhs=RTf[:, c, :], start=True, stop=True), reads=[bnx[p], BRTf, BRp[p]], writes=[bpr])
                                op("dve", lambda e: e.tensor_tensor(out=RTf[:, 2 * p:2 * p + 2].rearrange("p a b -> p (a b)"), in0=pr[:, 0:256], in1=RTf[:, 2 * p:2 * p + 2].rearrange("p a b -> p (a b)"), op=ALU.add), reads=[bpr, BRTf, BRp[p]], writes=[BRp[p]])
                            cur = nxt
                            bcur = [[bnx[p]] for p in range(NP)]
                        op("act", lambda e: e.activation(out=RT[:], in_=RTf[:], func=AF.Copy), reads=[BRTf] + BRp, writes=[BRT])
                        if h == 0 and chains[0] == (4, 0):
                            g.dump("d_Az", Az[:, 0], BAz, [128, 2, 128])
                            g.dump("d_RT", RT[:, 0], BRT, [128, 128], BF16)
                            g.dump("d_attnT", attnT[:, 0], Bat, [128, 128], BF16)
                            g.dump("d_kbg", kbg[:, 0], Bkbg, [128, 128], BF16)
                            g.dump("d_qgT", qgT[:, 0], Bqg, [128, 128], BF16)
                            g.dump("d_kdec", kdec[:, 0], Bkd, [128, 128], BF16)
                        for c in range(NCH):
                            pw, bpw = g.psf()
                            op("pe", lambda e: e.matmul(pw[:, 0:128], lhsT=kbg[:, c, :], rhs=RT[:, c, :], start=True, stop=True), reads=[Bkbg, BRT], writes=[bpw])
                            op("act", lambda e: e.activation(out=nwTa[:, c, :], in_=pw[:, 0:128], func=AF.Identity, scale=-1.0), reads=[bpw], writes=[BnwT])
                        for si, (sqi, d_, order, s_init, s_final) in enumerate(scans):
                            S, BS = Ss[si]
                            Sb, BSb = Sbs[si]
                            if s_init:
                                if SEQS[sqi][2]:
                                    k.dma("sp", S[:], dr["state_dn"][j, d_, h], writes=[BS], stream="st")
                                else:
                                    op("pool", lambda e: e.memset(S[:], 0.0), writes=[BS])
                                op("act", lambda e: e.activation(out=Sb[:], in_=S[:], func=AF.Copy), reads=[BS], writes=[BSb])
                        for st_ in range(max(len(sc[2]) for sc in scans)):
                            for si, (sqi, d_, order, s_init, s_final) in enumerate(scans):
                                if st_ >= len(order):
                                    continue
                                S, BS = Ss[si]
                                Sb, BSb = Sbs[si]
                                vnew, Bvn = vnews[si]
                                c = order[st_]
                                tt = chains[c][0]
                                tsl = slice(tt * 128, (tt + 1) * 128)
                                pv_, bpv = g.psf()
                                op("pe", lambda e: e.matmul(pv_[:, 0:128], lhsT=RT[:, c, :], rhs=vb[:, c, :], start=True, stop=False), reads=[BRT, Bvb], writes=[bpv])
                                op("pe", lambda e: e.matmul(pv_[:, 0:128], lhsT=nwTa[:, c, :], rhs=Sb[:], start=False, stop=True), reads=[BnwT, BSb], writes=[bpv])
                                op("dve", lambda e: e.tensor_copy(out=vnew[:], in_=pv_[:, 0:128]), reads=[bpv], writes=[Bvn])
                                pS, bpS = g.psf()
                                op("pe", lambda e: e.matmul(pS[:, 0:128], lhsT=kdec[:, c, :], rhs=vnew[:], start=True, stop=True), reads=[Bkd, Bvn], writes=[bpS])
                                po_, bpo = g.psf()
                                op("pe", lambda e: e.matmul(po_[:, 0:128], lhsT=Sb[:], rhs=qgT[:, c, :], start=True, stop=False), reads=[BSb, Bqg], writes=[bpo])
                                op("pe", lambda e: e.matmul(po_[:, 0:128], lhsT=vnew[:], rhs=attnT[:, c, :], start=False, stop=True), reads=[Bvn, Bat], writes=[bpo])
                                op("dve", lambda e: e.scalar_tensor_tensor(out=Sb[:], in0=S[:], scalar=glb[:, c:c + 1], in1=pS[:, 0:128], op0=ALU.mult, op1=ALU.add), reads=[BS, Bglb, bpS], writes=[BSb])
                                op("dve", lambda e: e.scalar_tensor_tensor(out=S[:], in0=S[:], scalar=glb[:, c:c + 1], in1=pS[:, 0:128], op0=ALU.mult, op1=ALU.add), reads=[BS, Bglb, bpS], writes=[BS])
                                if d_ == 0:
                                    op("act", lambda e: e.activation(out=oacc[:, tsl], in_=po_[:, 0:128], func=AF.Copy), reads=[bpo], writes=[Bo])
                                else:
                                    op("dve", lambda e: e.tensor_tensor(out=oacc[:, tsl], in0=po_[:, 0:128], in1=oacc[:, tsl], op=ALU.add), reads=[bpo, Bo], writes=[Bo])
                        for si, (sqi, d_, order, s_init, s_final) in enumerate(scans):
                            if s_final and not SEQS[sqi][2]:
                                k.dma("sp", dr["st_out"][sqi, j, d_, h], Ss[si][0][:], reads=[Ss[si][1]], stream="sto")
                        k.barrier()
                if h == 0:
                    g.dump("d_oacc", oacc[:], Bo, [128, T])
                for tg in range(3):
                    ts = slice(tg * 512, (tg + 1) * 512)
                    sq, bsq = sqb[tg % 2]
                    op("act", lambda e: e.activation(out=sq[:], in_=oacc[:, ts], func=AF.Square), reads=[Bo], writes=[bsq])
                    p2, bp2 = g.psf()
                    op("pe", lambda e: e.matmul(p2[:], lhsT=ones_b[:], rhs=sq[:], start=True, stop=True), reads=[bsq, CB], writes=[bp2])
                    rs_, brs_ = tmpf[tg % 3]
                    ss_to_rstd(p2[:], rs_[:], bp2, brs_, 128, EPS)
                    op("dve", lambda e: e.scalar_tensor_tensor(out=oacc[:, ts], in0=oacc[:, ts], scalar=dnn[:, 0:1], in1=rs_[:], op0=ALU.mult, op1=ALU.mult), reads=[Bo, Bdnn, brs_], writes=[Bo])
                    op("pool", lambda e: e.tensor_tensor(out=mixT[:, h, ts], in0=oacc[:, ts], in1=zs[:, ts], op=ALU.mult), reads=[Bo, Bzs], writes=[Bmix])
                k.barrier()
        HYENA(g, L, i)
        k.barrier()


def HYENA(g, L, i):
    k, dr, op = L["k"], L["dr"], L["op"]
    hT, Bh, mixT, Bmix, CB = L["hT"], L["Bh"], L["mixT"], L["Bmix"], L["CB"]
    ident_b = L["ident_b"]
    j = i // 2
    W = dr["ev_w_in"][j]
    CW = 256
    I32 = mybir.dt.int32
    with contextlib.ExitStack() as es:
        def sb(shape, dt, nm, es_=None):
            return g.sb(shape, dt, nm, es_ or es)
        cw, Bcw = sb([128, 3, 12], F32, "hycw")
        cbv, Bcbv = sb([128, 12], F32, "hycb")
        w1, Bw1 = sb([33, 64], F32, "hyw1")
        w2, Bw2 = sb([64, 64], F32, "hyw2")
        fr, Bfr = sb([64, 4], F32, "hyfr")
        k.dma("sp", cw[:], dr["hy_conv_w_fm"][j], writes=[Bcw], stream="hc")
        k.dma("sp", cbv[:], dr["hy_conv_b_fm"][j], writes=[Bcbv], stream="hc")
        k.dma("sp", w1[:], dr["hy_w1"][j], writes=[Bw1], stream="hc")
        k.dma("sp", w2[:], dr["hy_w2"][j], writes=[Bw2], stream="hc")
        k.dma("sp", fr[:, 0:1], dr["hy_freq1_fm"][j], writes=[Bfr], stream="hc")
        k.dma("sp", fr[:, 1:2], dr["hy_b1_fm"][j], writes=[Bfr], stream="hc")
        k.dma("sp", fr[:, 2:3], dr["hy_freq2_fm"][j], writes=[Bfr], stream="hc")
        k.dma("sp", fr[:, 3:4], dr["hy_b2_fm"][j], writes=[Bfr], stream="hc")
        op("dve", lambda e: e.tensor_tensor(out=fr[:, 1:2], in0=fr[:, 1:2], in1=fr[:, 0:1], op=ALU.mult), reads=[Bfr], writes=[Bfr])
        op("dve", lambda e: e.tensor_tensor(out=fr[:, 3:4], in0=fr[:, 3:4], in1=fr[:, 2:3], op=ALU.mult), reads=[Bfr], writes=[Bfr])

        for ch in range(512 // CW):
            with contextlib.ExitStack() as ec:
                x2T, Bx2 = sb([128, 2, T], BF16, "x2T", ec)
                ztok, Bzt = sb([128, 12, CW], BF16, "ztok", ec)
                x1tok, Bx1t = sb([128, 12, CW], BF16, "x1tok", ec)
                zm2, Bzm2 = sb([64, 1024], F32, "zm2", ec)
                with contextlib.ExitStack() as e1:
                    raw, Braw = sb([128, T], F32, "hraw", e1)
                    cv, Bcv = sb([128, T], F32, "hcv", e1)
                    cvb, Bcvb = sb([128, T], BF16, "hcvb", e1)
                    for comp in range(3):
                        wv, bw = g.wload(W, 2064 + comp * 512 + ch * CW, CW, 8)
                        for c2 in range(CW // 128):
                            cc = comp * 4 + ch * (CW // 128) + c2
                            for tg in range(3):
                                ts = slice(tg * 512, (tg + 1) * 512)
                                pp, bpp = g.psf()
                                for kc in range(8):
                                    op("pe", lambda e: e.matmul(pp[:], lhsT=wv[:, kc, c2 * 128:(c2 + 1) * 128], rhs=hT[:, kc, ts], start=(kc == 0), stop=(kc == 7)), reads=[bw, Bh], writes=[bpp])
                                op("act", lambda e: e.activation(out=raw[:, ts], in_=pp[:], func=AF.Copy), reads=[bpp], writes=[Braw])
                            for (s0, ln, _) in SEQS:
                                op("dve", lambda e: e.tensor_scalar(out=cv[:, s0:s0 + ln], in0=raw[:, s0:s0 + ln], scalar1=cw[:, 1, cc:cc + 1], scalar2=cbv[:, cc:cc + 1], op0=ALU.mult, op1=ALU.add), reads=[Braw, Bcw, Bcbv], writes=[Bcv])
                                op("dve", lambda e: e.scalar_tensor_tensor(out=cv[:, s0 + 1:s0 + ln], in0=raw[:, s0:s0 + ln - 1], scalar=cw[:, 0, cc:cc + 1], in1=cv[:, s0 + 1:s0 + ln], op0=ALU.mult, op1=ALU.add), reads=[Braw, Bcw, Bcv], writes=[Bcv])
                                op("dve", lambda e: e.scalar_tensor_tensor(out=cv[:, s0:s0 + ln - 1], in0=raw[:, s0 + 1:s0 + ln], scalar=cw[:, 2, cc:cc + 1], in1=cv[:, s0:s0 + ln - 1], op0=ALU.mult, op1=ALU.add), reads=[Braw, Bcw, Bcv], writes=[Bcv])
                            if comp == 1:
                                op("act", lambda e: e.activation(out=x2T[:, c2, :], in_=cv[:], func=AF.Copy), reads=[Bcv], writes=[Bx2])
                            else:
                                dst, bdst = (x1tok, Bx1t) if comp == 0 else (ztok, Bzt)
                                op("act", lambda e: e.activation(out=cvb[:], in_=cv[:], func=AF.Copy), reads=[Bcv], writes=[Bcvb])
                                for t4 in range(3):
                                    ptb, bptb = g.psb()
                                    for q4 in range(4):
                                        tt = t4 * 4 + q4
                                        op("pe", lambda e: e.transpose(out=ptb[:, q4 * 128:(q4 + 1) * 128], in_=cvb[:, tt * 128:(tt + 1) * 128], identity=ident_b[:]), reads=[Bcvb, CB], writes=[bptb])
                                    op("dve", lambda e: e.tensor_copy(out=dst[:, t4 * 4:t4 * 4 + 4, c2 * 128:(c2 + 1) * 128], in_=ptb[:, 0:512].rearrange("p (a b) -> p a b", a=4)), reads=[bptb], writes=[bdst])
                    k.barrier()

                for (Ls, seqs) in ((256, [SEQS[0], SEQS[1]]), (1024, [SEQS[2]])):
                    nt = Ls // 128
                    Fd, Fi = dr[f"dft_f{Ls}"], dr[f"dft_i{Ls}"]
                    with contextlib.ExitStack() as eL:
                        with contextlib.ExitStack() as em:
                            ft, Bft = sb([33, Ls], F32, "feats", em)
                            k.dma("sp", ft[:], dr[f"featsT{Ls}"], writes=[Bft], stream="hc")
                            zmlp, Bzm = sb([64, Ls], F32, "zmlp", em)
                            ti, Bti = sb([64, Ls], I32, "ti", em)
                            tf, Btf = sb([64, Ls], F32, "tf", em)

                            def sin_layer(dst, bdst, wmat, bwm, src, bsrc, kdim, fcol):
                                for c0 in range(0, Ls, 512):
                                    n = min(512, Ls - c0)
                                    pp, bpp = g.psf()
                                    op("pe", lambda e: e.matmul(pp[0:64, 0:n], lhsT=wmat[0:kdim, :], rhs=src[0:kdim, c0:c0 + n], start=True, stop=True), reads=[bwm, bsrc], writes=[bpp])
                                    op("dve", lambda e: e.tensor_scalar(out=dst[:, c0:c0 + n], in0=pp[0:64, 0:n], scalar1=fr[:, fcol:fcol + 1], scalar2=fr[:, fcol + 1:fcol + 2], op0=ALU.mult, op1=ALU.add), reads=[bpp, Bfr], writes=[bdst])
                                d_ = dst[:, 0:Ls]
                                op("dve", lambda e: e.tensor_scalar(out=d_, in0=d_, scalar1=1.0 / (2 * math.pi), scalar2=None, op0=ALU.mult), reads=[bdst], writes=[bdst])
                                op("dve", lambda e: e.tensor_copy(out=ti[:], in_=d_), reads=[bdst], writes=[Bti])
                                op("dve", lambda e: e.tensor_copy(out=tf[:], in_=ti[:]), reads=[Bti], writes=[Btf])
                                op("dve", lambda e: e.tensor_tensor(out=d_, in0=d_, in1=tf[:], op=ALU.subtract), reads=[bdst, Btf], writes=[bdst])
                                op("dve", lambda e: e.tensor_single_scalar(out=tf[:], in_=d_, scalar=0.5, op=ALU.is_gt), reads=[bdst], writes=[Btf])
                                op("dve", lambda e: e.tensor_tensor(out=d_, in0=d_, in1=tf[:], op=ALU.subtract), reads=[bdst, Btf], writes=[bdst])
                                op("dve", lambda e: e.tensor_single_scalar(out=tf[:], in_=d_, scalar=-0.5, op=ALU.is_lt), reads=[bdst], writes=[Btf])
                                op("dve", lambda e: e.tensor_tensor(out=d_, in0=d_, in1=tf[:], op=ALU.add), reads=[bdst, Btf], writes=[bdst])
                                op("act", lambda e: e.activation(out=d_, in_=d_, func=AF.Sin, scale=2 * math.pi), reads=[bdst], writes=[bdst])
                            sin_layer(zmlp, Bzm, w1, Bw1, ft, Bft, 33, 0)
                            sin_layer(zm2, Bzm2, w2, Bw2, zmlp, Bzm, 64, 2)
                            k.barrier()
                        z2tok, Bz2 = sb([128, len(seqs), nt, CW], BF16, "z2tok", eL)
                        Hc, BHc = sb([128, nt, CW], F32, "Hc", eL)
                        Hs, BHs = sb([128, nt, CW], F32, "Hs", eL)
                        brow, Bbrow = sb([128, CW], F32, "brow", eL)
                        w3s, Bw3 = sb([64, 2, CW], F32, "w3s", eL)
                        tq = [sb([128, CW], F32, "tq", eL) for _ in range(4)]

                        def fslabs(grp):
                            if Ls == 256:
                                v_, b_ = g.wload(Fd, 0, 512, nt)
                                return (v_, b_, 0), (v_, b_, 256)
                            vc, bc = g.wload(Fd, grp * 512, 512, nt)
                            vs, bs = g.wload(Fd, Ls + grp * 512, 512, nt)
                            return (vc, bc, 0), (vs, bs, 0)

                        for o in range(2):
                            k.dma("sp", brow[:], dr["hy_bias_bc"][:, j, o, ch * CW:(ch + 1) * CW], writes=[Bbrow], stream="hc")
                            for sd in range(2):
                                c0 = o * 1024 + sd * 512 + ch * CW
                                k.dma("sp", w3s[:, sd, :], dr["hy_w3"][j][:, c0:c0 + CW], writes=[Bw3], stream="hc")
                            with contextlib.ExitStack() as eS:
                                Sd, BSd = sb([128, nt, CW], BF16, "Sd", eS)
                                Dd, BDd = sb([128, nt, CW], BF16, "Dd", eS)
                                for tt in range(nt):
                                    pp, bpp = g.psf()
                                    for sd in range(2):
                                        op("pe", lambda e: e.matmul(pp[:, sd * CW:(sd + 1) * CW], lhsT=zm2[:, tt * 128:(tt + 1) * 128], rhs=w3s[:, sd, :], start=True, stop=True), reads=[Bzm2, Bw3], writes=[bpp])
                                    dec, bdec = tq[tt % 2]
                                    k.dma("sp", dec[:], dr[f"hydec{Ls}"][tt * 128:(tt + 1) * 128, ch * CW:(ch + 1) * CW], writes=[bdec], stream="hd%d" % (tt % 2))
                                    tfw, btfw = tq[2]
                                    tbw, btbw = tq[3]
                                    op("dve", lambda e: e.tensor_tensor(out=tfw[:], in0=pp[:, 0:CW], in1=dec[:], op=ALU.mult), reads=[bpp, bdec], writes=[btfw])
                                    op("dve", lambda e: e.tensor_tensor(out=tbw[:], in0=pp[:, CW:2 * CW], in1=dec[:], op=ALU.mult), reads=[bpp, bdec], writes=[btbw])
                                    if tt == 0:
                                        op("dve", lambda e: e.memset(tbw[0:1, :], 0.0), reads=[btbw], writes=[btbw])
                                    op("pool", lambda e: e.tensor_tensor(out=Sd[:, tt, :], in0=tfw[:], in1=tbw[:], op=ALU.add), reads=[btfw, btbw], writes=[BSd])
                                    op("pool", lambda e: e.tensor_tensor(out=Dd[:, tt, :], in0=tfw[:], in1=tbw[:], op=ALU.subtract), reads=[btfw, btbw], writes=[BDd])
                                for grp in range(max(1, nt // 4)):
                                    (vc, bc, oc), (vs, bs, os_) = fslabs(grp)
                                    for f4 in range(min(4, nt)):
                                        fc = grp * 4 + f4
                                        pc, bpc = g.psf()
                                        for tc in range(nt):
                                            op("pe", lambda e: e.matmul(pc[:, 0:CW], lhsT=vc[:, tc, oc + f4 * 128:oc + (f4 + 1) * 128], rhs=Sd[:, tc, :], start=(tc == 0), stop=(tc == nt - 1)), reads=[bc, BSd], writes=[bpc])
                                        op("dve", lambda e: e.tensor_tensor(out=Hc[:, fc, :], in0=pc[:, 0:CW], in1=brow[:], op=ALU.add), reads=[bpc, Bbrow], writes=[BHc])
                                        ps_, bps = g.psf()
                                        for tc in range(nt):
                                            op("pe", lambda e: e.matmul(ps_[:, 0:CW], lhsT=vs[:, tc, os_ + f4 * 128:os_ + (f4 + 1) * 128], rhs=Dd[:, tc, :], start=(tc == 0), stop=(tc == nt - 1)), reads=[bs, BDd], writes=[bps])
                                        op("act", lambda e: e.activation(out=Hs[:, fc, :], in_=ps_[:, 0:CW], func=AF.Copy), reads=[bps], writes=[BHs])
                                        if fc == 0:
                                            pn, bpn = g.psf()
                                            for tc in range(nt):
                                                op("pe", lambda e: e.matmul(pn[0:1, 0:CW], lhsT=vs[:, tc, os_:os_ + 1], rhs=Sd[:, tc, :], start=(tc == 0), stop=(tc == nt - 1)), reads=[bs, BSd], writes=[bpn])
                                            op("dve", lambda e: e.tensor_tensor(out=Hs[0:1, 0, :], in0=pn[0:1, 0:CW], in1=brow[0:1, :], op=ALU.add), reads=[bpn, Bbrow, BHs], writes=[BHs])
                                k.barrier()
                            eY = contextlib.ExitStack()
                            Y, BY = sb([128, 2 * nt, CW], BF16, "Y", eY)
                            for si, (s0, ln, smp) in enumerate(seqs):
                                t0 = s0 // 128
                                for grp in range(max(1, nt // 4)):
                                    (vc, bc, oc), (vs, bs, os_) = fslabs(grp)
                                    for f4 in range(min(4, nt)):
                                        fc = grp * 4 + f4
                                        pc, bpc = g.psf()
                                        ps_, bps = g.psf()
                                        for tc in range(nt):
                                            rhs = ztok[:, t0 + tc, :] if o == 0 else z2tok[:, si, tc, :]
                                            brhs = Bzt if o == 0 else Bz2
                                            op("pe", lambda e: e.matmul(pc[:, 0:CW], lhsT=vc[:, tc, oc + f4 * 128:oc + (f4 + 1) * 128], rhs=rhs, start=(tc == 0), stop=(tc == nt - 1)), reads=[bc, brhs], writes=[bpc])
                                        for tc in range(nt):
                                            rhs = ztok[:, t0 + tc, :] if o == 0 else z2tok[:, si, tc, :]
                                            brhs = Bzt if o == 0 else Bz2
                                            op("pe", lambda e: e.matmul(ps_[:, 0:CW], lhsT=vs[:, tc, os_ + f4 * 128:os_ + (f4 + 1) * 128], rhs=rhs, start=(tc == 0), stop=(tc == nt - 1)), reads=[bs, brhs], writes=[bps])
                                        (a1, ba1), (a2, ba2), (a3, ba3), (a4, ba4) = tq
                                        op("dve", lambda e: e.tensor_tensor(out=a1[:], in0=pc[:, 0:CW], in1=Hc[:, fc, :], op=ALU.mult), reads=[bpc, BHc], writes=[ba1])
                                        op("dve", lambda e: e.tensor_tensor(out=a3[:], in0=pc[:, 0:CW], in1=Hs[:, fc, :], op=ALU.mult), reads=[bpc, BHs], writes=[ba3])
                                        op("dve", lambda e: e.tensor_tensor(out=a2[:], in0=ps_[:, 0:CW], in1=Hs[:, fc, :], op=ALU.mult), reads=[bps, BHs], writes=[ba2])
                                        op("dve", lambda e: e.tensor_tensor(out=a4[:], in0=ps_[:, 0:CW], in1=Hc[:, fc, :], op=ALU.mult), reads=[bps, BHc], writes=[ba4])
                                        op("pool", lambda e: e.tensor_tensor(out=Y[:, fc, :], in0=a1[:], in1=a2[:], op=ALU.subtract), reads=[ba1, ba2], writes=[BY])
                                        op("pool", lambda e: e.tensor_tensor(out=Y[:, nt + fc, :], in0=a3[:], in1=a4[:], op=ALU.add), reads=[ba3, ba4], writes=[BY])
                                        if fc == 0:
                                            op("pool", lambda e: e.tensor_copy(out=Y[0:1, 0, :], in_=a1[0:1, :]), reads=[ba1, BY], writes=[BY])
                                            op("pool", lambda e: e.tensor_copy(out=Y[0:1, nt, :], in_=a2[0:1, :]), reads=[ba2, BY], writes=[BY])
                                ncol = 256 if Ls == 1024 else 256
                                for c0 in range(0, Ls, ncol):
                                    vi, bi = g.wload(Fi, c0, ncol, 2 * nt)
                                    if o == 0:
                                        for t2 in range(ncol // 128):
                                            tl = c0 // 128 + t2
                                            pp, bpp = g.psf()
                                            for fc in range(2 * nt):
                                                op("pe", lambda e: e.matmul(pp[:, 0:CW], lhsT=vi[:, fc, t2 * 128:(t2 + 1) * 128], rhs=Y[:, fc, :], start=(fc == 0), stop=(fc == 2 * nt - 1)), reads=[bi, BY], writes=[bpp])
                                            op("dve", lambda e: e.tensor_tensor(out=z2tok[:, si, tl, :], in0=pp[:, 0:CW], in1=x1tok[:, t0 + tl, :], op=ALU.mult), reads=[bpp, Bx1t], writes=[Bz2])
                                    else:
                                        for c2 in range(CW // 128):
                                            pp, bpp = g.psf()
                                            for fc in range(2 * nt):
                                                op("pe", lambda e: e.matmul(pp[:, 0:ncol], lhsT=Y[:, fc, c2 * 128:(c2 + 1) * 128], rhs=vi[:, fc, :], start=(fc == 0), stop=(fc == 2 * nt - 1)), reads=[bi, BY], writes=[bpp])
                                            op("dve", lambda e: e.tensor_tensor(out=mixT[:, 4 + ch * (CW // 128) + c2, s0 + c0:s0 + c0 + ncol], in0=pp[:, 0:ncol], in1=x2T[:, c2, s0 + c0:s0 + c0 + ncol], op=ALU.mult), reads=[bpp, Bx2], writes=[Bmix])
                            k.barrier()
                            eY.close()
                        k.barrier()
                k.barrier()
        k.barrier()


def ODD_MIXER(g, L, i):
    k, dr, op = L["k"], L["dr"], L["op"]
    hT, Bh, mixT, Bmix, CB = L["hT"], L["Bh"], L["mixT"], L["Bmix"], L["CB"]
    ones_b, ident_b = L["ones_b"], L["ident_b"]
    ss_to_rstd = L["ss_to_rstd"]
    j = i // 2
    W = dr["od_w_in"][j]
    scale = 128.0 ** -0.5
    with contextlib.ExitStack() as es:
        def sb(shape, dt, nm):
            return g.sb(shape, dt, nm, es)
        qT, Bq = sb([128, 4, T], BF16, "qT")
        kT, Bk = sb([128, 2, T], BF16, "kT")
        vtok, Bv = sb([128, 12, 2, 128], BF16, "vtok")
        ckT, Bck = sb([128, 2, 256], BF16, "ckT")
        cktok, Bckt = sb([128, 2, 2, 128], BF16, "cktok")
        cvtok, Bcv = sb([128, 2, 2, 128], BF16, "cvtok")
        ropec, Brc = sb([128, 1024], F32, "ropec")
        ropes, Brs = sb([128, 1024], F32, "ropes")
        k.dma("sp", ropec[:], dr["rope_c"], writes=[Brc])
        k.dma("sp", ropes[:], dr["rope_s"], writes=[Brs])
        rrm, Brm = sb([128, 128], BF16, "rrm")
        band, Bband = sb([128, 6, 512], BF16, "band")
        gq, Bgq = sb([128, 2], F32, "gq")
        grow, Bgrow = sb([128, 128], F32, "grow")
        sinkc, Bsink = sb([128, 4], F32, "sinkc")
        qf = [L["tmpf"][0], L["tmpf"][1]]
        qb = [sb([128, 512], BF16, "qb") for _ in range(2)]
        sqs = L["sqb"]
        t1 = [sb([128, 512], F32, "t1") for _ in range(2)]
        rs_, Brs_ = L["tmpf"][2]
        ebuf = [sb([128, 512], BF16, "ebuf") for _ in range(3)]
        rinv, Brinv = sb([128, 512], F32, "rinv")
        nq, Bnq = sb([128, 4, 3], F32, "nq")
        nk, Bnk = sb([128, 2, 4], F32, "nk")
        negM, BnegM = sb([128, 2, 4], F32, "negM")
        esink, Bes = sb([128, 2, 4], F32, "esink")
        kvf, Bkvf = sb([128, 512], F32, "kvf")
        kvo, Bkvo = sb([128, 2, 128], F32, "kvo")
        ssk, Bssk = sb([128, 2], F32, "ssk")
        sqk, Bsqk = sb([128, 2, 128], F32, "sqk")
        cnt = {"q": 0, "e": 0}

        k.dma("sp", rrm[:], dr["rope_rm"], writes=[Brm], stream="oc")
        k.dma("sp", band[:], dr["band"].rearrange("r s q -> s r q"), writes=[Bband], stream="oc")
        k.dma("sp", gq[:, 0:1], dr["c_q_norm_fm"][j], writes=[Bgq], stream="oc")
        k.dma("sp", gq[:, 1:2], dr["c_k_norm_fm"][j], writes=[Bgq], stream="oc")
        k.dma("sp", grow[:], dr["c_k_norm_row"][:, j, :], writes=[Bgrow], stream="oc")
        k.dma("sp", sinkc[:], dr["d_sink_bc"][:, j, :], writes=[Bsink], stream="oc")
        op("dve", lambda e: e.tensor_scalar(out=gq[:], in0=gq[:], scalar1=math.sqrt(128.0), scalar2=None, op0=ALU.mult), reads=[Bgq], writes=[Bgq])

        for grp in range(2):
            qc0 = grp * 1024
            kc0 = grp * 1024 + 512
            wq, bwq = g.wload(W, qc0, 512, 8)
            wk, bwk = g.wload(W, kc0, 256, 8)
            def stage0(ci, tg):
                isq = ci < 4
                wv, bw, cc = (wq, bwq, ci) if isq else (wk, bwk, ci - 4)
                dst, bdst, dc = (qT, Bq, ci) if isq else (kT, Bk, ci - 4)
                ts = slice(tg * 512, (tg + 1) * 512)
                pp, bpp = g.psf()
                for kc in range(8):
                    op("pe", lambda e: e.matmul(pp[:], lhsT=wv[:, kc, cc * 128:(cc + 1) * 128], rhs=hT[:, kc, ts], start=(kc == 0), stop=(kc == 7)), reads=[bw, Bh], writes=[bpp])
                cnt["q"] += 1
                par = cnt["q"]
                f_, bf_ = qf[par % 2]
                if grp == 1 and tg == 0:
                    op("act", lambda e: e.activation(out=dst[:, dc, ts], in_=pp[:], func=AF.Copy), reads=[bpp], writes=[bdst])
                else:
                    op("act", lambda e: e.activation(out=f_[:], in_=pp[:], func=AF.Copy), reads=[bpp], writes=[bf_])
                return (isq, dst, bdst, dc, ts, par, f_, bf_)

            def rest(ci, tg, ctx):
                isq, dst, bdst, dc, ts, par, f_, bf_ = ctx
                if grp == 0:
                    sq, bsq = sqs[par % 2]
                    op("act", lambda e: e.activation(out=sq[:], in_=f_[:], func=AF.Square), reads=[bf_], writes=[bsq])
                    p2, bp2 = g.psf()
                    op("pe", lambda e: e.matmul(p2[:], lhsT=ones_b[:], rhs=sq[:], start=True, stop=True), reads=[bsq, CB], writes=[bp2])
                    ss_to_rstd(p2[:], rs_[:], bp2, Brs_, 128, EPS)
                    gcol = gq[:, 0:1] if isq else gq[:, 1:2]
                    if tg == 0:
                        op("dve", lambda e: e.scalar_tensor_tensor(out=dst[:, dc, ts], in0=f_[:], scalar=gcol, in1=rs_[:], op0=ALU.mult, op1=ALU.mult), reads=[bf_, Bgq, Brs_], writes=[bdst])
                    else:
                        op("dve", lambda e: e.scalar_tensor_tensor(out=f_[:], in0=f_[:], scalar=gcol, in1=rs_[:], op0=ALU.mult, op1=ALU.mult), reads=[bf_, Bgq, Brs_], writes=[bf_])
                if tg > 0:
                    ps_ = slice((tg - 1) * 512, tg * 512)
                    b_, bb_ = qb[par % 2]
                    op("act", lambda e: e.activation(out=b_[:], in_=f_[:], func=AF.Copy), reads=[bf_], writes=[bb_])
                    p3, bp3 = g.psf()
                    op("pe", lambda e: e.matmul(p3[:], lhsT=rrm[:], rhs=b_[:], start=True, stop=True), reads=[Brm, bb_], writes=[bp3])
                    t_, bt_ = t1[par % 2]
                    op("dve", lambda e: e.tensor_tensor(out=t_[:], in0=f_[:], in1=ropec[:, ps_], op=ALU.mult), reads=[bf_, Brc], writes=[bt_])
                    op("dve", lambda e: e.tensor_tensor(out=f_[:], in0=p3[:], in1=ropes[:, ps_], op=ALU.mult), reads=[bp3, Brs, bf_], writes=[bf_])
                    op("dve", lambda e: e.tensor_tensor(out=dst[:, dc, ts], in0=f_[:], in1=t_[:], op=ALU.add), reads=[bf_, bt_], writes=[bdst])
                sq, bsq = sqs[(par + 1) % 2]
                op("act", lambda e: e.activation(out=sq[:], in_=dst[:, dc, ts], func=AF.Square), reads=[bdst], writes=[bsq])
                p4, bp4 = g.psf()
                op("pe", lambda e: e.matmul(p4[:], lhsT=ones_b[:], rhs=sq[:], start=True, stop=True), reads=[bsq, CB], writes=[bp4])
                if isq:
                    op("dve", lambda e: e.tensor_reduce(out=nq[:, dc, tg:tg + 1], in_=p4[:], axis=AX.X, op=ALU.max), reads=[bp4], writes=[Bnq])
                else:
                    op("dve", lambda e: e.tensor_reduce(out=nk[:, dc, tg:tg + 1], in_=p4[:], axis=AX.X, op=ALU.max), reads=[bp4], writes=[Bnk])

            items = [(ci, tg) for ci in range(6) for tg in range(3)]
            ctx = stage0(*items[0])
            for n_, it in enumerate(items):
                nctx = stage0(*items[n_ + 1]) if n_ + 1 < len(items) else None
                rest(it[0], it[1], ctx)
                ctx = nctx
            if DBG.get("odd_stop") == "A":
                continue
            wkv, bwkv = g.wload(W, kc0, 512, 8)
            kname, vname = ("kc_out", "vc_out") if grp == 0 else ("kd_out", "vd_out")
            for tt in range(12):
                pp, bpp = g.psf()
                for kc in range(8):
                    op("pe", lambda e: e.matmul(pp[:], lhsT=hT[:, kc, tt * 128:(tt + 1) * 128], rhs=wkv[:, kc, :], start=(kc == 0), stop=(kc == 7)), reads=[bwkv, Bh], writes=[bpp])
                op("act", lambda e: e.activation(out=vtok[:, tt], in_=pp[:, 256:512].rearrange("p (a b) -> p a b", a=2), func=AF.Copy), reads=[bpp], writes=[Bv])
                if tt < 4 and DBG.get("odd_stop") != "B1":
                    sq_, tb = tt // 2, tt % 2
                    op("dve", lambda e: e.tensor_copy(out=kvf[:], in_=pp[:]), reads=[bpp], writes=[Bkvf])
                    if DBG.get("odd_stop") == "B2":
                        continue
                    k.dma("sp", dr[vname][sq_, j, tb * 128:(tb + 1) * 128], kvf[:, 256:512].rearrange("p (a b) -> p a b", a=2), reads=[Bkvf], stream="kvo")
                    if grp == 1:
                        k.dma("sp", dr[kname][sq_, j, tb * 128:(tb + 1) * 128], kvf[:, 0:256].rearrange("p (a b) -> p a b", a=2), reads=[Bkvf], stream="kvo")
                    else:
                        kv3 = kvf[:, 0:256].rearrange("p (a b) -> p a b", a=2)
                        op("pool", lambda e: e.tensor_tensor(out=sqk[:], in0=kv3, in1=kv3, op=ALU.mult), reads=[Bkvf], writes=[Bsqk])
                        op("dve", lambda e: e.tensor_reduce(out=ssk[:], in_=sqk[:], axis=AX.X, op=ALU.add), reads=[Bsqk], writes=[Bssk])
                        ss_to_rstd(ssk[:], ssk[:], Bssk, Bssk, 1, EPS, scale=1.0 / 128.0)
                        for kv in range(2):
                            op("dve", lambda e: e.scalar_tensor_tensor(out=kvo[:, kv, :], in0=kvf[:, kv * 128:(kv + 1) * 128], scalar=ssk[:, kv:kv + 1], in1=grow[:], op0=ALU.mult, op1=ALU.mult), reads=[Bkvf, Bssk, Bgrow], writes=[Bkvo])
                        k.dma("sp", dr[kname][sq_, j, tb * 128:(tb + 1) * 128], kvo[:], reads=[Bkvo], stream="kvo")
            if DBG.get("odd_stop") in ("B", "B1", "B2"):
                continue
            ckn, cvn = ("cache_k_c", "cache_v_c") if grp == 0 else ("cache_k_d", "cache_v_d")
            k.dma("pool", cktok[:], dr[ckn][j].rearrange("(sb p) k d -> p sb k d", p=128), writes=[Bckt], stream="cch")
            k.dma("pool", cvtok[:], dr[cvn][j].rearrange("(sb p) k d -> p sb k d", p=128), writes=[Bcv], stream="cch")
            ptb, bptb = g.psb()
            for kv in range(2):
                for sbk in range(2):
                    o_ = (kv * 2 + sbk) * 128
                    op("pe", lambda e: e.transpose(out=ptb[:, o_:o_ + 128], in_=cktok[:, sbk, kv, :], identity=ident_b[:]), reads=[Bckt, CB], writes=[bptb])
            op("dve", lambda e: e.tensor_copy(out=ckT[:].rearrange("p a b -> p (a b)"), in_=ptb[:, 0:512]), reads=[bptb], writes=[Bck])
            for kv in range(2):
                sq, bsq = sqs[kv]
                op("act", lambda e: e.activation(out=sq[:, 0:256], in_=ckT[:, kv, :], func=AF.Square), reads=[Bck], writes=[bsq])
                p4, bp4 = g.psf()
                op("pe", lambda e: e.matmul(p4[:, 0:256], lhsT=ones_b[:], rhs=sq[:, 0:256], start=True, stop=True), reads=[bsq, CB], writes=[bp4])
                op("dve", lambda e: e.tensor_reduce(out=nk[:, kv, 3:4], in_=p4[:, 0:256], axis=AX.X, op=ALU.max), reads=[bp4], writes=[Bnk])
            if DBG.get("odd_stop") == "C":
                continue
            op("dve", lambda e: e.tensor_tensor(out=nq[:, :, 1], in0=nq[:, :, 1], in1=nq[:, :, 2], op=ALU.max), reads=[Bnq], writes=[Bnq])
            op("dve", lambda e: e.tensor_tensor(out=nk[:, :, 1], in0=nk[:, :, 1], in1=nk[:, :, 2], op=ALU.max), reads=[Bnk], writes=[Bnk])
            op("dve", lambda e: e.tensor_tensor(out=nk[:, :, 1], in0=nk[:, :, 1], in1=nk[:, :, 3], op=ALU.max), reads=[Bnk], writes=[Bnk])
            for r in range(2):
                for h in range(4):
                    op("dve", lambda e: e.tensor_tensor(out=negM[:, r, h:h + 1], in0=nq[:, h, r:r + 1], in1=nk[:, h // 2, r:r + 1], op=ALU.mult), reads=[Bnq, Bnk], writes=[BnegM])
            op("act", lambda e: e.activation(out=negM[:], in_=negM[:], func=AF.Sqrt), reads=[BnegM], writes=[BnegM])
            op("dve", lambda e: e.tensor_scalar(out=negM[:], in0=negM[:], scalar1=-scale, scalar2=None, op0=ALU.mult), reads=[BnegM], writes=[BnegM])
            if grp == 1:
                for r in range(2):
                    op("dve", lambda e: e.tensor_tensor(out=esink[:, r, :], in0=sinkc[:], in1=negM[:, r, :], op=ALU.add), reads=[Bsink, BnegM], writes=[Bes])
                op("act", lambda e: e.activation(out=esink[:], in_=esink[:], func=AF.Exp), reads=[Bes], writes=[Bes])
            if DBG.get("odd_stop") == "M":
                continue
            for (s0, ln, smp) in SEQS:
                for h in range(4):
                    kvh = h // 2
                    mcol = negM[:, smp, h:h + 1]
                    qgs = [(s0 + a, min(512, ln)) for a in range(0, ln, 512)]
                    for qi, (q0, n) in enumerate(qgs):
                        blocks = []
                        if smp:
                            for sbk in range(2):
                                blocks.append((ckT[:, kvh, sbk * 128:(sbk + 1) * 128], Bck, cvtok[:, sbk, kvh, :], Bcv, None))
                        for kb in range(ln // 128):
                            msk = None
                            if smp and grp == 1:
                                rel = kb - 4 * qi
                                if rel < -1 or rel > 4:
                                    continue
                                msk = rel + 1
                            t0 = s0 + kb * 128
                            blocks.append((kT[:, kvh, t0:t0 + 128], Bk, vtok[:, t0 // 128, kvh, :], Bv, msk))
                        po, bpo = g.psacc[0]
                        pm, bpm = g.psacc[1]

                        def score(bi_):
                            kap_, bk__ = blocks[bi_][0], blocks[bi_][1]
                            pst_, bpst_ = g.psf()
                            op("pe", lambda e: e.matmul(pst_[:, 0:n], lhsT=kap_, rhs=qT[:, h, q0:q0 + n], start=True, stop=True), reads=[bk__, Bq], writes=[bpst_])
                            return pst_, bpst_
                        nxt_score = score(0)
                        for bi, (kap, bk_, vap, bv_, msk) in enumerate(blocks):
                            pst, bpst = nxt_score
                            if bi + 1 < len(blocks):
                                nxt_score = score(bi + 1)
                            cnt["e"] += 1
                            eb, beb = ebuf[cnt["e"] % 3]
                            op("act", lambda e: e.activation(out=eb[:, 0:n], in_=pst[:, 0:n], func=AF.Exp, bias=mcol, scale=scale), reads=[bpst, BnegM], writes=[beb])
                            if msk is not None:
                                op("pool", lambda e: e.tensor_tensor(out=eb[:, 0:n], in0=eb[:, 0:n], in1=band[:, msk, 0:n], op=ALU.mult), reads=[beb, Bband], writes=[beb])
                            first, last = bi == 0, bi == len(blocks) - 1
                            op("pe", lambda e: e.matmul(po[:, 0:n], lhsT=vap, rhs=eb[:, 0:n], start=first, stop=last), reads=[bv_, beb], writes=[bpo])
                            op("pe", lambda e: e.matmul(pm[:, 0:n], lhsT=ones_b[:], rhs=eb[:, 0:n], start=first, stop=last), reads=[CB, beb], writes=[bpm])
                        if grp == 1:
                            op("dve", lambda e: e.tensor_scalar(out=rinv[:, 0:n], in0=pm[:, 0:n], scalar1=esink[:, smp, h:h + 1], scalar2=None, op0=ALU.add), reads=[bpm, Bes], writes=[Brinv])
                            op("dve", lambda e: e.reciprocal(out=rinv[:, 0:n], in_=rinv[:, 0:n]), reads=[Brinv], writes=[Brinv])
                        else:
                            op("dve", lambda e: e.reciprocal(out=rinv[:, 0:n], in_=pm[:, 0:n]), reads=[bpm], writes=[Brinv])
                        op("dve", lambda e: e.tensor_tensor(out=mixT[:, grp * 4 + h, q0:q0 + n], in0=po[:, 0:n], in1=rinv[:, 0:n], op=ALU.mult), reads=[bpo, Brinv], writes=[Bmix])
        k.barrier()


_CONSTS = None
_PROG = {}


def _specs(consts, percore, shared):
    sp = {}
    for nm, a in {**consts, **shared, **percore}.items():
        dt = BF16 if a.dtype == ml_dtypes.bfloat16 else F32
        sp[nm] = (a.shape, dt, "ExternalInput")
    return sp


def host_prepare(inp, core):
    pc = {}
    pc["x_all"] = np.ascontiguousarray(np.concatenate(
        [inp["x_prompt"][2 * core], inp["x_prompt"][2 * core + 1], inp["x_sample"][core]], axis=0))
    cv = np.stack([fm(inp["c_ctx"]), fm(inp["c"][core])], axis=-1)
    pc["cvec"] = np.ascontiguousarray(cv)
    pc["state_dn"] = np.ascontiguousarray(inp["state_dn"][core])
    for nm in ("cache_k_c", "cache_v_c", "cache_k_d", "cache_v_d"):
        pc[nm] = np.ascontiguousarray(inp[nm][core])
    return pc


def host_shared(inp):
    sh = {}
    for nm in ("w_mod", "w_out", "ffn_w_up", "ffn_w_down", "ev_w_in", "od_w_in", "hy_w3", "hy_w1", "hy_w2"):
        sh[nm] = inp[nm]
    sh["norm_mix"] = np.ascontiguousarray(np.stack([fm(inp["norm_mix"][i]) for i in range(DEPTH)], axis=1))
    sh["norm_ffn"] = np.ascontiguousarray(np.stack([fm(inp["norm_ffn"][i]) for i in range(DEPTH)], axis=1))
    sh["final_norm"] = fm(inp["final_norm"])
    bm = np.stack([fm(inp["b_mod"][i]) for i in range(DEPTH)], axis=1)
    sh["b_mod"] = np.ascontiguousarray(np.repeat(bm[..., None], 2, axis=-1))
    cw = np.stack([np.stack([fm(inp["ffn_conv_w"][i, t]) for t in range(3)], axis=1) for i in range(DEPTH)], axis=1)
    sh["ffn_conv_w"] = np.ascontiguousarray(cw)
    sh["c_q_norm_fm"] = np.ascontiguousarray(inp["c_q_norm"][:, :, None])
    sh["c_k_norm_fm"] = np.ascontiguousarray(inp["c_k_norm"][:, :, None])
    sh["c_k_norm_row"] = np.ascontiguousarray(np.broadcast_to(inp["c_k_norm"][None], (128, 2, 128)))
    sh["d_sink_bc"] = np.ascontiguousarray(np.broadcast_to(inp["d_sink"][None], (128, 2, 4)))
    cwd = np.stack([np.stack([fm(inp["dn_conv_w"][jj, t]) for t in range(3)], axis=1) for jj in range(2)], axis=0)
    sh["dn_conv_w_fm"] = np.ascontiguousarray(cwd)
    sh["dn_dt_bias_bc"] = np.ascontiguousarray(np.broadcast_to(inp["dn_dt_bias"].reshape(1, 2, 8), (128, 2, 8)))
    sh["dn_a_log_bc"] = np.ascontiguousarray(np.broadcast_to(inp["dn_a_log"].reshape(1, 2, 8), (128, 2, 8)))
    sh["dn_norm_fm"] = np.ascontiguousarray(inp["dn_norm"][:, :, None])
    cwh = np.stack([np.stack([fm(inp["hy_conv_w"][jj, t]) for t in range(3)], axis=1) for jj in range(2)], axis=0)
    sh["hy_conv_w_fm"] = np.ascontiguousarray(cwh)
    sh["hy_conv_b_fm"] = np.ascontiguousarray(np.stack([fm(inp["hy_conv_b"][jj]) for jj in range(2)], axis=0))
    for nm in ("hy_freq1", "hy_b1", "hy_freq2", "hy_b2"):
        sh[nm + "_fm"] = np.ascontiguousarray(inp[nm][:, :, None])
    sh["hy_bias_bc"] = np.ascontiguousarray(np.broadcast_to(inp["hy_bias"][None], (128, 2, 2, 512)))
    sh["ffn_conv_b"] = np.ascontiguousarray(np.stack([fm(inp["ffn_conv_b"][i]) for i in range(DEPTH)], axis=1))
    return sh


def kernel(**inp):
    global _CONSTS
    inp = {k_: np.asarray(v) for k_, v in inp.items()}
    if _CONSTS is None:
        _CONSTS = make_consts()
    consts = _CONSTS
    shared = host_shared(inp)
    per = [host_prepare(inp, c) for c in range(NCORES)]
    extra = DBG.get("extra_inputs")
    if extra:
        for c in range(NCORES):
            per[c].update(extra(c))
    specs = _specs(consts, per[0], shared)
    outs = {
        "y_all": ((T, D), F32, "ExternalOutput"),
    }
    outs["st_out"] = ((2, 2, 2, 4, 128, 128), F32, "ExternalOutput")
    for nm in ("kc_out", "vc_out", "kd_out", "vd_out"):
        outs[nm] = ((2, 2, 256, 2, 128), F32, "ExternalOutput")
    for nm, shp in DBG.get("extra_outputs", {}).items():
        outs[nm] = (shp, F32, "ExternalOutput")
    specs.update(outs)
    nc = build_program(specs)
    in_maps = [{**consts, **shared, **per[c]} for c in range(NCORES)]
    if DBG.get("trace"):
        res = run_bass_kernel_spmd(nc, in_maps, core_ids=list(range(NCORES)), trace=True)
        print("EXEC_NS", res.exec_time_ns)
    else:
        res = run_bass_kernel_spmd(nc, in_maps, core_ids=list(range(NCORES)))
    R = res.results
    DBG["last_results"] = R
    y_prompt = np.stack([R[c // 2]["y_all"][(c % 2) * 256:(c % 2) * 256 + 256] for c in range(16)])
    y_sample = np.stack([R[c]["y_all"][512:1536] for c in range(NCORES)])
    st = np.concatenate([R[c]["st_out"] for c in range(NCORES)], axis=0)
    kv = [np.concatenate([R[c][nm] for c in range(NCORES)], axis=0) for nm in ("kc_out", "vc_out", "kd_out", "vd_out")]
    return (y_prompt, y_sample, st, kv[0], kv[1], kv[2], kv[3])
```

```python
import math
import contextlib
import numpy as np
import ml_dtypes
import concourse.bass as bass
import concourse.mybir as mybir
from concourse.bass_utils import run_bass_kernel_spmd

F32 = mybir.dt.float32
BF16 = mybir.dt.bfloat16
AF = mybir.ActivationFunctionType
ALU = mybir.AluOpType
AX = mybir.AxisListType

NCORES = 8
D = 1024
T = 1536
SEQS = [(0, 256, 0), (256, 256, 0), (512, 1024, 1)]
DEPTH = 4
DFF = 2816
EPS = 1e-6
NEG = -1.0e30
DBG = {"layers": DEPTH, "dump": False}


class Buf:
    __slots__ = ("name", "lw", "rs", "excl", "lwx")

    carry = {}

    def __init__(self, name="b", excl=False):
        self.name = name
        self.lw = None
        self.lwx = {}
        self.rs = dict(Buf.carry)
        self.excl = excl


class K:
    def __init__(self, nc):
        self.nc = nc
        self.eng = {"pe": nc.tensor, "act": nc.scalar, "dve": nc.vector,
                    "pool": nc.gpsimd, "sp": nc.sync}
        self.sem = {}
        self.cnt = {}
        self.seen = {e: {} for e in self.eng}
        self._ctx = []
        for e in self.eng:
            self._newsem(e)
        self.n_instr = 0
        self.n_wait = 0
        Buf.carry = {}

    def _newsem(self, key):
        cm = self.nc.semaphore("s_" + key)
        s = cm.__enter__()
        self._ctx.append(cm)
        self.sem[key] = s
        self.cnt[key] = 0

    def _deps(self, e, reads, writes):
        deps = {}

        def add(k, v):
            if deps.get(k, 0) < v:
                deps[k] = v
        for b in reads:
            if b.lw is not None:
                add(*b.lw)
            for k, v in b.lwx.items():
                add(k, v)
            if b.excl:
                for k, v in b.rs.items():
                    if k != e:
                        add(k, v)
        for b in writes:
            if b.lw is not None:
                add(*b.lw)
            for k, v in b.lwx.items():
                add(k, v)
            for k, v in b.rs.items():
                if k != e:
                    add(k, v)
        if e == "pe":
            deps.pop("pe", None)
        return deps

    def _wait(self, e, deps):
        eng = self.eng[e]
        seen = self.seen[e]
        for k, v in deps.items():
            if seen.get(k, 0) < v:
                eng.wait_ge(self.sem[k], v)
                seen[k] = v
                self.n_wait += 1

    def op(self, e, fn, reads=(), writes=()):
        self._wait(e, self._deps(e, reads, writes))
        ins = fn(self.eng[e])
        self.cnt[e] += 1
        ins.then_inc(self.sem[e], 1)
        v = self.cnt[e]
        for b in writes:
            b.lw = (e, v)
            b.lwx = {}
            b.rs = {}
        for b in reads:
            if b.lw is None or b.lw != (e, v):
                b.rs[e] = v
        self.n_instr += 1
        return ins

    NDSEM = 56

    def dma(self, q, out, in_, reads=(), writes=(), stream="x"):
        if not hasattr(self, "_dn"):
            self._dn = {"p": 0, "h": 0}
        cls_, npool = ("p", 24) if q == "pool" else ("h", 32)
        key = "d_%s%d" % (cls_, self._dn[cls_] % npool)
        self._dn[cls_] += 1
        if key not in self.sem:
            self._newsem(key)
        deps = self._deps(key, reads, writes)
        if self.cnt[key] > 0:
            deps[key] = max(deps.get(key, 0), self.cnt[key])
        self._wait(q, deps)
        ins = self.eng[q].dma_start(out=out, in_=in_)
        self.cnt[key] += 16
        ins.then_inc(self.sem[key], 16)
        v = self.cnt[key]
        for b in writes:
            b.lwx[key] = v
            b.rs = {}
        for b in reads:
            b.rs[key] = v
        self.n_instr += 1
        return ins

    def barrier(self):
        Buf.carry = {k: c for k, c in self.cnt.items() if c > 0}

    def hard_barrier(self):
        for e in self.eng:
            deps = {k: c for k, c in self.cnt.items() if c > 0 and k != e}
            self._wait(e, deps)

    def finish(self):
        sp = self.eng["sp"]
        for k, s in self.sem.items():
            if self.cnt[k] > 0 and k != "sp":
                sp.wait_ge(s, self.cnt[k])
        for cm in reversed(self._ctx):
            cm.__exit__(None, None, None)


def _bf(a):
    return np.ascontiguousarray(a.astype(ml_dtypes.bfloat16))


def make_consts():
    c = {}
    eye = np.eye(128, dtype=np.float32)
    c["ident_f"] = eye
    c["ident_b"] = _bf(eye)
    c["ones_b"] = _bf(np.ones((128, 128), np.float32))
    c["ones_f"] = np.ones((128, 128), np.float32)
    j = np.arange(128)[:, None]
    i = np.arange(128)[None, :]
    c["tri_f"] = (j <= i).astype(np.float32)
    c["tri_b"] = (j >= i).astype(np.float32)
    c["m1_f"] = np.where(i < j, 0.0, NEG).astype(np.float32)
    c["m2_f"] = np.where(i >= j, 0.0, NEG).astype(np.float32)
    c["m1_b"] = np.where(i > j, 0.0, NEG).astype(np.float32)
    c["m2_b"] = np.where(i <= j, 0.0, NEG).astype(np.float32)
    rm = np.zeros((128, 128), np.float32)
    for d in range(128):
        if (d % 64) < 32:
            rm[d + 32, d] = -1.0
        else:
            rm[d - 32, d] = 1.0
    c["rope_rm"] = _bf(rm)
    L = 1024
    rows = L // 64
    row = np.repeat(np.arange(rows), 64).astype(np.float32)
    col = (np.arange(L) % 64).astype(np.float32)
    inv = (10000.0 ** (-np.arange(0, 64, 2, dtype=np.float32) / 64)).astype(np.float32)
    ang = np.concatenate([row[:, None] * inv, col[:, None] * inv], axis=-1)
    dd = np.arange(128)
    idx = (dd // 64) * 32 + (dd % 32)
    c["rope_c"] = np.ascontiguousarray(np.cos(ang)[:, idx].T.astype(np.float32))
    c["rope_s"] = np.ascontiguousarray(np.sin(ang)[:, idx].T.astype(np.float32))
    s = np.arange(128)[:, None]
    q = np.arange(512)[None, :]
    bm = np.stack([(np.abs(128 * rel + s - q) <= 128) for rel in range(-1, 5)]).astype(np.float32)
    c["band"] = _bf(bm)
    for Ls in (256, 1024):
        N = 2 * Ls
        t = np.arange(Ls, dtype=np.float64)[:, None]
        f = np.arange(Ls, dtype=np.float64)[None, :]
        th = 2.0 * np.pi / N
        Fc = np.cos(th * t * f)
        Fs = np.sin(th * t * f)
        Fs[:, 0] = (-1.0) ** np.arange(Ls)
        c[f"dft_f{Ls}"] = _bf(np.concatenate([Fc, Fs], axis=1))
        wf = np.full((Ls, 1), 2.0 / N)
        wf[0, 0] = 1.0 / N
        Ic = wf * Fc.T
        Is = (2.0 / N) * Fs.T
        Is[0, :] = (1.0 / N) * ((-1.0) ** np.arange(Ls))
        c[f"dft_i{Ls}"] = _bf(np.concatenate([Ic, Is], axis=0))
        tl = np.linspace(0.0, 1.0, Ls, dtype=np.float32)[:, None]
        w = ((2.0 * math.pi / Ls) * np.arange(Ls, dtype=np.float32))[:, None]
        fb = np.linspace(1e-4, 15, 16, dtype=np.float32)[None, :]
        feats = np.concatenate([tl, np.cos(fb * w), -np.sin(fb * w)], axis=-1).astype(np.float32)
        c[f"featsT{Ls}"] = np.ascontiguousarray(feats.T)
        deltas = np.abs(np.linspace(math.log(1e-2) / 1.5, math.log(1e-2) / 0.3, 512, dtype=np.float32))
        c[f"hydec{Ls}"] = np.exp(-tl * deltas[None, :]).astype(np.float32)
    return c


def fm(v, n=None):
    v = np.asarray(v, np.float32)
    return np.ascontiguousarray(v.reshape(-1, 128).T)


class Gen:
    def __init__(self, specs):
        self.nc = bass.Bass("TRN2", target_bir_lowering=False)
        nc = self.nc
        self.dr = {}
        for name, (shape, dt, kind) in specs.items():
            self.dr[name] = nc.dram_tensor(name, list(shape), dt, kind=kind).ap()
        self.es = contextlib.ExitStack()
        self.k = K(nc)
        self._uid = 0

    def sb(self, shape, dt, name=None, es=None):
        self._uid += 1
        nm = f"{name or 't'}_{self._uid}"
        if DBG.get("trace_alloc"):
            print("ALLOC", nm, shape, dt, int(np.prod(shape[1:])) * (2 if dt == BF16 else 4))
        t = (es or self.es).enter_context(self.nc.sbuf_tensor(nm, list(shape), dt))
        return t, Buf(nm)

    def setup_rings(self):
        nc = self.nc
        self.psf_ring = []
        for i in range(4):
            t = self.es.enter_context(nc.psum_tensor(f"psf{i}", [128, 512], F32))
            self.psf_ring.append((t, Buf(f"psf{i}", True)))
        self.psacc = []
        for i in range(2):
            t = self.es.enter_context(nc.psum_tensor(f"psacc{i}", [128, 512], F32))
            self.psacc.append((t, Buf(f"psacc{i}", True)))
        self.psb_ring = []
        for i in range(2):
            t = self.es.enter_context(nc.psum_tensor(f"psb{i}", [128, 1024], BF16))
            self.psb_ring.append((t, Buf(f"psb{i}", True)))
        self.slab_ring = [self.sb([128, 4096], BF16, "slab") for _ in range(4)]
        self._pf = self._pb = self._sl = 0

    def psf(self):
        r = self.psf_ring[self._pf % len(self.psf_ring)]
        self._pf += 1
        return r

    def psb(self):
        r = self.psb_ring[self._pb % len(self.psb_ring)]
        self._pb += 1
        return r

    def wload(self, w2d, c0, ncols, kc, segs=None):
        t, b = self.slab_ring[self._sl % len(self.slab_ring)]
        self._sl += 1
        view = t[:, 0:kc * ncols].rearrange("p (k n) -> p k n", k=kc)
        src = w2d.rearrange("(k p) n -> p k n", p=128)
        if segs is None:
            segs = [(c0, 0, ncols)]
        for (sc, dc, n) in segs:
            self.k.dma("pool", view[:, :, dc:dc + n], src[:, :, sc:sc + n], writes=[b], stream="w%d" % ((self._sl - 1) % 4))
        return view, b

    def dump(self, name, ap, buf, shape, dt=F32):
        if name not in DBG.get("extra_outputs", {}):
            return
        if dt != F32:
            with contextlib.ExitStack() as es3:
                t, b = self.sb(list(shape), F32, "dmp", es3)
                self.k.op("dve", lambda e: e.tensor_copy(out=t[:], in_=ap), reads=[buf], writes=[b])
                self.k.dma("sp", self.dr[name], t[:], reads=[b], stream="dump")
                self.k.barrier()
        else:
            self.k.dma("sp", self.dr[name], ap, reads=[buf], stream="dump")

    def op(self, e, fn, reads=(), writes=()):
        return self.k.op(e, fn, reads, writes)

    def load_const(self, name, shape, dt, q="sp"):
        t, b = self.sb(shape, dt, name)
        self.k.dma(q, t[:], self.dr[name], writes=[b], stream="c")
        return t, b


def build_program(specs):
    g = Gen(specs)
    nc, k, dr, op = g.nc, g.k, g.dr, g.op
    with g.es:
        g.setup_rings()
        ident_f, Bc = g.load_const("ident_f", [128, 128], F32)
        CB = Bc

        def lc(name, shape, dt):
            t, b = g.sb(shape, dt, name)
            k.dma("sp", t[:], dr[name], writes=[CB], stream="c")
            return t
        ident_b = lc("ident_b", [128, 128], BF16)
        ones_b = lc("ones_b", [128, 128], BF16)
        ones_f = lc("ones_f", [128, 128], F32)
        tri = [lc("tri_f", [128, 128], F32), lc("tri_b", [128, 128], F32)]
        m1 = [lc("m1_f", [128, 128], F32), lc("m1_b", [128, 128], F32)]
        m2 = [lc("m2_f", [128, 128], F32), lc("m2_b", [128, 128], F32)]
        pv = {}
        for nm in ("norm_mix", "norm_ffn"):
            pv[nm] = lc(nm, [128, DEPTH, 8], F32)
        pv["final_norm"] = lc("final_norm", [128, 8], F32)
        pv["b_mod"] = lc("b_mod", [128, DEPTH, 48, 2], F32)
        pv["ffn_conv_w"] = lc("ffn_conv_w", [128, DEPTH, 3, 22], F32)
        pv["ffn_conv_b"] = lc("ffn_conv_b", [128, DEPTH, 22], F32)
        pv["csil"] = lc("cvec", [128, 8, 2], F32)

        xT, Bx = g.sb([128, 8, T], F32, "xT")
        hT, Bh = g.sb([128, 8, T], BF16, "hT")
        mixT, Bmix = g.sb([128, 8, T], BF16, "mixT")
        modv, Bmod = g.sb([128, 48, 2], F32, "modv")
        modA, BmodA = g.sb([128, 2, 8, 2], F32, "modA")
        cbf, Bcbf = g.sb([128, 8, 2], BF16, "cbf")
        rstd, Brstd = g.sb([128, 512], F32, "rstd")
        sqb = [g.sb([128, 512], BF16, "sq") for _ in range(2)]
        tmpf = [g.sb([128, 512], F32, "tmpf") for _ in range(3)]
        _rr = {"sq": 0, "tmp": 0}

        def nsq():
            _rr["sq"] += 1
            return sqb[_rr["sq"] % 2]

        def ntmp():
            _rr["tmp"] += 1
            return tmpf[_rr["tmp"] % 3]

        op("act", lambda e: e.activation(out=cbf[:], in_=pv["csil"][:], func=AF.Silu), reads=[CB], writes=[Bcbf])

        with contextlib.ExitStack() as es2:
            xin = [g.sb([128, D], F32, "xin", es2) for _ in range(2)]
            for tt in range(T // 128):
                xt, bx = xin[tt % 2]
                k.dma("sp", xt[:], dr["x_all"][tt * 128:(tt + 1) * 128, :], writes=[bx], stream="xin%d" % (tt % 2))
                for half in range(2):
                    pt, bp = g.psf()
                    for jj in range(4):
                        kc = half * 4 + jj
                        op("pe", lambda e: e.transpose(out=pt[:, jj * 128:(jj + 1) * 128], in_=xt[:, kc * 128:(kc + 1) * 128], identity=ident_f[:]), reads=[bx, CB], writes=[bp])
                    op("dve" if half == 0 else "act",
                       (lambda e: e.tensor_copy(out=xT[:, half * 4:half * 4 + 4, tt * 128:(tt + 1) * 128], in_=pt[:].rearrange("p (a b) -> p a b", a=4))) if half == 0 else
                       (lambda e: e.activation(out=xT[:, half * 4:half * 4 + 4, tt * 128:(tt + 1) * 128], in_=pt[:].rearrange("p (a b) -> p a b", a=4), func=AF.Copy)),
                       reads=[bp], writes=[Bx])
            k.barrier()

        def modulation(i):
            pm, bpm = g.psacc[0]
            for sl in range(12):
                wv, bw = g.wload(dr["w_mod"][i], sl * 512, 512, 8)
                for cc in range(4):
                    ch = sl * 4 + cc
                    for kc in range(8):
                        op("pe", lambda e: e.matmul(pm[:, ch * 2:ch * 2 + 2], lhsT=wv[:, kc, cc * 128:(cc + 1) * 128], rhs=cbf[:, kc, :], start=(kc == 0), stop=(kc == 7)), reads=[bw, Bcbf], writes=[bpm])
            op("dve", lambda e: e.tensor_tensor(out=modv[:].rearrange("p a b -> p (a b)"), in0=pm[:, 0:96], in1=pv["b_mod"][:, i].rearrange("p a b -> p (a b)"), op=ALU.add), reads=[bpm, CB], writes=[Bmod])
            for which, (gname, sc0) in enumerate((("norm_mix", 8), ("norm_ffn", 32))):
                for r in range(2):
                    op("dve", lambda e: e.tensor_scalar(out=modA[:, which, :, r], in0=modv[:, sc0:sc0 + 8, r], scalar1=1.0, scalar2=math.sqrt(D), op0=ALU.add, op1=ALU.mult), reads=[Bmod], writes=[BmodA])
                    op("dve", lambda e: e.tensor_tensor(out=modA[:, which, :, r], in0=modA[:, which, :, r], in1=pv[gname][:, i, :], op=ALU.mult), reads=[BmodA, CB], writes=[BmodA])

        epsc = {}

        def eps_col(val):
            if val not in epsc:
                t, b = g.sb([128, 1], F32, "epsc")
                op("pool", lambda e: e.memset(t[:], float(val)), writes=[b])
                epsc[val] = (t, b)
            return epsc[val]

        for _v in (D * EPS, EPS, 128 * EPS):
            eps_col(float(_v))

        def ss_to_rstd(ps_ap, out_ap, bps, bout, n, epsn, scale=1.0):
            et, eb = eps_col(float(n * epsn))
            op("act", lambda e: e.activation(out=out_ap, in_=ps_ap, func=AF.Sqrt, bias=et[:, 0:1], scale=float(scale)), reads=[bps, eb], writes=[bout])
            op("dve", lambda e: e.reciprocal(out=out_ap, in_=out_ap), reads=[bout], writes=[bout])

        def adaln(which, sh0):
            for tg in range(3):
                r = 0 if tg == 0 else 1
                ts = slice(tg * 512, (tg + 1) * 512)
                pss, bpss = g.psf()
                for kc in range(8):
                    sq, bsq = nsq()
                    op("act", lambda e: e.activation(out=sq[:], in_=xT[:, kc, ts], func=AF.Square), reads=[Bx], writes=[bsq])
                    op("pe", lambda e: e.matmul(pss[:], lhsT=ones_b[:], rhs=sq[:], start=(kc == 0), stop=(kc == 7)), reads=[bsq, CB], writes=[bpss])
                ss_to_rstd(pss[:], rstd[:], bpss, Brstd, D, EPS)
                for kc in range(8):
                    tm, btm = ntmp()
                    op("dve", lambda e: e.tensor_tensor(out=tm[:], in0=xT[:, kc, ts], in1=rstd[:], op=ALU.mult), reads=[Bx, Brstd], writes=[btm])
                    op("act", lambda e: e.activation(out=hT[:, kc, ts], in_=tm[:], func=AF.Identity, scale=modA[:, which, kc, r:r + 1], bias=modv[:, sh0 + kc, r:r + 1]), reads=[btm, BmodA, Bmod], writes=[Bh])

        def proj_residual(w2d, kcn, src, bsrc, g0):
            ncol = 512 if kcn <= 8 else 128
            for c0 in range(0, D, ncol):
                wv, bw = g.wload(w2d, c0, ncol, kcn)
                for mm in range(ncol // 128):
                    m = c0 // 128 + mm
                    for tg in range(3):
                        r = 0 if tg == 0 else 1
                        ts = slice(tg * 512, (tg + 1) * 512)
                        pp, bpp = g.psf()
                        for kc in range(kcn):
                            op("pe", lambda e: e.matmul(pp[:], lhsT=wv[:, kc, mm * 128:(mm + 1) * 128], rhs=src[:, kc, ts], start=(kc == 0), stop=(kc == kcn - 1)), reads=[bw, bsrc], writes=[bpp])
                        op("dve", lambda e: e.scalar_tensor_tensor(out=xT[:, m, ts], in0=pp[:], scalar=modv[:, g0 + m, r:r + 1], in1=xT[:, m, ts], op0=ALU.mult, op1=ALU.add), reads=[bpp, Bmod, Bx], writes=[Bx])

        def ffn(i):
            with contextlib.ExitStack() as es2:
                gbuf, Bg = g.sb([128, 11, T], BF16, "gbuf", es2)
                abuf = [g.sb([128, T], F32, "abuf", es2) for _ in range(2)]
                cbuf = [g.sb([128, T], F32, "cbuf", es2) for _ in range(2)]
                cw = pv["ffn_conv_w"]
                for half in range(2):
                    slabs = {}

                    def part_a(cp):
                        c = half * 11 + cp
                        wv, bw = g.wload(dr["ffn_w_up"][i], 0, 256, 8, segs=[(c * 128, 0, 128), (DFF + c * 128, 128, 128)])
                        slabs[cp] = (wv, bw)
                        ab, bab = abuf[c % 2]
                        cb_, bcb = cbuf[c % 2]
                        for tg in range(3):
                            ts = slice(tg * 512, (tg + 1) * 512)
                            pa, bpa = g.psf()
                            for kc in range(8):
                                op("pe", lambda e: e.matmul(pa[:], lhsT=wv[:, kc, 0:128], rhs=hT[:, kc, ts], start=(kc == 0), stop=(kc == 7)), reads=[bw, Bh], writes=[bpa])
                            op("act", lambda e: e.activation(out=ab[:, ts], in_=pa[:], func=AF.Copy), reads=[bpa], writes=[bab])
                        for (s0, ln, _) in SEQS:
                            op("dve", lambda e: e.tensor_scalar(out=cb_[:, s0:s0 + ln], in0=ab[:, s0:s0 + ln], scalar1=cw[:, i, 1, c:c + 1], scalar2=pv["ffn_conv_b"][:, i, c:c + 1], op0=ALU.mult, op1=ALU.add), reads=[bab, CB], writes=[bcb])
                            op("dve", lambda e: e.scalar_tensor_tensor(out=cb_[:, s0 + 1:s0 + ln], in0=ab[:, s0:s0 + ln - 1], scalar=cw[:, i, 0, c:c + 1], in1=cb_[:, s0 + 1:s0 + ln], op0=ALU.mult, op1=ALU.add), reads=[bab, CB, bcb], writes=[bcb])
                            op("dve", lambda e: e.scalar_tensor_tensor(out=cb_[:, s0:s0 + ln - 1], in0=ab[:, s0 + 1:s0 + ln], scalar=cw[:, i, 2, c:c + 1], in1=cb_[:, s0:s0 + ln - 1], op0=ALU.mult, op1=ALU.add), reads=[bab, CB, bcb], writes=[bcb])
                        op("act", lambda e: e.activation(out=ab[:], in_=cb_[:], func=AF.Silu), reads=[bcb], writes=[bab])

                    def part_b(cp):
                        c = half * 11 + cp
                        wv, bw = slabs.pop(cp)
                        ab, bab = abuf[c % 2]
                        for tg in range(3):
                            ts = slice(tg * 512, (tg + 1) * 512)
                            pb_, bpb = g.psf()
                            for kc in range(8):
                                op("pe", lambda e: e.matmul(pb_[:], lhsT=wv[:, kc, 128:256], rhs=hT[:, kc, ts], start=(kc == 0), stop=(kc == 7)), reads=[bw, Bh], writes=[bpb])
                            op("dve", lambda e: e.tensor_tensor(out=gbuf[:, cp, ts], in0=pb_[:], in1=ab[:, ts], op=ALU.mult), reads=[bpb, bab], writes=[Bg])

                    part_a(0)
                    for cp in range(11):
                        if cp + 1 < 11:
                            part_a(cp + 1)
                        part_b(cp)
                    wd = dr["ffn_w_down"][i][half * 11 * 128:(half + 1) * 11 * 128, :]
                    proj_residual(wd, 11, gbuf, Bg, 40)
                k.barrier()

        import sys
        nl = DBG["layers"]
        for i in range(nl):
            modulation(i)
            adaln(0, 0)
            if i % 2 == 0:
                EVEN_MIXER(g, locals(), i)
            else:
                ODD_MIXER(g, locals(), i)
            proj_residual(dr["w_out"][i], 8, mixT, Bmix, 16)
            adaln(1, 24)
            ffn(i)

        with contextlib.ExitStack() as es2:
            yout = [g.sb([128, D], F32, "yout", es2) for _ in range(2)]
            fin, Bfin = g.sb([128, 8, 512], F32, "fin", es2)
            for tg in range(3):
                ts = slice(tg * 512, (tg + 1) * 512)
                pss, bpss = g.psf()
                for kc in range(8):
                    sq, bsq = nsq()
                    op("act", lambda e: e.activation(out=sq[:], in_=xT[:, kc, ts], func=AF.Square), reads=[Bx], writes=[bsq])
                    op("pe", lambda e: e.matmul(pss[:], lhsT=ones_b[:], rhs=sq[:], start=(kc == 0), stop=(kc == 7)), reads=[bsq, CB], writes=[bpss])
                ss_to_rstd(pss[:], rstd[:], bpss, Brstd, 1, EPS, scale=1.0 / D)
                for kc in range(8):
                    op("dve", lambda e: e.scalar_tensor_tensor(out=fin[:, kc, :], in0=xT[:, kc, ts], scalar=pv["final_norm"][:, kc:kc + 1], in1=rstd[:], op0=ALU.mult, op1=ALU.mult), reads=[Bx, CB, Brstd], writes=[Bfin])
                for t4 in range(4):
                    tt = tg * 4 + t4
                    yo, byo = yout[tt % 2]
                    for half in range(2):
                        pt, bp = g.psf()
                        for jj in range(4):
                            kc = half * 4 + jj
                            op("pe", lambda e: e.transpose(out=pt[:, jj * 128:(jj + 1) * 128], in_=fin[:, kc, t4 * 128:(t4 + 1) * 128], identity=ident_f[:]), reads=[Bfin, CB], writes=[bp])
                        op("act" if half else "dve",
                           (lambda e: e.activation(out=yo[:, half * 512:(half + 1) * 512], in_=pt[:], func=AF.Copy)) if half else
                           (lambda e: e.tensor_copy(out=yo[:, half * 512:(half + 1) * 512], in_=pt[:])),
                           reads=[bp], writes=[byo])
                    k.dma("sp", dr["y_all"][tt * 128:(tt + 1) * 128, :], yo[:], reads=[byo], stream="yo%d" % (tt % 2))
            k.barrier()
        k.finish()
    print("program: instr", k.n_instr, "waits", k.n_wait)
    return nc


def _zero_fill(g, L, names):
    k, dr, op = L["k"], L["dr"], L["op"]
    with contextlib.ExitStack() as es2:
        z, bz = g.sb([128, 2048], F32, "zfill", es2)
        op("pool", lambda e: e.memset(z[:], 0.0), writes=[bz])
        for nm, view in names:
            k.dma("sp", view, z[:, 0:view.shape[-1]] if len(view.shape) == 2 else z[:].rearrange("p (a b) -> p a b", b=view.shape[-1])[:, 0:view.shape[1], :], reads=[bz], stream="zf")
        k.barrier()


def EVEN_MIXER(g, L, i):
    k, dr, op = L["k"], L["dr"], L["op"]
    hT, Bh, mixT, Bmix, CB = L["hT"], L["Bh"], L["mixT"], L["Bmix"], L["CB"]
    ones_b, ones_f, ident_b = L["ones_b"], L["ones_f"], L["ident_b"]
    tri, m1, m2 = L["tri"], L["m1"], L["m2"]
    ss_to_rstd = L["ss_to_rstd"]
    sqb, tmpf = L["sqb"], L["tmpf"]
    j = i // 2
    W = dr["ev_w_in"][j]
    with contextlib.ExitStack() as es:
        def sb(shape, dt, nm, es_=None):
            return g.sb(shape, dt, nm, es_ or es)
        cw, Bcw = sb([128, 3, 12], F32, "dncw")
        dtb, Bdtb = sb([128, 8], F32, "dtb")
        negA, BnegA = sb([128, 8], F32, "negA")
        dnn, Bdnn = sb([128, 1], F32, "dnn")
        beta, Bbeta = sb([128, 12, 8], F32, "beta")
        gg, Bgg = sb([128, 12, 8], F32, "gg")
        gcum, Bgcum = sb([128, 12, 8], F32, "gcum")
        bg, Bbg = sb([128, 12, 8], F32, "bg")
        k.dma("sp", cw[:], dr["dn_conv_w_fm"][j], writes=[Bcw], stream="ec")
        k.dma("sp", dtb[:], dr["dn_dt_bias_bc"][:, j, :], writes=[Bdtb], stream="ec")
        k.dma("sp", negA[:], dr["dn_a_log_bc"][:, j, :], writes=[BnegA], stream="ec")
        k.dma("sp", dnn[:], dr["dn_norm_fm"][j], writes=[Bdnn], stream="ec")
        op("act", lambda e: e.activation(out=negA[:], in_=negA[:], func=AF.Exp), reads=[BnegA], writes=[BnegA])
        op("dve", lambda e: e.tensor_scalar(out=negA[:], in0=negA[:], scalar1=-1.0, scalar2=None, op0=ALU.mult), reads=[BnegA], writes=[BnegA])
        op("dve", lambda e: e.tensor_scalar(out=dnn[:], in0=dnn[:], scalar1=math.sqrt(128.0), scalar2=None, op0=ALU.mult), reads=[Bdnn], writes=[Bdnn])
        wba, bwba = g.wload(W, 2048, 16, 8)
        for tt in range(12):
            pp, bpp = g.psf()
            for kc in range(8):
                op("pe", lambda e: e.matmul(pp[:, 0:16], lhsT=hT[:, kc, tt * 128:(tt + 1) * 128], rhs=wba[:, kc, :], start=(kc == 0), stop=(kc == 7)), reads=[bwba, Bh], writes=[bpp])
            op("act", lambda e: e.activation(out=beta[:, tt, :], in_=pp[:, 0:8], func=AF.Sigmoid), reads=[bpp], writes=[Bbeta])
            op("dve", lambda e: e.tensor_tensor(out=gg[:, tt, :], in0=pp[:, 8:16], in1=dtb[:], op=ALU.add), reads=[bpp, Bdtb], writes=[Bgg])
        op("act", lambda e: e.activation(out=gg[:], in_=gg[:], func=AF.Exp), reads=[Bgg], writes=[Bgg])
        op("act", lambda e: e.activation(out=gg[:], in_=gg[:], func=AF.Ln, bias=ones_f[:, 0:1]), reads=[Bgg, CB], writes=[Bgg])
        for tt in range(12):
            op("dve", lambda e: e.tensor_tensor(out=gg[:, tt, :], in0=gg[:, tt, :], in1=negA[:], op=ALU.mult), reads=[Bgg, BnegA], writes=[Bgg])
        for tt in range(12):
            pp, bpp = g.psf()
            for d_ in range(2):
                op("pe", lambda e: e.matmul(pp[:, d_ * 4:d_ * 4 + 4], lhsT=tri[d_][:], rhs=gg[:, tt, d_ * 4:d_ * 4 + 4], start=True, stop=True), reads=[CB, Bgg], writes=[bpp])
            op("dve", lambda e: e.tensor_copy(out=gcum[:, tt, :], in_=pp[:, 0:8]), reads=[bpp], writes=[Bgcum])
        op("act", lambda e: e.activation(out=bg[:], in_=gcum[:], func=AF.Exp), reads=[Bgcum], writes=[Bbg])
        op("dve", lambda e: e.tensor_tensor(out=bg[:], in0=bg[:], in1=beta[:], op=ALU.mult), reads=[Bbg, Bbeta], writes=[Bbg])

        g.dump("d_beta", beta[:], Bbeta, [128, 12, 8])
        g.dump("d_gg", gg[:], Bgg, [128, 12, 8])
        g.dump("d_gcum", gcum[:], Bgcum, [128, 12, 8])
        for h in range(4):
            with contextlib.ExitStack() as eh:
                qT, Bq = sb([128, T], BF16, "dq", eh)
                kT, Bk = sb([128, T], BF16, "dk", eh)
                vT, Bv = sb([128, T], BF16, "dv", eh)
                zs, Bzs = sb([128, T], F32, "dz", eh)
                oacc, Bo = sb([128, T], F32, "doa", eh)
                ktok, Bkt = sb([128, 12, 128], BF16, "dkt", eh)
                vtok, Bvt = sb([128, 12, 128], BF16, "dvt", eh)
                wv, bw = g.wload(W, 0, 512, 8, segs=[(h * 128, 0, 128), (512 + h * 128, 128, 128), (1024 + h * 128, 256, 128), (1536 + h * 128, 384, 128)])
                with contextlib.ExitStack() as e1:
                    raw, Braw = sb([128, T], F32, "draw", e1)
                    cv, Bcv = sb([128, T], F32, "dcv", e1)
                    for comp in range(4):
                        for tg in range(3):
                            ts = slice(tg * 512, (tg + 1) * 512)
                            pp, bpp = g.psf()
                            for kc in range(8):
                                op("pe", lambda e: e.matmul(pp[:], lhsT=wv[:, kc, comp * 128:(comp + 1) * 128], rhs=hT[:, kc, ts], start=(kc == 0), stop=(kc == 7)), reads=[bw, Bh], writes=[bpp])
                            if comp == 3:
                                op("act", lambda e: e.activation(out=zs[:, ts], in_=pp[:], func=AF.Silu), reads=[bpp], writes=[Bzs])
                            else:
                                op("act", lambda e: e.activation(out=raw[:, ts], in_=pp[:], func=AF.Copy), reads=[bpp], writes=[Braw])
                        if comp == 3:
                            continue
                        cc = comp * 4 + h
                        for (s0, ln, _) in SEQS:
                            op("dve", lambda e: e.tensor_scalar(out=cv[:, s0:s0 + ln], in0=raw[:, s0:s0 + ln], scalar1=cw[:, 1, cc:cc + 1], scalar2=None, op0=ALU.mult), reads=[Braw, Bcw], writes=[Bcv])
                            op("dve", lambda e: e.scalar_tensor_tensor(out=cv[:, s0 + 1:s0 + ln], in0=raw[:, s0:s0 + ln - 1], scalar=cw[:, 0, cc:cc + 1], in1=cv[:, s0 + 1:s0 + ln], op0=ALU.mult, op1=ALU.add), reads=[Braw, Bcw, Bcv], writes=[Bcv])
                            op("dve", lambda e: e.scalar_tensor_tensor(out=cv[:, s0:s0 + ln - 1], in0=raw[:, s0 + 1:s0 + ln], scalar=cw[:, 2, cc:cc + 1], in1=cv[:, s0:s0 + ln - 1], op0=ALU.mult, op1=ALU.add), reads=[Braw, Bcw, Bcv], writes=[Bcv])
                        op("act", lambda e: e.activation(out=cv[:], in_=cv[:], func=AF.Silu), reads=[Bcv], writes=[Bcv])
                        if comp == 2:
                            op("dve", lambda e: e.tensor_copy(out=vT[:], in_=cv[:]), reads=[Bcv], writes=[Bv])
                            continue
                        dst, bdst = (qT, Bq) if comp == 0 else (kT, Bk)
                        for tg in range(3):
                            ts = slice(tg * 512, (tg + 1) * 512)
                            sq, bsq = sqb[tg % 2]
                            op("act", lambda e: e.activation(out=sq[:], in_=cv[:, ts], func=AF.Square), reads=[Bcv], writes=[bsq])
                            p2, bp2 = g.psf()
                            op("pe", lambda e: e.matmul(p2[:], lhsT=ones_b[:], rhs=sq[:], start=True, stop=True), reads=[bsq, CB], writes=[bp2])
                            rs_, brs_ = tmpf[tg % 3]
                            ss_to_rstd(p2[:], rs_[:], bp2, brs_, 1, EPS)
                            if comp == 0:
                                op("dve", lambda e: e.scalar_tensor_tensor(out=dst[:, ts], in0=cv[:, ts], scalar=128.0 ** -0.5, in1=rs_[:], op0=ALU.mult, op1=ALU.mult), reads=[Bcv, brs_], writes=[bdst])
                            else:
                                op("dve", lambda e: e.tensor_tensor(out=dst[:, ts], in0=cv[:, ts], in1=rs_[:], op=ALU.mult), reads=[Bcv, brs_], writes=[bdst])
                    k.barrier()
                if h == 0:
                    g.dump("d_qT", qT[:], Bq, [128, T], BF16)
                    g.dump("d_kT", kT[:], Bk, [128, T], BF16)
                    g.dump("d_vT", vT[:], Bv, [128, T], BF16)
                for tt in range(12):
                    ptb, bptb = g.psb()
                    op("pe", lambda e: e.transpose(out=ptb[:, 0:128], in_=kT[:, tt * 128:(tt + 1) * 128], identity=ident_b[:]), reads=[Bk, CB], writes=[bptb])
                    op("pe", lambda e: e.transpose(out=ptb[:, 128:256], in_=vT[:, tt * 128:(tt + 1) * 128], identity=ident_b[:]), reads=[Bv, CB], writes=[bptb])
                    op("dve", lambda e: e.tensor_copy(out=ktok[:, tt, :], in_=ptb[:, 0:128]), reads=[bptb], writes=[Bkt])
                    op("act", lambda e: e.activation(out=vtok[:, tt, :], in_=ptb[:, 128:256], func=AF.Copy), reads=[bptb], writes=[Bvt])

                groups = []
                groups.append(([(0, 0), (1, 0), (2, 0), (3, 0)], [(0, 0, [0, 1], True, True), (1, 0, [2, 3], True, True)]))
                groups.append(([(0, 1), (1, 1), (2, 1), (3, 1)], [(0, 1, [1, 0], True, True), (1, 1, [3, 2], True, True)]))
                groups.append(([(4 + c, 0) for c in range(4)], [(2, 0, [0, 1, 2, 3], True, False)]))
                groups.append(([(8 + c, 0) for c in range(4)], [(2, 0, [0, 1, 2, 3], False, True)]))
                groups.append(([(8 + c, 1) for c in range(4)], [(2, 1, [3, 2, 1, 0], True, False)]))
                groups.append(([(4 + c, 1) for c in range(4)], [(2, 1, [3, 2, 1, 0], False, True)]))
                Ss = [sb([128, 128], F32, "S", eh) for _ in range(2)]
                Sbs = [sb([128, 128], BF16, "Sb", eh) for _ in range(2)]
                for chains, scans in groups:
                    with contextlib.ExitStack() as e2:
                        NCH = len(chains)
                        Az, BAz = sb([128, NCH, 2, 128], F32, "Az", e2)
                        PP = [sb([128, NCH, 2, 128], F32, "PP", e2) for _ in range(2)]
                        RTf, BRTf = sb([128, NCH, 128], F32, "RTf", e2)
                        RT, BRT = sb([128, NCH, 128], BF16, "RT", e2)
                        attnT, Bat = sb([128, NCH, 128], BF16, "attnT", e2)
                        kbg, Bkbg = sb([128, NCH, 128], BF16, "kbg", e2)
                        vb, Bvb = sb([128, NCH, 128], BF16, "vb", e2)
                        kdec, Bkd = sb([128, NCH, 128], BF16, "kdec", e2)
                        qgT, Bqg = sb([128, NCH, 128], BF16, "qgT", e2)
                        glb, Bglb = sb([128, NCH], F32, "glb", e2)
                        kde, Bkde = sb([128, NCH], F32, "kde", e2)
                        Lg = [sb([128, 128], F32, "Lg", e2) for _ in range(2)]
                        D1 = [sb([128, 128], F32, "D1", e2) for _ in range(2)]
                        D2 = [sb([128, 128], F32, "D2", e2) for _ in range(2)]
                        EG = [sb([128, 128], F32, "EG", e2) for _ in range(2)]
                        nwTa, BnwT = sb([128, NCH, 128], BF16, "nwTa", e2)
                        vnews = [sb([128, 128], BF16, "vnew", e2) for _ in range(2)]
                        BSZ = 2
                        for b0 in range(0, NCH, BSZ):
                            batch = [(c, chains[c][0], chains[c][1]) for c in range(b0, min(NCH, b0 + BSZ))]
                            st = {}
                            for (c, tt, d_) in batch:
                                col = d_ * 4 + h
                                lg, blg = Lg[c % 2]
                                op("act", lambda e: e.activation(out=lg[:], in_=ones_f[:], func=AF.Identity, scale=gg[:, tt, col:col + 1]), reads=[CB, Bgg], writes=[blg])
                            for (c, tt, d_) in batch:
                                tsl = slice(tt * 128, (tt + 1) * 128)
                                lg, blg = Lg[c % 2]
                                pG, bpG = g.psf()
                                op("pe", lambda e: e.matmul(pG[:, 0:128], lhsT=lg[:], rhs=tri[d_][:], start=True, stop=True), reads=[blg, CB], writes=[bpG])
                                pK, bpK = g.psf()
                                op("pe", lambda e: e.matmul(pK[:, 0:128], lhsT=kT[:, tsl], rhs=kT[:, tsl], start=True, stop=True), reads=[Bk], writes=[bpK])
                                op("pe", lambda e: e.matmul(pK[:, 128:256], lhsT=kT[:, tsl], rhs=qT[:, tsl], start=True, stop=True), reads=[Bk, Bq], writes=[bpK])
                                st[c] = (pG, bpG, pK, bpK)
                            for (c, tt, d_) in batch:
                                col = d_ * 4 + h
                                gcol = gcum[:, tt, col:col + 1]
                                pG, bpG, pK, bpK = st[c]
                                d1, bd1 = D1[c % 2]
                                d2, bd2 = D2[c % 2]
                                eg, beg = EG[c % 2]
                                lastc = 127 if d_ == 0 else 0
                                op("dve", lambda e: e.scalar_tensor_tensor(out=d1[:], in0=pG[:, 0:128], scalar=gcol, in1=m1[d_][:], op0=ALU.subtract, op1=ALU.subtract), reads=[bpG, Bgcum, CB], writes=[bd1])
                                op("dve", lambda e: e.scalar_tensor_tensor(out=d2[:], in0=pG[:, 0:128], scalar=gcol, in1=m2[d_][:], op0=ALU.subtract, op1=ALU.add), reads=[bpG, Bgcum, CB], writes=[bd2])
                                op("dve", lambda e: e.tensor_scalar(out=kde[:, c:c + 1], in0=pG[:, lastc:lastc + 1], scalar1=gcol, scalar2=None, op0=ALU.subtract), reads=[bpG, Bgcum], writes=[Bkde])
                                op("act", lambda e: e.activation(out=eg[:], in_=pG[:, 0:128], func=AF.Exp), reads=[bpG], writes=[beg])
                                op("act", lambda e: e.activation(out=d1[:], in_=d1[:], func=AF.Exp, scale=-1.0), reads=[bd1], writes=[bd1])
                                op("act", lambda e: e.activation(out=d2[:], in_=d2[:], func=AF.Exp), reads=[bd2], writes=[bd2])
                                op("act", lambda e: e.activation(out=kde[:, c:c + 1], in_=kde[:, c:c + 1], func=AF.Exp), reads=[Bkde], writes=[Bkde])
                            for (c, tt, d_) in batch:
                                col = d_ * 4 + h
                                tsl = slice(tt * 128, (tt + 1) * 128)
                                pG, bpG, pK, bpK = st[c]
                                d1, bd1 = D1[c % 2]
                                d2, bd2 = D2[c % 2]
                                eg, beg = EG[c % 2]
                                lastc = 127 if d_ == 0 else 0
                                op("dve", lambda e: e.scalar_tensor_tensor(out=Az[:, c, 0, :], in0=pK[:, 0:128], scalar=beta[:, tt, col:col + 1], in1=d1[:], op0=ALU.mult, op1=ALU.mult), reads=[bpK, Bbeta, bd1], writes=[BAz])
                                op("dve", lambda e: e.tensor_tensor(out=attnT[:, c, :], in0=pK[:, 128:256], in1=d2[:], op=ALU.mult), reads=[bpK, bd2], writes=[Bat])
                                op("dve", lambda e: e.tensor_copy(out=glb[:, c:c + 1], in_=eg[:, lastc:lastc + 1]), reads=[beg], writes=[Bglb])
                                op("pool", lambda e: e.tensor_tensor(out=qgT[:, c, :], in0=qT[:, tsl], in1=eg[:], op=ALU.mult), reads=[Bq, beg], writes=[Bqg])
                            for (c, tt, d_) in batch:
                                ptb, bptb = g.psf()
                                op("pe", lambda e: e.transpose(out=ptb[:, 0:128], in_=Az[:, c, 0, :], identity=L["ident_f"][:]), reads=[BAz, CB], writes=[bptb])
                                op("dve", lambda e: e.tensor_copy(out=Az[:, c, 1, :], in_=ptb[:, 0:128]), reads=[bptb], writes=[BAz])
                                op("dve", lambda e: e.tensor_tensor(out=RTf[:, c, :], in0=L["ident_f"][:], in1=ptb[:, 0:128], op=ALU.subtract), reads=[bptb, CB], writes=[BRTf])
                            for (c, tt, d_) in batch:
                                col = d_ * 4 + h
                                op("dve", lambda e: e.tensor_scalar(out=kbg[:, c, :], in0=ktok[:, tt, :], scalar1=bg[:, tt, col:col + 1], scalar2=None, op0=ALU.mult), reads=[Bkt, Bbg], writes=[Bkbg])
                                op("act", lambda e: e.activation(out=vb[:, c, :], in_=vtok[:, tt, :], func=AF.Identity, scale=beta[:, tt, col:col + 1]), reads=[Bvt, Bbeta], writes=[Bvb])
                                op("dve", lambda e: e.tensor_scalar(out=kdec[:, c, :], in0=ktok[:, tt, :], scalar1=kde[:, c:c + 1], scalar2=None, op0=ALU.mult), reads=[Bkt, Bkde], writes=[Bkd])
                        NP = NCH // 2
                        BPPp = [[Buf("PPp") for _ in range(NP)] for _ in range(2)]
                        BRp = [Buf("RTp") for _ in range(NP)]
                        cur = Az
                        bcur = [[BAz] for _ in range(NP)]
                        for step in range(6):
                            nxt = PP[step % 2][0]
                            bnx = BPPp[step % 2]
                            last = step == 5
                            for p in range(NP):
                                pp, bpp = g.psf()
                                for cc_ in range(2):
                                    c = 2 * p + cc_
                                    op("pe", lambda e: e.matmul(pp[:, cc_ * 256:cc_ * 256 + 128], lhsT=cur[:, c, 1, :], rhs=cur[:, c, 0, :], start=True, stop=True), reads=bcur[p], writes=[bpp])
                                    if not last:
                                        op("pe", lambda e: e.matmul(pp[:, cc_ * 256 + 128:cc_ * 256 + 256], lhsT=cur[:, c, 0, :], rhs=cur[:, c, 1, :], start=True, stop=True), reads=bcur[p], writes=[bpp])
                                if last:
                                    op("dve" if p % 2 == 0 else "act",
                                       (lambda e: e.tensor_copy(out=nxt[:, 2 * p:2 * p + 2, 0, :], in_=pp[:].rearrange("p (a b c) -> p a b c", a=2, b=2)[:, :, 0, :])) if p % 2 == 0 else
                                       (lambda e: e.activation(out=nxt[:, 2 * p:2 * p + 2, 0, :], in_=pp[:].rearrange("p (a b c) -> p a b c", a=2, b=2)[:, :, 0, :], func=AF.Copy)),
                                       reads=[bpp], writes=[bnx[p]])
                                elif p % 2 == 0:
                                    op("dve", lambda e: e.tensor_copy(out=nxt[:, 2 * p:2 * p + 2].rearrange("p a b c -> p (a b c)"), in_=pp[:]), reads=[bpp], writes=[bnx[p]])
                                else:
                                    op("act", lambda e: e.activation(out=nxt[:, 2 * p:2 * p + 2].rearrange("p a b c -> p (a b c)"), in_=pp[:], func=AF.Copy), reads=[bpp], writes=[bnx[p]])
                            for p in range(NP):
                                pr, bpr = g.psf()
                                for cc_ in range(2):
                                    c = 2 * p + cc_
                                    op("pe", lambda e: e.matmul(pr[:, cc_ * 128:(cc_ + 1) * 128], lhsT=nxt[:, c, 0, :], rhs=RTf[:, c, :], start=True, stop=True), reads=[bnx[p], BRTf, BRp[p]], writes=[bpr])
                                op("dve", lambda e: e.tensor_tensor(out=RTf[:, 2 * p:2 * p + 2].rearrange("p a b -> p (a b)"), in0=pr[:, 0:256], in1=RTf[:, 2 * p:2 * p + 2].rearrange("p a b -> p (a b)"), op=ALU.add), reads=[bpr, BRTf, BRp[p]], writes=[BRp[p]])
                            cur = nxt
                            bcur = [[bnx[p]] for p in range(NP)]
                        op("act", lambda e: e.activation(out=RT[:], in_=RTf[:], func=AF.Copy), reads=[BRTf] + BRp, writes=[BRT])
                        if h == 0 and chains[0] == (4, 0):
                            g.dump("d_Az", Az[:, 0], BAz, [128, 2, 128])
                            g.dump("d_RT", RT[:, 0], BRT, [128, 128], BF16)
                            g.dump("d_attnT", attnT[:, 0], Bat, [128, 128], BF16)
                            g.dump("d_kbg", kbg[:, 0], Bkbg, [128, 128], BF16)
                            g.dump("d_qgT", qgT[:, 0], Bqg, [128, 128], BF16)
                            g.dump("d_kdec", kdec[:, 0], Bkd, [128, 128], BF16)
                        for c in range(NCH):
                            pw, bpw = g.psf()
                            op("pe", lambda e: e.matmul(pw[:, 0:128], lhsT=kbg[:, c, :], rhs=RT[:, c, :], start=True, stop=True), reads=[Bkbg, BRT], writes=[bpw])
                            op("act", lambda e: e.activation(out=nwTa[:, c, :], in_=pw[:, 0:128], func=AF.Identity, scale=-1.0), reads=[bpw], writes=[BnwT])
                        for si, (sqi, d_, order, s_init, s_final) in enumerate(scans):
                            S, BS = Ss[si]
                            Sb, BSb = Sbs[si]
                            if s_init:
                                if SEQS[sqi][2]:
                                    k.dma("sp", S[:], dr["state_dn"][j, d_, h], writes=[BS], stream="st")
                                else:
                                    op("pool", lambda e: e.memset(S[:], 0.0), writes=[BS])
                                op("act", lambda e: e.activation(out=Sb[:], in_=S[:], func=AF.Copy), reads=[BS], writes=[BSb])
                        for st_ in range(max(len(sc[2]) for sc in scans)):
                            for si, (sqi, d_, order, s_init, s_final) in enumerate(scans):
                                if st_ >= len(order):
                                    continue
                                S, BS = Ss[si]
                                Sb, BSb = Sbs[si]
                                vnew, Bvn = vnews[si]
                                c = order[st_]
                                tt = chains[c][0]
                                tsl = slice(tt * 128, (tt + 1) * 128)
                                pv_, bpv = g.psf()
                                op("pe", lambda e: e.matmul(pv_[:, 0:128], lhsT=RT[:, c, :], rhs=vb[:, c, :], start=True, stop=False), reads=[BRT, Bvb], writes=[bpv])
                                op("pe", lambda e: e.matmul(pv_[:, 0:128], lhsT=nwTa[:, c, :], rhs=Sb[:], start=False, stop=True), reads=[BnwT, BSb], writes=[bpv])
                                op("dve", lambda e: e.tensor_copy(out=vnew[:], in_=pv_[:, 0:128]), reads=[bpv], writes=[Bvn])
                                pS, bpS = g.psf()
                                op("pe", lambda e: e.matmul(pS[:, 0:128], lhsT=kdec[:, c, :], rhs=vnew[:], start=True, stop=True), reads=[Bkd, Bvn], writes=[bpS])
                                po_, bpo = g.psf()
                                op("pe", lambda e: e.matmul(po_[:, 0:128], lhsT=Sb[:], rhs=qgT[:, c, :], start=True, stop=False), reads=[BSb, Bqg], writes=[bpo])
                                op("pe", lambda e: e.matmul(po_[:, 0:128], lhsT=vnew[:], rhs=attnT[:, c, :], start=False, stop=True), reads=[Bvn, Bat], writes=[bpo])
                                op("dve", lambda e: e.scalar_tensor_tensor(out=Sb[:], in0=S[:], scalar=glb[:, c:c + 1], in1=pS[:, 0:128], op0=ALU.mult, op1=ALU.add), reads=[BS, Bglb, bpS], writes=[BSb])
                                op("dve", lambda e: e.scalar_tensor_tensor(out=S[:], in0=S[:], scalar=glb[:, c:c + 1], in1=pS[:, 0:128], op0=ALU.mult, op1=ALU.add), reads=[BS, Bglb, bpS], writes=[BS])
                                if d_ == 0:
                                    op("act", lambda e: e.activation(out=oacc[:, tsl], in_=po_[:, 0:128], func=AF.Copy), reads=[bpo], writes=[Bo])
                                else:
                                    op("dve", lambda e: e.tensor_tensor(out=oacc[:, tsl], in0=po_[:, 0:128], in1=oacc[:, tsl], op=ALU.add), reads=[bpo, Bo], writes=[Bo])
                        for si, (sqi, d_, order, s_init, s_final) in enumerate(scans):
                            if s_final and not SEQS[sqi][2]:
                                k.dma("sp", dr["st_out"][sqi, j, d_, h], Ss[si][0][:], reads=[Ss[si][1]], stream="sto")
                        k.barrier()
                if h == 0:
                    g.dump("d_oacc", oacc[:], Bo, [128, T])
                for tg in range(3):
                    ts = slice(tg * 512, (tg + 1) * 512)
                    sq, bsq = sqb[tg % 2]
                    op("act", lambda e: e.activation(out=sq[:], in_=oacc[:, ts], func=AF.Square), reads=[Bo], writes=[bsq])
                    p2, bp2 = g.psf()
                    op("pe", lambda e: e.matmul(p2[:], lhsT=ones_b[:], rhs=sq[:], start=True, stop=True), reads=[bsq, CB], writes=[bp2])
                    rs_, brs_ = tmpf[tg % 3]
                    ss_to_rstd(p2[:], rs_[:], bp2, brs_, 128, EPS)
                    op("dve", lambda e: e.scalar_tensor_tensor(out=oacc[:, ts], in0=oacc[:, ts], scalar=dnn[:, 0:1], in1=rs_[:], op0=ALU.mult, op1=ALU.mult), reads=[Bo, Bdnn, brs_], writes=[Bo])
                    op("dve", lambda e: e.tensor_tensor(out=mixT[:, h, ts], in0=oacc[:, ts], in1=zs[:, ts], op=ALU.mult), reads=[Bo, Bzs], writes=[Bmix])
                k.barrier()
        HYENA(g, L, i)
        k.barrier()


def HYENA(g, L, i):
    k, dr, op = L["k"], L["dr"], L["op"]
    hT, Bh, mixT, Bmix, CB = L["hT"], L["Bh"], L["mixT"], L["Bmix"], L["CB"]
    ident_b = L["ident_b"]
    j = i // 2
    W = dr["ev_w_in"][j]
    CW = 256
    I32 = mybir.dt.int32
    with contextlib.ExitStack() as es:
        def sb(shape, dt, nm, es_=None):
            return g.sb(shape, dt, nm, es_ or es)
        cw, Bcw = sb([128, 3, 12], F32, "hycw")
        cbv, Bcbv = sb([128, 12], F32, "hycb")
        w1, Bw1 = sb([33, 64], F32, "hyw1")
        w2, Bw2 = sb([64, 64], F32, "hyw2")
        fr, Bfr = sb([64, 4], F32, "hyfr")
        k.dma("sp", cw[:], dr["hy_conv_w_fm"][j], writes=[Bcw], stream="hc")
        k.dma("sp", cbv[:], dr["hy_conv_b_fm"][j], writes=[Bcbv], stream="hc")
        k.dma("sp", w1[:], dr["hy_w1"][j], writes=[Bw1], stream="hc")
        k.dma("sp", w2[:], dr["hy_w2"][j], writes=[Bw2], stream="hc")
        k.dma("sp", fr[:, 0:1], dr["hy_freq1_fm"][j], writes=[Bfr], stream="hc")
        k.dma("sp", fr[:, 1:2], dr["hy_b1_fm"][j], writes=[Bfr], stream="hc")
        k.dma("sp", fr[:, 2:3], dr["hy_freq2_fm"][j], writes=[Bfr], stream="hc")
        k.dma("sp", fr[:, 3:4], dr["hy_b2_fm"][j], writes=[Bfr], stream="hc")
        op("dve", lambda e: e.tensor_tensor(out=fr[:, 1:2], in0=fr[:, 1:2], in1=fr[:, 0:1], op=ALU.mult), reads=[Bfr], writes=[Bfr])
        op("dve", lambda e: e.tensor_tensor(out=fr[:, 3:4], in0=fr[:, 3:4], in1=fr[:, 2:3], op=ALU.mult), reads=[Bfr], writes=[Bfr])

        for ch in range(512 // CW):
            with contextlib.ExitStack() as ec:
                x2T, Bx2 = sb([128, 2, T], BF16, "x2T", ec)
                ztok, Bzt = sb([128, 12, CW], BF16, "ztok", ec)
                x1tok, Bx1t = sb([128, 12, CW], BF16, "x1tok", ec)
                zm2, Bzm2 = sb([64, 1024], F32, "zm2", ec)
                with contextlib.ExitStack() as e1:
                    raw, Braw = sb([128, T], F32, "hraw", e1)
                    cv, Bcv = sb([128, T], F32, "hcv", e1)
                    cvb, Bcvb = sb([128, T], BF16, "hcvb", e1)
                    for comp in range(3):
                        wv, bw = g.wload(W, 2064 + comp * 512 + ch * CW, CW, 8)
                        for c2 in range(CW // 128):
                            cc = comp * 4 + ch * (CW // 128) + c2
                            for tg in range(3):
                                ts = slice(tg * 512, (tg + 1) * 512)
                                pp, bpp = g.psf()
                                for kc in range(8):
                                    op("pe", lambda e: e.matmul(pp[:], lhsT=wv[:, kc, c2 * 128:(c2 + 1) * 128], rhs=hT[:, kc, ts], start=(kc == 0), stop=(kc == 7)), reads=[bw, Bh], writes=[bpp])
                                op("act", lambda e: e.activation(out=raw[:, ts], in_=pp[:], func=AF.Copy), reads=[bpp], writes=[Braw])
                            for (s0, ln, _) in SEQS:
                                op("dve", lambda e: e.tensor_scalar(out=cv[:, s0:s0 + ln], in0=raw[:, s0:s0 + ln], scalar1=cw[:, 1, cc:cc + 1], scalar2=cbv[:, cc:cc + 1], op0=ALU.mult, op1=ALU.add), reads=[Braw, Bcw, Bcbv], writes=[Bcv])
                                op("dve", lambda e: e.scalar_tensor_tensor(out=cv[:, s0 + 1:s0 + ln], in0=raw[:, s0:s0 + ln - 1], scalar=cw[:, 0, cc:cc + 1], in1=cv[:, s0 + 1:s0 + ln], op0=ALU.mult, op1=ALU.add), reads=[Braw, Bcw, Bcv], writes=[Bcv])
                                op("dve", lambda e: e.scalar_tensor_tensor(out=cv[:, s0:s0 + ln - 1], in0=raw[:, s0 + 1:s0 + ln], scalar=cw[:, 2, cc:cc + 1], in1=cv[:, s0:s0 + ln - 1], op0=ALU.mult, op1=ALU.add), reads=[Braw, Bcw, Bcv], writes=[Bcv])
                            if comp == 1:
                                op("act", lambda e: e.activation(out=x2T[:, c2, :], in_=cv[:], func=AF.Copy), reads=[Bcv], writes=[Bx2])
                            else:
                                dst, bdst = (x1tok, Bx1t) if comp == 0 else (ztok, Bzt)
                                op("act", lambda e: e.activation(out=cvb[:], in_=cv[:], func=AF.Copy), reads=[Bcv], writes=[Bcvb])
                                for t4 in range(3):
                                    ptb, bptb = g.psb()
                                    for q4 in range(4):
                                        tt = t4 * 4 + q4
                                        op("pe", lambda e: e.transpose(out=ptb[:, q4 * 128:(q4 + 1) * 128], in_=cvb[:, tt * 128:(tt + 1) * 128], identity=ident_b[:]), reads=[Bcvb, CB], writes=[bptb])
                                    op("dve", lambda e: e.tensor_copy(out=dst[:, t4 * 4:t4 * 4 + 4, c2 * 128:(c2 + 1) * 128], in_=ptb[:, 0:512].rearrange("p (a b) -> p a b", a=4)), reads=[bptb], writes=[bdst])
                    k.barrier()

                for (Ls, seqs) in ((256, [SEQS[0], SEQS[1]]), (1024, [SEQS[2]])):
                    nt = Ls // 128
                    Fd, Fi = dr[f"dft_f{Ls}"], dr[f"dft_i{Ls}"]
                    with contextlib.ExitStack() as eL:
                        with contextlib.ExitStack() as em:
                            ft, Bft = sb([33, Ls], F32, "feats", em)
                            k.dma("sp", ft[:], dr[f"featsT{Ls}"], writes=[Bft], stream="hc")
                            zmlp, Bzm = sb([64, Ls], F32, "zmlp", em)
                            ti, Bti = sb([64, Ls], I32, "ti", em)
                            tf, Btf = sb([64, Ls], F32, "tf", em)

                            def sin_layer(dst, bdst, wmat, bwm, src, bsrc, kdim, fcol):
                                for c0 in range(0, Ls, 512):
                                    n = min(512, Ls - c0)
                                    pp, bpp = g.psf()
                                    op("pe", lambda e: e.matmul(pp[0:64, 0:n], lhsT=wmat[0:kdim, :], rhs=src[0:kdim, c0:c0 + n], start=True, stop=True), reads=[bwm, bsrc], writes=[bpp])
                                    op("dve", lambda e: e.tensor_scalar(out=dst[:, c0:c0 + n], in0=pp[0:64, 0:n], scalar1=fr[:, fcol:fcol + 1], scalar2=fr[:, fcol + 1:fcol + 2], op0=ALU.mult, op1=ALU.add), reads=[bpp, Bfr], writes=[bdst])
                                d_ = dst[:, 0:Ls]
                                op("dve", lambda e: e.tensor_scalar(out=d_, in0=d_, scalar1=1.0 / (2 * math.pi), scalar2=None, op0=ALU.mult), reads=[bdst], writes=[bdst])
                                op("dve", lambda e: e.tensor_copy(out=ti[:], in_=d_), reads=[bdst], writes=[Bti])
                                op("dve", lambda e: e.tensor_copy(out=tf[:], in_=ti[:]), reads=[Bti], writes=[Btf])
                                op("dve", lambda e: e.tensor_tensor(out=d_, in0=d_, in1=tf[:], op=ALU.subtract), reads=[bdst, Btf], writes=[bdst])
                                op("dve", lambda e: e.tensor_single_scalar(out=tf[:], in_=d_, scalar=0.5, op=ALU.is_gt), reads=[bdst], writes=[Btf])
                                op("dve", lambda e: e.tensor_tensor(out=d_, in0=d_, in1=tf[:], op=ALU.subtract), reads=[bdst, Btf], writes=[bdst])
                                op("dve", lambda e: e.tensor_single_scalar(out=tf[:], in_=d_, scalar=-0.5, op=ALU.is_lt), reads=[bdst], writes=[Btf])
                                op("dve", lambda e: e.tensor_tensor(out=d_, in0=d_, in1=tf[:], op=ALU.add), reads=[bdst, Btf], writes=[bdst])
                                op("act", lambda e: e.activation(out=d_, in_=d_, func=AF.Sin, scale=2 * math.pi), reads=[bdst], writes=[bdst])
                            sin_layer(zmlp, Bzm, w1, Bw1, ft, Bft, 33, 0)
                            sin_layer(zm2, Bzm2, w2, Bw2, zmlp, Bzm, 64, 2)
                            k.barrier()
                        z2tok, Bz2 = sb([128, len(seqs), nt, CW], BF16, "z2tok", eL)
                        Hc, BHc = sb([128, nt, CW], F32, "Hc", eL)
                        Hs, BHs = sb([128, nt, CW], F32, "Hs", eL)
                        brow, Bbrow = sb([128, CW], F32, "brow", eL)
                        w3s, Bw3 = sb([64, 2, CW], F32, "w3s", eL)
                        tq = [sb([128, CW], F32, "tq", eL) for _ in range(4)]

                        def fslabs(grp):
                            if Ls == 256:
                                v_, b_ = g.wload(Fd, 0, 512, nt)
                                return (v_, b_, 0), (v_, b_, 256)
                            vc, bc = g.wload(Fd, grp * 512, 512, nt)
                            vs, bs = g.wload(Fd, Ls + grp * 512, 512, nt)
                            return (vc, bc, 0), (vs, bs, 0)

                        for o in range(2):
                            k.dma("sp", brow[:], dr["hy_bias_bc"][:, j, o, ch * CW:(ch + 1) * CW], writes=[Bbrow], stream="hc")
                            for sd in range(2):
                                c0 = o * 1024 + sd * 512 + ch * CW
                                k.dma("sp", w3s[:, sd, :], dr["hy_w3"][j][:, c0:c0 + CW], writes=[Bw3], stream="hc")
                            with contextlib.ExitStack() as eS:
                                Sd, BSd = sb([128, nt, CW], BF16, "Sd", eS)
                                Dd, BDd = sb([128, nt, CW], BF16, "Dd", eS)
                                for tt in range(nt):
                                    pp, bpp = g.psf()
                                    for sd in range(2):
                                        op("pe", lambda e: e.matmul(pp[:, sd * CW:(sd + 1) * CW], lhsT=zm2[:, tt * 128:(tt + 1) * 128], rhs=w3s[:, sd, :], start=True, stop=True), reads=[Bzm2, Bw3], writes=[bpp])
                                    dec, bdec = tq[tt % 2]
                                    k.dma("sp", dec[:], dr[f"hydec{Ls}"][tt * 128:(tt + 1) * 128, ch * CW:(ch + 1) * CW], writes=[bdec], stream="hd%d" % (tt % 2))
                                    tfw, btfw = tq[2]
                                    tbw, btbw = tq[3]
                                    op("dve", lambda e: e.tensor_tensor(out=tfw[:], in0=pp[:, 0:CW], in1=dec[:], op=ALU.mult), reads=[bpp, bdec], writes=[btfw])
                                    op("dve", lambda e: e.tensor_tensor(out=tbw[:], in0=pp[:, CW:2 * CW], in1=dec[:], op=ALU.mult), reads=[bpp, bdec], writes=[btbw])
                                    if tt == 0:
                                        op("dve", lambda e: e.memset(tbw[0:1, :], 0.0), reads=[btbw], writes=[btbw])
                                    op("pool", lambda e: e.tensor_tensor(out=Sd[:, tt, :], in0=tfw[:], in1=tbw[:], op=ALU.add), reads=[btfw, btbw], writes=[BSd])
                                    op("pool", lambda e: e.tensor_tensor(out=Dd[:, tt, :], in0=tfw[:], in1=tbw[:], op=ALU.subtract), reads=[btfw, btbw], writes=[BDd])
                                for grp in range(max(1, nt // 4)):
                                    (vc, bc, oc), (vs, bs, os_) = fslabs(grp)
                                    for f4 in range(min(4, nt)):
                                        fc = grp * 4 + f4
                                        pc, bpc = g.psf()
                                        for tc in range(nt):
                                            op("pe", lambda e: e.matmul(pc[:, 0:CW], lhsT=vc[:, tc, oc + f4 * 128:oc + (f4 + 1) * 128], rhs=Sd[:, tc, :], start=(tc == 0), stop=(tc == nt - 1)), reads=[bc, BSd], writes=[bpc])
                                        op("dve", lambda e: e.tensor_tensor(out=Hc[:, fc, :], in0=pc[:, 0:CW], in1=brow[:], op=ALU.add), reads=[bpc, Bbrow], writes=[BHc])
                                        ps_, bps = g.psf()
                                        for tc in range(nt):
                                            op("pe", lambda e: e.matmul(ps_[:, 0:CW], lhsT=vs[:, tc, os_ + f4 * 128:os_ + (f4 + 1) * 128], rhs=Dd[:, tc, :], start=(tc == 0), stop=(tc == nt - 1)), reads=[bs, BDd], writes=[bps])
                                        op("act", lambda e: e.activation(out=Hs[:, fc, :], in_=ps_[:, 0:CW], func=AF.Copy), reads=[bps], writes=[BHs])
                                        if fc == 0:
                                            pn, bpn = g.psf()
                                            for tc in range(nt):
                                                op("pe", lambda e: e.matmul(pn[0:1, 0:CW], lhsT=vs[:, tc, os_:os_ + 1], rhs=Sd[:, tc, :], start=(tc == 0), stop=(tc == nt - 1)), reads=[bs, BSd], writes=[bpn])
                                            op("dve", lambda e: e.tensor_tensor(out=Hs[0:1, 0, :], in0=pn[0:1, 0:CW], in1=brow[0:1, :], op=ALU.add), reads=[bpn, Bbrow, BHs], writes=[BHs])
                                k.barrier()
                            eY = contextlib.ExitStack()
                            Y, BY = sb([128, 2 * nt, CW], BF16, "Y", eY)
                            for si, (s0, ln, smp) in enumerate(seqs):
                                t0 = s0 // 128
                                for grp in range(max(1, nt // 4)):
                                    (vc, bc, oc), (vs, bs, os_) = fslabs(grp)
                                    for f4 in range(min(4, nt)):
                                        fc = grp * 4 + f4
                                        pc, bpc = g.psf()
                                        ps_, bps = g.psf()
                                        for tc in range(nt):
                                            rhs = ztok[:, t0 + tc, :] if o == 0 else z2tok[:, si, tc, :]
                                            brhs = Bzt if o == 0 else Bz2
                                            op("pe", lambda e: e.matmul(pc[:, 0:CW], lhsT=vc[:, tc, oc + f4 * 128:oc + (f4 + 1) * 128], rhs=rhs, start=(tc == 0), stop=(tc == nt - 1)), reads=[bc, brhs], writes=[bpc])
                                        for tc in range(nt):
                                            rhs = ztok[:, t0 + tc, :] if o == 0 else z2tok[:, si, tc, :]
                                            brhs = Bzt if o == 0 else Bz2
                                            op("pe", lambda e: e.matmul(ps_[:, 0:CW], lhsT=vs[:, tc, os_ + f4 * 128:os_ + (f4 + 1) * 128], rhs=rhs, start=(tc == 0), stop=(tc == nt - 1)), reads=[bs, brhs], writes=[bps])
                                        (a1, ba1), (a2, ba2), (a3, ba3), (a4, ba4) = tq
                                        op("dve", lambda e: e.tensor_tensor(out=a1[:], in0=pc[:, 0:CW], in1=Hc[:, fc, :], op=ALU.mult), reads=[bpc, BHc], writes=[ba1])
                                        op("dve", lambda e: e.tensor_tensor(out=a3[:], in0=pc[:, 0:CW], in1=Hs[:, fc, :], op=ALU.mult), reads=[bpc, BHs], writes=[ba3])
                                        op("dve", lambda e: e.tensor_tensor(out=a2[:], in0=ps_[:, 0:CW], in1=Hs[:, fc, :], op=ALU.mult), reads=[bps, BHs], writes=[ba2])
                                        op("dve", lambda e: e.tensor_tensor(out=a4[:], in0=ps_[:, 0:CW], in1=Hc[:, fc, :], op=ALU.mult), reads=[bps, BHc], writes=[ba4])
                                        op("pool", lambda e: e.tensor_tensor(out=Y[:, fc, :], in0=a1[:], in1=a2[:], op=ALU.subtract), reads=[ba1, ba2], writes=[BY])
                                        op("pool", lambda e: e.tensor_tensor(out=Y[:, nt + fc, :], in0=a3[:], in1=a4[:], op=ALU.add), reads=[ba3, ba4], writes=[BY])
                                        if fc == 0:
                                            op("pool", lambda e: e.tensor_copy(out=Y[0:1, 0, :], in_=a1[0:1, :]), reads=[ba1, BY], writes=[BY])
                                            op("pool", lambda e: e.tensor_copy(out=Y[0:1, nt, :], in_=a2[0:1, :]), reads=[ba2, BY], writes=[BY])
                                ncol = 256 if Ls == 1024 else 256
                                for c0 in range(0, Ls, ncol):
                                    vi, bi = g.wload(Fi, c0, ncol, 2 * nt)
                                    if o == 0:
                                        for t2 in range(ncol // 128):
                                            tl = c0 // 128 + t2
                                            pp, bpp = g.psf()
                                            for fc in range(2 * nt):
                                                op("pe", lambda e: e.matmul(pp[:, 0:CW], lhsT=vi[:, fc, t2 * 128:(t2 + 1) * 128], rhs=Y[:, fc, :], start=(fc == 0), stop=(fc == 2 * nt - 1)), reads=[bi, BY], writes=[bpp])
                                            op("dve", lambda e: e.tensor_tensor(out=z2tok[:, si, tl, :], in0=pp[:, 0:CW], in1=x1tok[:, t0 + tl, :], op=ALU.mult), reads=[bpp, Bx1t], writes=[Bz2])
                                    else:
                                        for c2 in range(CW // 128):
                                            pp, bpp = g.psf()
                                            for fc in range(2 * nt):
                                                op("pe", lambda e: e.matmul(pp[:, 0:ncol], lhsT=Y[:, fc, c2 * 128:(c2 + 1) * 128], rhs=vi[:, fc, :], start=(fc == 0), stop=(fc == 2 * nt - 1)), reads=[bi, BY], writes=[bpp])
                                            op("dve", lambda e: e.tensor_tensor(out=mixT[:, 4 + ch * (CW // 128) + c2, s0 + c0:s0 + c0 + ncol], in0=pp[:, 0:ncol], in1=x2T[:, c2, s0 + c0:s0 + c0 + ncol], op=ALU.mult), reads=[bpp, Bx2], writes=[Bmix])
                            k.barrier()
                            eY.close()
                        k.barrier()
                k.barrier()
        k.barrier()


def ODD_MIXER(g, L, i):
    k, dr, op = L["k"], L["dr"], L["op"]
    hT, Bh, mixT, Bmix, CB = L["hT"], L["Bh"], L["mixT"], L["Bmix"], L["CB"]
    ones_b, ident_b = L["ones_b"], L["ident_b"]
    ss_to_rstd = L["ss_to_rstd"]
    j = i // 2
    W = dr["od_w_in"][j]
    scale = 128.0 ** -0.5
    with contextlib.ExitStack() as es:
        def sb(shape, dt, nm):
            return g.sb(shape, dt, nm, es)
        qT, Bq = sb([128, 4, T], BF16, "qT")
        kT, Bk = sb([128, 2, T], BF16, "kT")
        vtok, Bv = sb([128, 12, 2, 128], BF16, "vtok")
        ckT, Bck = sb([128, 2, 256], BF16, "ckT")
        cktok, Bckt = sb([128, 2, 2, 128], BF16, "cktok")
        cvtok, Bcv = sb([128, 2, 2, 128], BF16, "cvtok")
        ropec, Brc = sb([128, 1024], F32, "ropec")
        ropes, Brs = sb([128, 1024], F32, "ropes")
        k.dma("sp", ropec[:], dr["rope_c"], writes=[Brc])
        k.dma("sp", ropes[:], dr["rope_s"], writes=[Brs])
        rrm, Brm = sb([128, 128], BF16, "rrm")
        band, Bband = sb([128, 6, 512], BF16, "band")
        gq, Bgq = sb([128, 2], F32, "gq")
        grow, Bgrow = sb([128, 128], F32, "grow")
        sinkc, Bsink = sb([128, 4], F32, "sinkc")
        qf = [L["tmpf"][0], L["tmpf"][1]]
        qb = [sb([128, 512], BF16, "qb") for _ in range(2)]
        sqs = L["sqb"]
        t1 = [sb([128, 512], F32, "t1") for _ in range(2)]
        rs_, Brs_ = L["tmpf"][2]
        ebuf = [sb([128, 512], BF16, "ebuf") for _ in range(3)]
        rinv, Brinv = sb([128, 512], F32, "rinv")
        nq, Bnq = sb([128, 4, 3], F32, "nq")
        nk, Bnk = sb([128, 2, 4], F32, "nk")
        negM, BnegM = sb([128, 2, 4], F32, "negM")
        esink, Bes = sb([128, 2, 4], F32, "esink")
        kvf, Bkvf = sb([128, 512], F32, "kvf")
        kvo, Bkvo = sb([128, 2, 128], F32, "kvo")
        ssk, Bssk = sb([128, 2], F32, "ssk")
        sqk, Bsqk = sb([128, 2, 128], F32, "sqk")
        cnt = {"q": 0, "e": 0}

        k.dma("sp", rrm[:], dr["rope_rm"], writes=[Brm], stream="oc")
        k.dma("sp", band[:], dr["band"].rearrange("r s q -> s r q"), writes=[Bband], stream="oc")
        k.dma("sp", gq[:, 0:1], dr["c_q_norm_fm"][j], writes=[Bgq], stream="oc")
        k.dma("sp", gq[:, 1:2], dr["c_k_norm_fm"][j], writes=[Bgq], stream="oc")
        k.dma("sp", grow[:], dr["c_k_norm_row"][:, j, :], writes=[Bgrow], stream="oc")
        k.dma("sp", sinkc[:], dr["d_sink_bc"][:, j, :], writes=[Bsink], stream="oc")
        op("dve", lambda e: e.tensor_scalar(out=gq[:], in0=gq[:], scalar1=math.sqrt(128.0), scalar2=None, op0=ALU.mult), reads=[Bgq], writes=[Bgq])

        for grp in range(2):
            qc0 = grp * 1024
            kc0 = grp * 1024 + 512
            wq, bwq = g.wload(W, qc0, 512, 8)
            wk, bwk = g.wload(W, kc0, 256, 8)
            def stage0(ci, tg):
                isq = ci < 4
                wv, bw, cc = (wq, bwq, ci) if isq else (wk, bwk, ci - 4)
                dst, bdst, dc = (qT, Bq, ci) if isq else (kT, Bk, ci - 4)
                ts = slice(tg * 512, (tg + 1) * 512)
                pp, bpp = g.psf()
                for kc in range(8):
                    op("pe", lambda e: e.matmul(pp[:], lhsT=wv[:, kc, cc * 128:(cc + 1) * 128], rhs=hT[:, kc, ts], start=(kc == 0), stop=(kc == 7)), reads=[bw, Bh], writes=[bpp])
                cnt["q"] += 1
                par = cnt["q"]
                f_, bf_ = qf[par % 2]
                if grp == 1 and tg == 0:
                    op("act", lambda e: e.activation(out=dst[:, dc, ts], in_=pp[:], func=AF.Copy), reads=[bpp], writes=[bdst])
                else:
                    op("act", lambda e: e.activation(out=f_[:], in_=pp[:], func=AF.Copy), reads=[bpp], writes=[bf_])
                return (isq, dst, bdst, dc, ts, par, f_, bf_)

            def rest(ci, tg, ctx):
                isq, dst, bdst, dc, ts, par, f_, bf_ = ctx
                if grp == 0:
                    sq, bsq = sqs[par % 2]
                    op("act", lambda e: e.activation(out=sq[:], in_=f_[:], func=AF.Square), reads=[bf_], writes=[bsq])
                    p2, bp2 = g.psf()
                    op("pe", lambda e: e.matmul(p2[:], lhsT=ones_b[:], rhs=sq[:], start=True, stop=True), reads=[bsq, CB], writes=[bp2])
                    ss_to_rstd(p2[:], rs_[:], bp2, Brs_, 128, EPS)
                    gcol = gq[:, 0:1] if isq else gq[:, 1:2]
                    if tg == 0:
                        op("dve", lambda e: e.scalar_tensor_tensor(out=dst[:, dc, ts], in0=f_[:], scalar=gcol, in1=rs_[:], op0=ALU.mult, op1=ALU.mult), reads=[bf_, Bgq, Brs_], writes=[bdst])
                    else:
                        op("dve", lambda e: e.scalar_tensor_tensor(out=f_[:], in0=f_[:], scalar=gcol, in1=rs_[:], op0=ALU.mult, op1=ALU.mult), reads=[bf_, Bgq, Brs_], writes=[bf_])
                if tg > 0:
                    ps_ = slice((tg - 1) * 512, tg * 512)
                    b_, bb_ = qb[par % 2]
                    op("act", lambda e: e.activation(out=b_[:], in_=f_[:], func=AF.Copy), reads=[bf_], writes=[bb_])
                    p3, bp3 = g.psf()
                    op("pe", lambda e: e.matmul(p3[:], lhsT=rrm[:], rhs=b_[:], start=True, stop=True), reads=[Brm, bb_], writes=[bp3])
                    t_, bt_ = t1[par % 2]
                    op("dve", lambda e: e.tensor_tensor(out=t_[:], in0=f_[:], in1=ropec[:, ps_], op=ALU.mult), reads=[bf_, Brc], writes=[bt_])
                    op("dve", lambda e: e.tensor_tensor(out=f_[:], in0=p3[:], in1=ropes[:, ps_], op=ALU.mult), reads=[bp3, Brs, bf_], writes=[bf_])
                    op("dve", lambda e: e.tensor_tensor(out=dst[:, dc, ts], in0=f_[:], in1=t_[:], op=ALU.add), reads=[bf_, bt_], writes=[bdst])
                sq, bsq = sqs[(par + 1) % 2]
                op("act", lambda e: e.activation(out=sq[:], in_=dst[:, dc, ts], func=AF.Square), reads=[bdst], writes=[bsq])
                p4, bp4 = g.psf()
                op("pe", lambda e: e.matmul(p4[:], lhsT=ones_b[:], rhs=sq[:], start=True, stop=True), reads=[bsq, CB], writes=[bp4])
                if isq:
                    op("dve", lambda e: e.tensor_reduce(out=nq[:, dc, tg:tg + 1], in_=p4[:], axis=AX.X, op=ALU.max), reads=[bp4], writes=[Bnq])
                else:
                    op("dve", lambda e: e.tensor_reduce(out=nk[:, dc, tg:tg + 1], in_=p4[:], axis=AX.X, op=ALU.max), reads=[bp4], writes=[Bnk])

            items = [(ci, tg) for ci in range(6) for tg in range(3)]
            ctx = stage0(*items[0])
            for n_, it in enumerate(items):
                nctx = stage0(*items[n_ + 1]) if n_ + 1 < len(items) else None
                rest(it[0], it[1], ctx)
                ctx = nctx
            if DBG.get("odd_stop") == "A":
                continue
            wkv, bwkv = g.wload(W, kc0, 512, 8)
            kname, vname = ("kc_out", "vc_out") if grp == 0 else ("kd_out", "vd_out")
            for tt in range(12):
                pp, bpp = g.psf()
                for kc in range(8):
                    op("pe", lambda e: e.matmul(pp[:], lhsT=hT[:, kc, tt * 128:(tt + 1) * 128], rhs=wkv[:, kc, :], start=(kc == 0), stop=(kc == 7)), reads=[bwkv, Bh], writes=[bpp])
                op("act", lambda e: e.activation(out=vtok[:, tt], in_=pp[:, 256:512].rearrange("p (a b) -> p a b", a=2), func=AF.Copy), reads=[bpp], writes=[Bv])
                if tt < 4 and DBG.get("odd_stop") != "B1":
                    sq_, tb = tt // 2, tt % 2
                    op("dve", lambda e: e.tensor_copy(out=kvf[:], in_=pp[:]), reads=[bpp], writes=[Bkvf])
                    if DBG.get("odd_stop") == "B2":
                        continue
                    k.dma("sp", dr[vname][sq_, j, tb * 128:(tb + 1) * 128], kvf[:, 256:512].rearrange("p (a b) -> p a b", a=2), reads=[Bkvf], stream="kvo")
                    if grp == 1:
                        k.dma("sp", dr[kname][sq_, j, tb * 128:(tb + 1) * 128], kvf[:, 0:256].rearrange("p (a b) -> p a b", a=2), reads=[Bkvf], stream="kvo")
                    else:
                        kv3 = kvf[:, 0:256].rearrange("p (a b) -> p a b", a=2)
                        op("pool", lambda e: e.tensor_tensor(out=sqk[:], in0=kv3, in1=kv3, op=ALU.mult), reads=[Bkvf], writes=[Bsqk])
                        op("dve", lambda e: e.tensor_reduce(out=ssk[:], in_=sqk[:], axis=AX.X, op=ALU.add), reads=[Bsqk], writes=[Bssk])
                        ss_to_rstd(ssk[:], ssk[:], Bssk, Bssk, 1, EPS, scale=1.0 / 128.0)
                        for kv in range(2):
                            op("dve", lambda e: e.scalar_tensor_tensor(out=kvo[:, kv, :], in0=kvf[:, kv * 128:(kv + 1) * 128], scalar=ssk[:, kv:kv + 1], in1=grow[:], op0=ALU.mult, op1=ALU.mult), reads=[Bkvf, Bssk, Bgrow], writes=[Bkvo])
                        k.dma("sp", dr[kname][sq_, j, tb * 128:(tb + 1) * 128], kvo[:], reads=[Bkvo], stream="kvo")
            if DBG.get("odd_stop") in ("B", "B1", "B2"):
                continue
            ckn, cvn = ("cache_k_c", "cache_v_c") if grp == 0 else ("cache_k_d", "cache_v_d")
            k.dma("pool", cktok[:], dr[ckn][j].rearrange("(sb p) k d -> p sb k d", p=128), writes=[Bckt], stream="cch")
            k.dma("pool", cvtok[:], dr[cvn][j].rearrange("(sb p) k d -> p sb k d", p=128), writes=[Bcv], stream="cch")
            ptb, bptb = g.psb()
            for kv in range(2):
                for sbk in range(2):
                    o_ = (kv * 2 + sbk) * 128
                    op("pe", lambda e: e.transpose(out=ptb[:, o_:o_ + 128], in_=cktok[:, sbk, kv, :], identity=ident_b[:]), reads=[Bckt, CB], writes=[bptb])
            op("dve", lambda e: e.tensor_copy(out=ckT[:].rearrange("p a b -> p (a b)"), in_=ptb[:, 0:512]), reads=[bptb], writes=[Bck])
            for kv in range(2):
                sq, bsq = sqs[kv]
                op("act", lambda e: e.activation(out=sq[:, 0:256], in_=ckT[:, kv, :], func=AF.Square), reads=[Bck], writes=[bsq])
                p4, bp4 = g.psf()
                op("pe", lambda e: e.matmul(p4[:, 0:256], lhsT=ones_b[:], rhs=sq[:, 0:256], start=True, stop=True), reads=[bsq, CB], writes=[bp4])
                op("dve", lambda e: e.tensor_reduce(out=nk[:, kv, 3:4], in_=p4[:, 0:256], axis=AX.X, op=ALU.max), reads=[bp4], writes=[Bnk])
            if DBG.get("odd_stop") == "C":
                continue
            op("dve", lambda e: e.tensor_tensor(out=nq[:, :, 1], in0=nq[:, :, 1], in1=nq[:, :, 2], op=ALU.max), reads=[Bnq], writes=[Bnq])
            op("dve", lambda e: e.tensor_tensor(out=nk[:, :, 1], in0=nk[:, :, 1], in1=nk[:, :, 2], op=ALU.max), reads=[Bnk], writes=[Bnk])
            op("dve", lambda e: e.tensor_tensor(out=nk[:, :, 1], in0=nk[:, :, 1], in1=nk[:, :, 3], op=ALU.max), reads=[Bnk], writes=[Bnk])
            for r in range(2):
                for h in range(4):
                    op("dve", lambda e: e.tensor_tensor(out=negM[:, r, h:h + 1], in0=nq[:, h, r:r + 1], in1=nk[:, h // 2, r:r + 1], op=ALU.mult), reads=[Bnq, Bnk], writes=[BnegM])
            op("act", lambda e: e.activation(out=negM[:], in_=negM[:], func=AF.Sqrt), reads=[BnegM], writes=[BnegM])
            op("dve", lambda e: e.tensor_scalar(out=negM[:], in0=negM[:], scalar1=-scale, scalar2=None, op0=ALU.mult), reads=[BnegM], writes=[BnegM])
            if grp == 1:
                for r in range(2):
                    op("dve", lambda e: e.tensor_tensor(out=esink[:, r, :], in0=sinkc[:], in1=negM[:, r, :], op=ALU.add), reads=[Bsink, BnegM], writes=[Bes])
                op("act", lambda e: e.activation(out=esink[:], in_=esink[:], func=AF.Exp), reads=[Bes], writes=[Bes])
            if DBG.get("odd_stop") == "M":
                continue
            for (s0, ln, smp) in SEQS:
                for h in range(4):
                    kvh = h // 2
                    mcol = negM[:, smp, h:h + 1]
                    qgs = [(s0 + a, min(512, ln)) for a in range(0, ln, 512)]
                    for qi, (q0, n) in enumerate(qgs):
                        blocks = []
                        if smp:
                            for sbk in range(2):
                                blocks.append((ckT[:, kvh, sbk * 128:(sbk + 1) * 128], Bck, cvtok[:, sbk, kvh, :], Bcv, None))
                        for kb in range(ln // 128):
                            msk = None
                            if smp and grp == 1:
                                rel = kb - 4 * qi
                                if rel < -1 or rel > 4:
                                    continue
                                msk = rel + 1
                            t0 = s0 + kb * 128
                            blocks.append((kT[:, kvh, t0:t0 + 128], Bk, vtok[:, t0 // 128, kvh, :], Bv, msk))
                        po, bpo = g.psacc[0]
                        pm, bpm = g.psacc[1]

                        def score(bi_):
                            kap_, bk__ = blocks[bi_][0], blocks[bi_][1]
                            pst_, bpst_ = g.psf()
                            op("pe", lambda e: e.matmul(pst_[:, 0:n], lhsT=kap_, rhs=qT[:, h, q0:q0 + n], start=True, stop=True), reads=[bk__, Bq], writes=[bpst_])
                            return pst_, bpst_
                        nxt_score = score(0)
                        for bi, (kap, bk_, vap, bv_, msk) in enumerate(blocks):
                            pst, bpst = nxt_score
                            if bi + 1 < len(blocks):
                                nxt_score = score(bi + 1)
                            cnt["e"] += 1
                            eb, beb = ebuf[cnt["e"] % 3]
                            op("act", lambda e: e.activation(out=eb[:, 0:n], in_=pst[:, 0:n], func=AF.Exp, bias=mcol, scale=scale), reads=[bpst, BnegM], writes=[beb])
                            if msk is not None:
                                op("pool", lambda e: e.tensor_tensor(out=eb[:, 0:n], in0=eb[:, 0:n], in1=band[:, msk, 0:n], op=ALU.mult), reads=[beb, Bband], writes=[beb])
                            first, last = bi == 0, bi == len(blocks) - 1
                            op("pe", lambda e: e.matmul(po[:, 0:n], lhsT=vap, rhs=eb[:, 0:n], start=first, stop=last), reads=[bv_, beb], writes=[bpo])
                            op("pe", lambda e: e.matmul(pm[:, 0:n], lhsT=ones_b[:], rhs=eb[:, 0:n], start=first, stop=last), reads=[CB, beb], writes=[bpm])
                        if grp == 1:
                            op("dve", lambda e: e.tensor_scalar(out=rinv[:, 0:n], in0=pm[:, 0:n], scalar1=esink[:, smp, h:h + 1], scalar2=None, op0=ALU.add), reads=[bpm, Bes], writes=[Brinv])
                            op("dve", lambda e: e.reciprocal(out=rinv[:, 0:n], in_=rinv[:, 0:n]), reads=[Brinv], writes=[Brinv])
                        else:
                            op("dve", lambda e: e.reciprocal(out=rinv[:, 0:n], in_=pm[:, 0:n]), reads=[bpm], writes=[Brinv])
                        op("dve", lambda e: e.tensor_tensor(out=mixT[:, grp * 4 + h, q0:q0 + n], in0=po[:, 0:n], in1=rinv[:, 0:n], op=ALU.mult), reads=[bpo, Brinv], writes=[Bmix])
        k.barrier()


_CONSTS = None
_PROG = {}


def _specs(consts, percore, shared):
    sp = {}
    for nm, a in {**consts, **shared, **percore}.items():
        dt = BF16 if a.dtype == ml_dtypes.bfloat16 else F32
        sp[nm] = (a.shape, dt, "ExternalInput")
    return sp


def host_prepare(inp, core):
    pc = {}
    pc["x_all"] = np.ascontiguousarray(np.concatenate(
        [inp["x_prompt"][2 * core], inp["x_prompt"][2 * core + 1], inp["x_sample"][core]], axis=0))
    cv = np.stack([fm(inp["c_ctx"]), fm(inp["c"][core])], axis=-1)
    pc["cvec"] = np.ascontiguousarray(cv)
    pc["state_dn"] = np.ascontiguousarray(inp["state_dn"][core])
    for nm in ("cache_k_c", "cache_v_c", "cache_k_d", "cache_v_d"):
        pc[nm] = np.ascontiguousarray(inp[nm][core])
    return pc


def host_shared(inp):
    sh = {}
    for nm in ("w_mod", "w_out", "ffn_w_up", "ffn_w_down", "ev_w_in", "od_w_in", "hy_w3", "hy_w1", "hy_w2"):
        sh[nm] = inp[nm]
    sh["norm_mix"] = np.ascontiguousarray(np.stack([fm(inp["norm_mix"][i]) for i in range(DEPTH)], axis=1))
    sh["norm_ffn"] = np.ascontiguousarray(np.stack([fm(inp["norm_ffn"][i]) for i in range(DEPTH)], axis=1))
    sh["final_norm"] = fm(inp["final_norm"])
    bm = np.stack([fm(inp["b_mod"][i]) for i in range(DEPTH)], axis=1)
    sh["b_mod"] = np.ascontiguousarray(np.repeat(bm[..., None], 2, axis=-1))
    cw = np.stack([np.stack([fm(inp["ffn_conv_w"][i, t]) for t in range(3)], axis=1) for i in range(DEPTH)], axis=1)
    sh["ffn_conv_w"] = np.ascontiguousarray(cw)
    sh["c_q_norm_fm"] = np.ascontiguousarray(inp["c_q_norm"][:, :, None])
    sh["c_k_norm_fm"] = np.ascontiguousarray(inp["c_k_norm"][:, :, None])
    sh["c_k_norm_row"] = np.ascontiguousarray(np.broadcast_to(inp["c_k_norm"][None], (128, 2, 128)))
    sh["d_sink_bc"] = np.ascontiguousarray(np.broadcast_to(inp["d_sink"][None], (128, 2, 4)))
    cwd = np.stack([np.stack([fm(inp["dn_conv_w"][jj, t]) for t in range(3)], axis=1) for jj in range(2)], axis=0)
    sh["dn_conv_w_fm"] = np.ascontiguousarray(cwd)
    sh["dn_dt_bias_bc"] = np.ascontiguousarray(np.broadcast_to(inp["dn_dt_bias"].reshape(1, 2, 8), (128, 2, 8)))
    sh["dn_a_log_bc"] = np.ascontiguousarray(np.broadcast_to(inp["dn_a_log"].reshape(1, 2, 8), (128, 2, 8)))
    sh["dn_norm_fm"] = np.ascontiguousarray(inp["dn_norm"][:, :, None])
    cwh = np.stack([np.stack([fm(inp["hy_conv_w"][jj, t]) for t in range(3)], axis=1) for jj in range(2)], axis=0)
    sh["hy_conv_w_fm"] = np.ascontiguousarray(cwh)
    sh["hy_conv_b_fm"] = np.ascontiguousarray(np.stack([fm(inp["hy_conv_b"][jj]) for jj in range(2)], axis=0))
    for nm in ("hy_freq1", "hy_b1", "hy_freq2", "hy_b2"):
        sh[nm + "_fm"] = np.ascontiguousarray(inp[nm][:, :, None])
    sh["hy_bias_bc"] = np.ascontiguousarray(np.broadcast_to(inp["hy_bias"][None], (128, 2, 2, 512)))
    sh["ffn_conv_b"] = np.ascontiguousarray(np.stack([fm(inp["ffn_conv_b"][i]) for i in range(DEPTH)], axis=1))
    return sh


def kernel(**inp):
    global _CONSTS
    inp = {k_: np.asarray(v) for k_, v in inp.items()}
    if _CONSTS is None:
        _CONSTS = make_consts()
    consts = _CONSTS
    shared = host_shared(inp)
    per = [host_prepare(inp, c) for c in range(NCORES)]
    extra = DBG.get("extra_inputs")
    if extra:
        for c in range(NCORES):
            per[c].update(extra(c))
    specs = _specs(consts, per[0], shared)
    outs = {
        "y_all": ((T, D), F32, "ExternalOutput"),
    }
    outs["st_out"] = ((2, 2, 2, 4, 128, 128), F32, "ExternalOutput")
    for nm in ("kc_out", "vc_out", "kd_out", "vd_out"):
        outs[nm] = ((2, 2, 256, 2, 128), F32, "ExternalOutput")
    for nm, shp in DBG.get("extra_outputs", {}).items():
        outs[nm] = (shp, F32, "ExternalOutput")
    specs.update(outs)
    nc = build_program(specs)
    in_maps = [{**consts, **shared, **per[c]} for c in range(NCORES)]
    if DBG.get("trace"):
        res = run_bass_kernel_spmd(nc, in_maps, core_ids=list(range(NCORES)), trace=True)
        print("EXEC_NS", res.exec_time_ns)
    else:
        res = run_bass_kernel_spmd(nc, in_maps, core_ids=list(range(NCORES)))
    R = res.results
    DBG["last_results"] = R
    y_prompt = np.stack([R[c // 2]["y_all"][(c % 2) * 256:(c % 2) * 256 + 256] for c in range(16)])
    y_sample = np.stack([R[c]["y_all"][512:1536] for c in range(NCORES)])
    st = np.concatenate([R[c]["st_out"] for c in range(NCORES)], axis=0)
    kv = [np.concatenate([R[c][nm] for c in range(NCORES)], axis=0) for nm in ("kc_out", "vc_out", "kd_out", "vd_out")]
    return (y_prompt, y_sample, st, kv[0], kv[1], kv[2], kv[3])
```
